# Optimizing a Trainium2 kernel written in Bass

```python
import jax, jax.numpy as jnp
from jax import lax
import numpy as np

D_MODEL = 2048
BATCH = 4
SEQ = 8192
DEPTH = 1

D_MIX = D_MODEL
D_HGRN = D_MIX // 2
HGRN_HEAD = 128
HGRN_HEADS = D_HGRN // HGRN_HEAD
HGRN_CHUNK = 64
D_ATTN = D_MIX - D_HGRN
ATTN_HEAD = 64
ATTN_HEADS = D_ATTN // ATTN_HEAD
DILATED_PATTERNS = ((128, 1), (512, 4), (2048, 16))
NORM_EPS = 1e-6
SPLIT_WIDTHS = (D_HGRN, D_HGRN, D_HGRN, D_HGRN, D_ATTN, D_ATTN, D_ATTN, D_ATTN)
D_IN = sum(SPLIT_WIDTHS)

kernel_name = "hymba_hgrn2_dilated_alibi_block"


def rms_norm(x, gain):
    xf = x.astype(jnp.float32)
    y = xf * lax.rsqrt(jnp.mean(xf * xf, axis=-1, keepdims=True) + NORM_EPS)
    return (y * gain.astype(jnp.float32)).astype(x.dtype)


def alibi_slopes(n_heads):
    return jnp.exp2(-8.0 * jnp.arange(1, n_heads + 1, dtype=jnp.float32) / n_heads)


def chunked_gated_recurrence(q, k, log_f, v):
    B, H, S, Dk = q.shape
    Dv = v.shape[-1]
    C = HGRN_CHUNK
    N = S // C
    q, k, log_f = (a.reshape(B, H, N, C, Dk) for a in (q, k, log_f))
    v = v.reshape(B, H, N, C, Dv)
    b = jnp.cumsum(log_f, axis=3)
    b_last = b[:, :, :, -1:, :]
    q_dec = q * jnp.exp(b)
    k_dec = k * jnp.exp(-b)
    k_end = k * jnp.exp(b_last - b)
    causal = jnp.tril(jnp.ones((C, C), dtype=bool))
    a_intra = jnp.where(causal, jnp.einsum('bhncd,bhnsd->bhncs', q_dec, k_dec), 0.0)
    o_intra = jnp.einsum('bhncs,bhnsv->bhncv', a_intra, v)
    kv_chunk = jnp.einsum('bhncd,bhncv->nbhdv', k_end, v)
    chunk_decay = jnp.exp(b_last[:, :, :, 0, :]).transpose(2, 0, 1, 3)

    def step(state, inp):
        dec, kv_n = inp
        return dec[..., None] * state + kv_n, state

    _, prev_states = lax.scan(step, jnp.zeros((B, H, Dk, Dv), jnp.float32),
                              (chunk_decay, kv_chunk))
    o_inter = jnp.einsum('bhncd,nbhdv->bhncv', q_dec, prev_states)
    return (o_intra + o_inter).reshape(B, H, S, Dv)


def hgrn2_mixer(q_pre, f_pre, i_pre, lb, g_norm):
    B, S, _ = q_pre.shape

    def heads(a):
        return a.astype(jnp.float32).reshape(B, S, HGRN_HEADS, HGRN_HEAD).transpose(0, 2, 1, 3)

    q = jax.nn.silu(heads(q_pre))
    lbh = lb.astype(jnp.float32).reshape(HGRN_HEADS, 1, HGRN_HEAD)
    f = lbh + (1.0 - lbh) * jax.nn.sigmoid(heads(f_pre))
    k = 1.0 - f
    o = chunked_gated_recurrence(q, k, jnp.log(f), heads(i_pre))
    o = o * lax.rsqrt(jnp.mean(o * o, axis=-1, keepdims=True) + NORM_EPS) * g_norm.astype(jnp.float32)
    return o.transpose(0, 2, 1, 3).reshape(B, S, D_HGRN)


def dilated_pattern_attention(q, k, v, slopes, window, dilation):
    B, H, S, Dh = q.shape
    d = dilation
    W = window // d
    L = S // d
    nb = -(-L // W)
    Lp = nb * W

    def regroup(a):
        a = a.reshape(B, H, L, d, Dh).transpose(0, 1, 3, 2, 4)
        a = jnp.pad(a, ((0, 0), (0, 0), (0, 0), (0, Lp - L), (0, 0)))
        return a.reshape(B, H, d, nb, W, Dh)

    def with_prev_block(a):
        prev = jnp.pad(a, ((0, 0), (0, 0), (0, 0), (1, 0), (0, 0), (0, 0)))[:, :, :, :-1]
        return jnp.concatenate([prev, a], axis=4)

    qb = regroup(q)
    kc = with_prev_block(regroup(k))
    vc = with_prev_block(regroup(v))
    s = jnp.einsum('bhrnqd,bhrnkd->bhrnqk', qb, kc) * (Dh ** -0.5)
    i_idx = jnp.arange(W)[:, None]
    j_idx = jnp.arange(2 * W)[None, :]
    delta = W + i_idx - j_idx
    blk = jnp.arange(nb)[:, None, None]
    valid = (delta >= 0) & (delta <= W) & ((blk - 1) * W + j_idx >= 0)
    bias = -slopes[:, None, None] * (d * delta).astype(jnp.float32)
    s = jnp.where(valid, s + bias[None, :, None, None], -jnp.inf)
    m = jnp.max(s, axis=-1, keepdims=True)
    p = jnp.exp(s - m)
    den = jnp.sum(p, axis=-1, keepdims=True)
    o = jnp.einsum('bhrnqk,bhrnkd->bhrnqd', p, vc) / den
    lse = (m + jnp.log(den))[..., 0]
    o = o.reshape(B, H, d, Lp, Dh)[:, :, :, :L].transpose(0, 1, 3, 2, 4).reshape(B, H, S, Dh)
    lse = lse.reshape(B, H, d, Lp)[:, :, :, :L].transpose(0, 1, 3, 2).reshape(B, H, S)
    return o, lse


def dilated_attention_mixer(q_pre, k_pre, v_pre):
    B, S, _ = q_pre.shape

    def heads(a):
        return a.astype(jnp.float32).reshape(B, S, ATTN_HEADS, ATTN_HEAD).transpose(0, 2, 1, 3)

    q, k, v = heads(q_pre), heads(k_pre), heads(v_pre)
    slopes = alibi_slopes(ATTN_HEADS)
    outs, lses = [], []
    for window, dilation in DILATED_PATTERNS:
        o, lse = dilated_pattern_attention(q, k, v, slopes, window, dilation)
        outs.append(o)
        lses.append(lse)
    weights = jax.nn.softmax(jnp.stack(lses, axis=0), axis=0)
    o = jnp.einsum('pbhs,pbhsd->bhsd', weights, jnp.stack(outs, axis=0))
    return o.transpose(0, 2, 1, 3).reshape(B, S, D_ATTN)


def setup_inputs(seed: int = 0) -> dict:
    key = jax.random.key(seed)
    ks = jax.random.split(key, 8)
    x = jax.random.normal(ks[0], (BATCH, SEQ, D_MODEL), jnp.float32)
    norm_gain = 1.0 + 0.01 * jax.random.normal(ks[1], (DEPTH, D_MODEL), jnp.float32)
    w_in = jax.random.normal(ks[2], (DEPTH, D_MODEL, D_IN), jnp.float32) * D_MODEL ** -0.5
    lb_logits = 0.1 * jax.random.normal(ks[3], (DEPTH + 1, D_HGRN), jnp.float32)
    hgrn_gnorm = 1.0 + 0.01 * jax.random.normal(ks[4], (DEPTH, HGRN_HEAD), jnp.float32)
    w_out = jax.random.normal(ks[5], (DEPTH, D_MIX, D_MODEL), jnp.float32) * D_MIX ** -0.5
    final_gain = 1.0 + 0.01 * jax.random.normal(ks[6], (D_MODEL,), jnp.float32)
    return {"x": x, "norm_gain": norm_gain, "w_in": w_in, "lb_logits": lb_logits,
            "hgrn_gnorm": hgrn_gnorm, "w_out": w_out, "final_gain": final_gain}


def reference(x, norm_gain, w_in, lb_logits, hgrn_gnorm, w_out, final_gain):
    lb_all = jnp.cumsum(jax.nn.softmax(lb_logits.astype(jnp.float32), axis=0), axis=0)
    split_points = [int(v) for v in np.cumsum(SPLIT_WIDTHS)[:-1]]
    for layer in range(DEPTH):
        h = rms_norm(x, norm_gain[layer])
        z = jnp.einsum('bsd,de->bse', h, w_in[layer])
        q_h, f_h, i_h, g_h, q_a, k_a, v_a, g_a = jnp.split(z, split_points, axis=-1)
        y_h = hgrn2_mixer(q_h, f_h, i_h, lb_all[layer], hgrn_gnorm[layer]) * jax.nn.silu(g_h.astype(jnp.float32))
        y_a = dilated_attention_mixer(q_a, k_a, v_a) * jax.nn.silu(g_a.astype(jnp.float32))
        y = jnp.concatenate([y_h, y_a], axis=-1).astype(x.dtype)
        x = x + jnp.einsum('bse,ed->bsd', y, w_out[layer])
    return rms_norm(x, final_gain)
```

```python
import contextlib
import numpy as np
import ml_dtypes
import concourse.bass as bass
import concourse.mybir as mybir
from concourse.bass_utils import run_bass_kernel_spmd

F32 = mybir.dt.float32
BF16 = mybir.dt.bfloat16
ALU = mybir.AluOpType
AF = mybir.ActivationFunctionType

S = 8192
D = 2048
NKC = 16
EPS = 1e-6
HH = 4
NPAIR = 4
BIG = 1.0e8
SAME_ENGINE_SYNC = True
PAIRS = [[0, 1], [2, 3], [4, 5], [6, 7]]


class _Op:
    __slots__ = ("eng", "fn", "deps", "signal", "dma_key", "dma_val", "count", "inc")

    def __init__(self, eng, fn, deps, dma_key, inc):
        self.eng, self.fn, self.deps, self.dma_key, self.inc = eng, fn, deps, dma_key, inc
        self.signal = False
        self.dma_val = None
        self.count = None


class Sched:
    ENGS = ("pe", "act", "dve", "pool", "sp")

    def __init__(self):
        self.ops = {e: [] for e in self.ENGS}
        self.res_w = {}
        self.res_r = {}
        self.dma_cnt = {}
        self.dma_last = {}
        self.bar_deps = set()
        self.bar_pending = set()
        self.halted = False

    def add(self, eng, fn, reads=(), writes=(), dma=None, inc=16):
        if self.halted:
            return None
        deps = set()
        for r in reads:
            if r in self.res_w:
                deps.add(self.res_w[r])
        for w in writes:
            if w in self.res_w:
                deps.add(self.res_w[w])
            for e in self.res_r.get(w, ()):
                deps.add(e)
        if eng in self.bar_pending:
            deps |= self.bar_deps
            self.bar_pending.discard(eng)
        op = _Op(eng, fn, deps, dma, inc)
        if dma is not None:
            self.dma_cnt[dma] = self.dma_cnt.get(dma, 0) + inc
            op.dma_val = self.dma_cnt[dma]
            self.dma_last[dma] = op
        for d in deps:
            if d.dma_key is None:
                d.signal = True
        self.ops[eng].append(op)
        for r in reads:
            self.res_r.setdefault(r, []).append(op)
        for w in writes:
            self.res_w[w] = op
            self.res_r[w] = []
        return op

    def barrier(self):
        deps = set(self.dma_last.values())
        for e in self.ENGS:
            if self.ops[e]:
                deps.add(self.ops[e][-1])
        self.bar_deps = deps
        self.bar_pending = set(self.ENGS)

    def emit(self, block, sems, dma_sems):
        for e in self.ENGS:
            c = 0
            for op in self.ops[e]:
                if op.dma_key is None and op.signal:
                    c += 1
                    op.count = c
        handles = {"pe": "tensor", "act": "scalar", "dve": "vector", "pool": "gpsimd", "sp": "sync"}

        def make(e):
            def body(engine):
                seen = {}
                for op in self.ops[e]:
                    need = {}
                    for d in op.deps:
                        if d.dma_key is not None:
                            sem, val, key = dma_sems[d.dma_key], d.dma_val, ("d", d.dma_key)
                        else:
                            if d.eng == e and (e in ("pe", "sp") or not SAME_ENGINE_SYNC):
                                continue
                            sem, val, key = sems[d.eng], d.count, ("e", d.eng)
                        if key not in need or need[key][1] < val:
                            need[key] = (sem, val)
                    for key, (sem, val) in need.items():
                        if seen.get(key, 0) >= val:
                            continue
                        seen[key] = val
                        engine.wait_ge(sem, val)
                    ins = op.fn(engine)
                    if op.dma_key is not None:
                        ins.then_inc(dma_sems[op.dma_key], op.inc)
                    elif op.signal:
                        ins.then_inc(sems[e], 1)
            return body

        for e in self.ENGS:
            getattr(block, handles[e])(make(e))


def build_program():
    nc = bass.Bass("TRN2", target_bir_lowering=False)
    dt = nc.dram_tensor
    x_in = dt("x", [S, D], F32, kind="ExternalInput").ap()
    xres_in = dt("xres", [S // 2, D], F32, kind="ExternalInput").ap()
    wa_in = dt("wa", [D, 2048], F32, kind="ExternalInput").ap()
    wb_in = dt("wb", [D, 2048], F32, kind="ExternalInput").ap()
    wo_in = dt("wo", [D, D], F32, kind="ExternalInput").ap()
    gain_in = dt("gain", [128, NKC], F32, kind="ExternalInput").ap()
    lbl_in = dt("lbl", [128, 2 * HH], F32, kind="ExternalInput").ap()
    gn_in = dt("gn", [128, 1], F32, kind="ExternalInput").ap()
    fg_in = dt("fg", [1, D], F32, kind="ExternalInput").ap()
    cv_in = dt("cv", [128, 24], F32, kind="ExternalInput").ap()
    ident_in = dt("ident", [128, 128], BF16, kind="ExternalInput").ap()
    mask512_in = dt("mask512", [128, 512], F32, kind="ExternalInput").ap()
    hmask_in = dt("hmask", [128, 512], BF16, kind="ExternalInput").ap()
    dmat_in = dt("dmat", [128, 256], F32, kind="ExternalInput").ap()
    out = dt("out", [S // 2, D], F32, kind="ExternalOutput").ap()

    hT = dt("hT", [NKC, 128, S], BF16).ap()
    ytA = dt("ytA", [8, 512, 1024], BF16).ap()
    ytB = dt("ytB", [8, 512, 1024], BF16).ap()
    ygA = dt("ygA", [8, 1024, 1024], BF16).ap()
    ygB = dt("ygB", [8, 1024, 1024], BF16).ap()
    vd = dt("vd", [S, 8, 128], BF16).ap()

    sc = Sched()

    with contextlib.ExitStack() as es:
        def sb(name, shape, dtype):
            return es.enter_context(nc.sbuf_tensor(name, shape, dtype))

        def ps(name, shape, dtype):
            return es.enter_context(nc.psum_tensor(name, shape, dtype))

        wbuf = sb("wbuf", [128, NKC, 2048], BF16)
        hx = [sb(f"hx{i}", [128, NKC, 512], BF16) for i in range(2)]
        a32 = sb("a32", [128, 6144], F32)
        a16 = sb("a16", [128, 37888], BF16)
        ident = sb("ident_s", [128, 128], BF16)
        ones = sb("ones_s", [128, 128], BF16)
        mask512 = sb("mask512_s", [128, 512], F32)
        hmask = sb("hmask_s", [128, 512], BF16)
        dmat = sb("dmat_s", [128, 256], F32)
        gain = sb("gain_s", [128, NKC], F32)
        lbl = sb("lbl_s", [128, 2 * HH], F32)
        lb = sb("lb_s", [128, HH], F32)
        oml = sb("oml_s", [128, HH], F32)
        noml = sb("noml_s", [128, HH], F32)
        gn = sb("gn_s", [128, 1], F32)
        cv = sb("cv_s", [128, 24], F32)
        ssq = sb("ssq_s", [128, 64], F32)
        rstd = sb("rstd_s", [128, 64], F32)
        ssq2 = sb("ssq2_s", [128, 32], F32)
        rstd2 = sb("rstd2_s", [128, 32], F32)
        s32 = sb("s32_s", [128, HH, 128], F32)
        s16 = sb("s16_s", [128, HH, 128], BF16)
        ebl4 = sb("ebl_s", [128, HH, 8], F32)

        pbank = [ps(f"pb{i}", [128, 512], F32) for i in range(4)]
        pacc = ps("pacc", [128, 2048], F32)

        sems = {e: es.enter_context(nc.semaphore(f"sem_{e}")) for e in ("pe", "act", "dve", "pool")}
        dma_sems = {}

        def dsem(key):
            if key not in dma_sems:
                dma_sems[key] = es.enter_context(nc.semaphore(f"dsem_{len(dma_sems)}"))
            return key

        block = es.enter_context(nc.Block())

        def dma(out_ap, in_ap, key, reads=(), writes=()):
            dsem(key)
            return sc.add("sp", lambda e, o=out_ap, i=in_ap: e.dma_start(out=o, in_=i),
                          reads=reads, writes=writes, dma=key)

        dma(ident[:], ident_in[:, :], "c_ident", writes=["ident"])
        dma(mask512[:], mask512_in[:, :], "c_mask512", writes=["mask512"])
        dma(hmask[:], hmask_in[:, :], "c_hmask", writes=["hmask"])
        dma(dmat[:], dmat_in[:, :], "c_dmat", writes=["dmat"])
        dma(gain[:], gain_in[:, :], "c_gain", writes=["gain"])
        dma(lbl[:], lbl_in[:, :], "c_lbl", writes=["lbl"])
        dma(gn[:], gn_in[:, :], "c_gn", writes=["gn"])
        dma(cv[:], cv_in[:, :], "c_cv", writes=["cv"])
        sc.add("dve", lambda e: e.memset(ones[:], 1.0), writes=["ones"])
        sc.add("dve", lambda e: e.tensor_tensor(out=lb[:], in0=lbl[:, 0:HH], in1=lbl[:, HH:2 * HH],
                                                op=ALU.subtract), reads=["lbl"], writes=["lb"])
        sc.add("act", lambda e: e.activation(out=lb[:], in_=lb[:], func=AF.Sigmoid),
               reads=["lb"], writes=["lb"])
        sc.add("dve", lambda e: e.tensor_scalar(out=oml[:], in0=lb[:], scalar1=-1.0, scalar2=1.0,
                                                op0=ALU.mult, op1=ALU.add), reads=["lb"], writes=["oml"])
        sc.add("dve", lambda e: e.tensor_scalar(out=noml[:], in0=oml[:], scalar1=-1.0, scalar2=None,
                                                op0=ALU.mult), reads=["oml"], writes=["noml"])

        def load_weights(w_in, use_gain):
            for k in range(NKC):
                slot = k % 2
                st = a32[:, slot * 2048:(slot + 1) * 2048]
                dma(st, w_in[k * 128:(k + 1) * 128, :], f"wst{slot}", writes=[f"wst{slot}"])
                if use_gain:
                    fn = lambda e, st=st, k=k: e.tensor_scalar(out=wbuf[:, k, :], in0=st,
                                                               scalar1=gain[:, k:k + 1], scalar2=None,
                                                               op0=ALU.mult)
                else:
                    fn = lambda e, st=st, k=k: e.tensor_copy(out=wbuf[:, k, :], in_=st)
                sc.add("pool" if k % 2 else "dve", fn, reads=[f"wst{slot}", "gain"], writes=["wbuf"])

        import os
        _stop = os.environ.get("KSTOP", "")

        def _chk(tag):
            if _stop == tag:
                sc.halted = True

        load_weights(wa_in, True)
        sc.barrier()
        _chk("W")

        xt = [a32[:, i * 2048:(i + 1) * 2048] for i in range(2)]
        hb = a16[:, 0:2048]
        junk = a16[:, 2048:4096]
        for g in range(int(os.environ.get('KNG', 16))):
            hslot = hx[g % 2]
            hres = f"hx{g % 2}"
            for j in range(4):
                i = g * 4 + j
                xs = xt[i % 2]
                xres_ = f"xt{i % 2}"
                dma(xs, x_in[i * 128:(i + 1) * 128, :], xres_, writes=[xres_])
                sc.add("act", lambda e, xs=xs, i=i: e.activation(out=junk, in_=xs, func=AF.Square,
                                                                 accum_out=ssq[:, i:i + 1]),
                       reads=[xres_], writes=["junk", f"ssq{i}"])
                sc.add("act", lambda e, i=i: e.activation(out=rstd[:, i:i + 1], in_=ssq[:, i:i + 1],
                                                          func=AF.Sqrt, scale=1.0 / D, bias=EPS),
                       reads=[f"ssq{i}"], writes=[f"rstd{i}"])
                sc.add("dve", lambda e, i=i: e.reciprocal(out=rstd[:, i:i + 1], in_=rstd[:, i:i + 1]),
                       reads=[f"rstd{i}"], writes=[f"rstd{i}"])
                sc.add("pool", lambda e, xs=xs, i=i: e.tensor_scalar(out=hb, in0=xs, scalar1=rstd[:, i:i + 1],
                                                                     scalar2=None, op0=ALU.mult),
                       reads=[xres_, f"rstd{i}"], writes=["hb"])
                pp = (i % 2) * 2
                for k in range(NKC):
                    o = pbank[pp + k // 8][:].bitcast(BF16)[:, (k % 8) * 128:(k % 8 + 1) * 128]
                    sc.add("pe", lambda e, o=o, k=k: e.transpose(o, hb[:, k * 128:(k + 1) * 128], ident[:]),
                           reads=["hb", "ident"], writes=[f"pb{pp + k // 8}"])
                for half in range(2):
                    src = pbank[pp + half][:].bitcast(BF16).rearrange("p (k t) -> p k t", k=8)
                    dst = hslot[:, half * 8:(half + 1) * 8, j * 128:(j + 1) * 128]
                    if half == 0:
                        sc.add("act", lambda e, s=src, d=dst: e.activation(out=d, in_=s, func=AF.Copy),
                               reads=[f"pb{pp + half}"], writes=[hres])
                    else:
                        sc.add("dve", lambda e, s=src, d=dst: e.tensor_copy(out=d, in_=s),
                               reads=[f"pb{pp + half}"], writes=[hres])
            dma(hT[:, :, g * 512:(g + 1) * 512].rearrange("k p t -> p k t"), hslot[:], f"hTst{g % 2}",
                reads=[hres], writes=[f"hT{g}"])
        sc.barrier()
        _chk("P")

        o32 = [0]
        o16 = [0]

        def f32buf(n):
            v = a32[:, o32[0]:o32[0] + n]
            o32[0] += n
            return v

        def b16buf(n):
            v = a16[:, o16[0]:o16[0] + n]
            o16[0] += n
            return v

        qs, sg, ff, lf, bb, e1, e2, kk, kd32, rr, tt = [f32buf(512) for _ in range(11)]
        kd16, ke16, sq, yT = [b16buf(512) for _ in range(4)]
        qd4 = [b16buf(512) for _ in range(HH)]
        keT4 = [b16buf(512) for _ in range(HH)]
        am4 = [b16buf(512) for _ in range(HH)]
        gg4 = [b16buf(512) for _ in range(HH)]
        v16 = b16buf(2048)
        v163 = v16.rearrange("p (j c) -> p j c", j=4)

        sc.add("dve", lambda e: e.memset(s32[:], 0.0), writes=[f"s32{h}" for h in range(HH)])
        sc.add("dve", lambda e: e.memset(s16[:], 0.0), writes=[f"s16{h}" for h in range(HH)])
        pk = pbank[1][:, 0:256].bitcast(BF16)
        oacc = [pacc[:, h * 512:(h + 1) * 512] for h in range(HH)]

        for blk in range(int(os.environ.get('KNB', 16))):
            hs = hx[blk % 2]
            hres = f"hx{blk % 2}"
            dma(hs[:], hT[:, :, blk * 512:(blk + 1) * 512].rearrange("k p t -> p k t"), f"hxld{blk % 2}",
                reads=[f"hT{blk}"], writes=[hres])
            for j in range(4):
                vb = j % 2
                for k in range(NKC):
                    sc.add("pe", lambda e, j=j, k=k, hs=hs, vb=vb: e.matmul(
                        pbank[vb][:, :], lhsT=hs[:, k, j * 128:(j + 1) * 128], rhs=wbuf[:, k, 1024:1536],
                        start=(k == 0), stop=(k == NKC - 1)), reads=[hres, "wbuf"], writes=[f"pb{vb}"])
                sc.add("act", lambda e, j=j, vb=vb: e.activation(out=v163[:, j, :], in_=pbank[vb][:, :], func=AF.Copy),
                       reads=[f"pb{vb}"], writes=["v16"])
            for hd in range(HH):
                qd, keT, am, gg = qd4[hd], keT4[hd], am4[hd], gg4[hd]
                keT3 = keT.rearrange("p (j d) -> p j d", j=4)
                R = lambda nm, hd=hd: f"{nm}{hd}"

                def proj(bank, c0, hs=hs, hres=hres):
                    for k in range(NKC):
                        sc.add("pe", lambda e, k=k, bank=bank, c0=c0, hs=hs: e.matmul(
                            pbank[bank][:, :], lhsT=wbuf[:, k, c0:c0 + 128], rhs=hs[:, k, :],
                            start=(k == 0), stop=(k == NKC - 1)), reads=[hres, "wbuf"], writes=[f"pb{bank}"])
                proj(0, hd * 128)
                sc.add("act", lambda e: e.activation(out=qs, in_=pbank[0][:, :], func=AF.Silu),
                       reads=["pb0"], writes=["qs"])
                proj(1, 512 + hd * 128)
                sc.add("act", lambda e: e.activation(out=sg, in_=pbank[1][:, :], func=AF.Sigmoid),
                       reads=["pb1"], writes=["sg"])
                proj(0, 1536 + hd * 128)
                sc.add("act", lambda e, gg=gg: e.activation(out=gg, in_=pbank[0][:, :], func=AF.Silu),
                       reads=["pb0"], writes=[R("gg")])
                sc.add("dve", lambda e, hd=hd: e.tensor_scalar(out=ff, in0=sg, scalar1=oml[:, hd:hd + 1],
                                                               scalar2=lb[:, hd:hd + 1], op0=ALU.mult, op1=ALU.add),
                       reads=["sg", "oml", "lb"], writes=["ff"])
                sc.add("pool", lambda e, hd=hd: e.tensor_scalar(out=kk, in0=sg, scalar1=noml[:, hd:hd + 1],
                                                                scalar2=oml[:, hd:hd + 1], op0=ALU.mult, op1=ALU.add),
                       reads=["sg", "oml", "noml"], writes=["kk"])
                sc.add("pool", lambda e, gg=gg: e.tensor_scalar(out=gg, in0=gg, scalar1=gn[:, 0:1], scalar2=None,
                                                                op0=ALU.mult), reads=[R("gg"), "gn"], writes=[R("gg")])
                sc.add("act", lambda e: e.activation(out=lf, in_=ff, func=AF.Ln), reads=["ff"], writes=["lf"])
                sc.add("dve", lambda e: e.tensor_tensor_scan(out=bb, data0=mask512[:], data1=lf, initial=0.0,
                                                             op0=ALU.mult, op1=ALU.add),
                       reads=["lf", "mask512"], writes=["bb"])
                sc.add("act", lambda e: e.activation(out=e1, in_=bb, func=AF.Exp), reads=["bb"], writes=["e1"])
                sc.add("act", lambda e: e.activation(out=e2, in_=bb, func=AF.Exp, scale=-1.0),
                       reads=["bb"], writes=["e2"])
                sc.add("dve", lambda e, hd=hd: e.tensor_copy(out=ebl4[:, hd, :],
                                                             in_=e1.rearrange("p (n t) -> p n t", t=64)[:, :, 63]),
                       reads=["e1"], writes=[R("ebl")])
                sc.add("pool", lambda e, qd=qd: e.tensor_tensor(out=qd, in0=qs, in1=e1, op=ALU.mult),
                       reads=["qs", "e1"], writes=[R("qd")])
                sc.add("dve", lambda e: e.tensor_tensor(out=kd32, in0=kk, in1=e2, op=ALU.mult),
                       reads=["kk", "e2"], writes=["kd32"])
                sc.add("pool", lambda e: e.tensor_copy(out=kd16, in_=kd32), reads=["kd32"], writes=["kd16"])
                sc.add("dve", lambda e, hd=hd: e.tensor_tensor(
                    out=ke16.rearrange("p (n t) -> p n t", t=64), in0=kd32.rearrange("p (n t) -> p n t", t=64),
                    in1=ebl4[:, hd, :].unsqueeze(2).to_broadcast([128, 8, 64]), op=ALU.mult),
                    reads=["kd32", R("ebl")], writes=["ke16"])
                for j in range(4):
                    sc.add("pe", lambda e, j=j: e.transpose(pk[:, j * 128:(j + 1) * 128],
                                                            ke16[:, j * 128:(j + 1) * 128], ident[:]),
                           reads=["ke16", "ident"], writes=["pb1"])
                sc.add("act", lambda e, keT=keT: e.activation(out=keT, in_=pk, func=AF.Copy),
                       reads=["pb1"], writes=[R("keT")])
                for j in range(4):
                    sc.add("pe", lambda e, j=j, qd=qd: e.matmul(pbank[0][:, j * 128:(j + 1) * 128],
                                                                lhsT=kd16[:, j * 128:(j + 1) * 128],
                                                                rhs=qd[:, j * 128:(j + 1) * 128], start=True, stop=True,
                                                                skip_group_check=True),
                           reads=["kd16", R("qd")], writes=["pb0"])
                sc.add("dve", lambda e, am=am: e.tensor_tensor(out=am, in0=pbank[0][:, :], in1=hmask[:], op=ALU.mult),
                       reads=["pb0", "hmask"], writes=[R("am")])
            step = 0
            for n in range(8):
                j, half = n // 2, n % 2
                rows = slice(half * 64, half * 64 + 64)
                for hd in range(HH):
                    qd, keT, am = qd4[hd], keT4[hd], am4[hd]
                    keT3 = keT.rearrange("p (j d) -> p j d", j=4)
                    R = lambda nm, hd=hd: f"{nm}{hd}"
                    kb = 2 + step % 2
                    step += 1
                    kvb = pbank[kb][:, 0:128]
                    kvres = f"pb{kb}"
                    sc.add("pe", lambda e, j=j, rows=rows, kvb=kvb, hd=hd, keT3=keT3: e.matmul(
                        kvb, lhsT=keT3[rows, j, :], rhs=v163[rows, j, hd * 128:(hd + 1) * 128],
                        start=True, stop=True), reads=[R("keT"), "v16"], writes=[kvres])
                    sc.add("pe", lambda e, n=n, hd=hd, qd=qd: e.matmul(
                        oacc[hd][:, n * 64:(n + 1) * 64], lhsT=s16[:, hd, :], rhs=qd[:, n * 64:(n + 1) * 64],
                        start=(n == 0), stop=False, skip_group_check=True),
                        reads=[R("s16"), R("qd")], writes=[R("oacc")])
                    if half == 1:
                        sc.add("pe", lambda e, j=j, hd=hd, am=am: e.matmul(
                            oacc[hd][:, j * 128:(j + 1) * 128], lhsT=v163[:, j, hd * 128:(hd + 1) * 128],
                            rhs=am[:, j * 128:(j + 1) * 128], start=False, stop=True, skip_group_check=True),
                            reads=["v16", R("am")], writes=[R("oacc")])
                    sc.add("dve", lambda e, n=n, kvb=kvb, hd=hd: e.scalar_tensor_tensor(
                        out=s32[:, hd, :], in0=s32[:, hd, :], scalar=ebl4[:, hd, n:n + 1], in1=kvb,
                        op0=ALU.mult, op1=ALU.add), reads=[R("s32"), R("ebl"), kvres], writes=[R("s32")])
                    sc.add("act", lambda e, hd=hd: e.activation(out=s16[:, hd, :], in_=s32[:, hd, :], func=AF.Copy),
                           reads=[R("s32")], writes=[R("s16")])
            for hd in range(HH):
                gg = gg4[hd]
                R = lambda nm, hd=hd: f"{nm}{hd}"
                kb = 2 + hd % 2
                sc.add("act", lambda e, hd=hd: e.activation(out=sq, in_=oacc[hd], func=AF.Square),
                       reads=[R("oacc")], writes=["sq"])
                sc.add("pe", lambda e, kb=kb: e.matmul(pbank[kb][:, :], lhsT=ones[:], rhs=sq, start=True, stop=True),
                       reads=["sq", "ones"], writes=[f"pb{kb}"])
                sc.add("act", lambda e, kb=kb: e.activation(out=rr, in_=pbank[kb][:, :], func=AF.Ln,
                                                            scale=1.0 / 128, bias=EPS), reads=[f"pb{kb}"], writes=["rr"])
                sc.add("act", lambda e: e.activation(out=rr, in_=rr, func=AF.Exp, scale=-0.5),
                       reads=["rr"], writes=["rr"])
                sc.add("dve", lambda e, hd=hd: e.tensor_tensor(out=tt, in0=oacc[hd], in1=rr, op=ALU.mult),
                       reads=[R("oacc"), "rr"], writes=["tt"])
                sc.add("pool", lambda e, gg=gg: e.tensor_tensor(out=yT, in0=tt, in1=gg, op=ALU.mult),
                       reads=["tt", R("gg")], writes=["yT"])
                dma(ytA[blk // 2, hd * 128:(hd + 1) * 128, (blk % 2) * 512:(blk % 2 + 1) * 512], yT, "ytst",
                    reads=["yT"], writes=[f"ytA{hd}_{blk}"])
        sc.barrier()
        _chk("A")
        dsem("agA")
        if os.environ.get("KCC", "1") == "1":
          for c8 in range(8):
            sc.add("pool", lambda e, c8=c8: e.collective_compute("AllGather", ALU.bypass, replica_groups=PAIRS,
                                                                 ins=[ytA[c8]], outs=[ygA[c8]]),
                   writes=[f"ygA{c8}"], dma="agA", inc=1)
        _chk("AGA")

        load_weights(wb_in, True)
        sc.barrier()
        _chk("WB")
        o32[0] = 0
        o16[0] = 0
        NSB = 4
        tbuf = [f32buf(256) for _ in range(NSB)]
        rec = f32buf(512)
        t2 = f32buf(512)
        mbuf = [f32buf(256) for _ in range(3)]
        QT = b16buf(2048)
        KT = b16buf(4096)
        GT = b16buf(2048)
        VX = [b16buf(8192) for _ in range(3)]
        vst = [b16buf(1024) for _ in range(2)]
        PT = [b16buf(256) for _ in range(NSB)]
        yTB = b16buf(2048)
        for i in range(2):
            sc.add("dve", lambda e, i=i: e.memset(vst[i], 1.0), writes=[f"vst{i}"])

        def tile_ap(buf, base, d, ti, off=0):
            if d == 1:
                s0, st = 128 * ti, 1
            elif d == 4:
                s0, st = 512 * (ti // 4) + (ti % 4), 4
            else:
                s0, st = ti, 16
            return buf[base:base + 64, off + s0: off + s0 + 127 * st + 1: st]

        def prev_tile(d, ti):
            if d == 1:
                return (True, ti - 1) if ti > 0 else (False, 15)
            if d == 4:
                return (True, ti - 4) if ti >= 4 else (False, ti + 12)
            return (False, ti)

        accb = [pacc[:, b * 512:(b + 1) * 512] for b in range(4)]
        vcnt = 0
        for pr in range(int(os.environ.get('KNP', NPAIR))):
            for n in range(int(os.environ.get('KNS', 4))):
                kslot = (n % 2) * 2048
                kres = f"KT{n % 2}"
                for sbk in range(4):
                    blk = n * 4 + sbk
                    hs = hx[blk % 2]
                    hres = f"hx{blk % 2}"
                    dma(hs[:], hT[:, :, blk * 512:(blk + 1) * 512].rearrange("k p t -> p k t"), f"hxld{blk % 2}",
                        reads=[f"hT{blk}"], writes=[hres])
                    for which, c0, bank in (("q", pr * 128, 1), ("k", 512 + pr * 128, 2), ("g", 1536 + pr * 128, 0)):
                        for k in range(NKC):
                            sc.add("pe", lambda e, k=k, bank=bank, c0=c0, hs=hs: e.matmul(
                                pbank[bank][:, :], lhsT=wbuf[:, k, c0:c0 + 128], rhs=hs[:, k, :],
                                start=(k == 0), stop=(k == NKC - 1)),
                                reads=[hres, "wbuf"], writes=[f"pb{bank}"])
                        if which == "q":
                            sc.add("act", lambda e, sbk=sbk: e.activation(out=QT[:, sbk * 512:(sbk + 1) * 512],
                                                                          in_=pbank[1][:, :], func=AF.Copy, scale=0.125),
                                   reads=["pb1"], writes=["QT"])
                        elif which == "k":
                            sc.add("act", lambda e, sbk=sbk, kslot=kslot: e.activation(
                                out=KT[:, kslot + sbk * 512: kslot + (sbk + 1) * 512], in_=pbank[2][:, :], func=AF.Copy),
                                reads=["pb2"], writes=[kres])
                        else:
                            sc.add("act", lambda e, sbk=sbk: e.activation(out=GT[:, sbk * 512:(sbk + 1) * 512],
                                                                          in_=pbank[0][:, :], func=AF.Silu),
                                   reads=["pb0"], writes=["GT"])
                    for j in range(4):
                        for k in range(NKC):
                            sc.add("pe", lambda e, j=j, k=k, hs=hs, pr=pr: e.matmul(
                                pbank[3][:, j * 128:(j + 1) * 128], lhsT=hs[:, k, j * 128:(j + 1) * 128],
                                rhs=wbuf[:, k, 1024 + pr * 128: 1024 + (pr + 1) * 128],
                                start=(k == 0), stop=(k == NKC - 1), skip_group_check=True),
                                reads=[hres, "wbuf"], writes=["pb3"])
                    vs = vst[vcnt % 2]
                    vres = f"vst{vcnt % 2}"
                    vs4 = vs.rearrange("p (j h c) -> p j h c", j=4, h=2)
                    pv3 = pbank[3][:, :].rearrange("p (j c) -> p j c", j=4)
                    sc.add("dve", lambda e, vs4=vs4, pv3=pv3: e.tensor_copy(out=vs4[:, :, 0, 0:64], in_=pv3[:, :, 0:64]),
                           reads=["pb3"], writes=[vres])
                    sc.add("dve", lambda e, vs4=vs4, pv3=pv3: e.tensor_copy(out=vs4[:, :, 1, 64:128], in_=pv3[:, :, 64:128]),
                           reads=["pb3"], writes=[vres])
                    dma(vd[blk * 512:(blk + 1) * 512, pr * 2:(pr + 1) * 2, :].rearrange("(j p) h c -> p j h c", p=128),
                        vs4, f"vdst{vcnt % 2}", reads=[vres], writes=[f"vd{pr}_{blk}"])
                    vcnt += 1
                for d_i, d in enumerate((1, 4, 16)):
                    vx4 = VX[d_i].rearrange("p (m t c) -> p m t c", m=2, t=16)
                    for m_i, m in enumerate((n - 1, n)):
                        if m < 0:
                            continue
                        src = vd[m * 2048:(m + 1) * 2048, pr * 2:(pr + 1) * 2, :]
                        rd = [f"vd{pr}_{m * 4 + q}" for q in range(4)]
                        if d == 1:
                            dma(vx4[:, m_i, :, :], src.rearrange("(t j) h c -> j t (h c)", j=128),
                                f"vx{d_i}_{m_i}_0", reads=rd, writes=[f"VX{d_i}_{m_i}_0"])
                        elif d == 4:
                            for b4 in range(4):
                                dma(vx4[:, m_i, b4 * 4:(b4 + 1) * 4, :],
                                    src[b4 * 512:(b4 + 1) * 512].rearrange("(j r) h c -> j r (h c)", r=4),
                                    f"vx{d_i}_{m_i}_{b4}", reads=rd, writes=[f"VX{d_i}_{m_i}_{b4}"])
                        else:
                            dma(vx4[:, m_i, :, :], src.rearrange("(j r) h c -> j r (h c)", r=16),
                                f"vx{d_i}_{m_i}_0", reads=rd, writes=[f"VX{d_i}_{m_i}_0"])
                for hd in range(2):
                    base = hd * 64
                    first = [True] * 4
                    tiles = []
                    for d_i, d in enumerate((1, 4, 16)):
                        cidx = (pr * 2 + hd) * 3 + d_i
                        mb = mbuf[d_i]
                        mbres = f"mbuf{d_i}"
                        sc.add("pool", lambda e, mb=mb, cidx=cidx: e.tensor_scalar(
                            out=mb, in0=dmat[:], scalar1=cv[:, cidx:cidx + 1], scalar2=-100.0,
                            op0=ALU.mult, op1=ALU.max), reads=["dmat", "cv"], writes=[mbres])
                        for ti in range(16):
                            tiles.append((d_i, d, ti, mb, mbres))

                    def emit_S(t, tiles=tiles, base=base, n=n, kslot=kslot, kres=kres):
                        d_i, d, ti, mb, mbres = tiles[t]
                        same, tk = prev_tile(d, ti)
                        has_prev = same or n > 0
                        sl = t % NSB
                        stb = pbank[sl][:, 0:256]
                        qa = tile_ap(QT, base, d, ti)
                        klist = []
                        if has_prev:
                            koff = kslot if same else (2048 - kslot)
                            klist.append((0, tile_ap(KT, base, d, tk, koff), (1 if same else 0), tk,
                                          kres if same else f"KT{(n - 1) % 2}"))
                        klist.append((1, tile_ap(KT, base, d, ti, kslot), 1, ti, kres))
                        for w, ka, m_i, tkk, kr in klist:
                            sc.add("pe", lambda e, w=w, ka=ka, qa=qa, stb=stb: e.matmul(
                                stb[:, w * 128:(w + 1) * 128], lhsT=ka, rhs=qa, start=True, stop=True,
                                skip_group_check=True), reads=[kr, "QT"], writes=[f"pb{sl}"])
                        return klist, has_prev

                    def emit_R(t, klist, has_prev, tiles=tiles, hd=hd, first=first):
                        d_i, d, ti, mb, mbres = tiles[t]
                        vx4 = VX[d_i].rearrange("p (m t c) -> p m t c", m=2, t=16)
                        sl = t % NSB
                        stb = pbank[sl][:, 0:256]
                        tb, pt = tbuf[sl], PT[sl]
                        c_lo = 0 if has_prev else 128
                        sc.add("dve", lambda e, tb=tb, stb=stb, c_lo=c_lo, mb=mb: e.tensor_tensor(
                            out=tb[:, c_lo:256], in0=stb[:, c_lo:256], in1=mb[:, c_lo:256], op=ALU.add),
                            reads=[f"pb{sl}", mbres], writes=[f"tb{sl}"])
                        sc.add("act", lambda e, tb=tb, pt=pt, c_lo=c_lo: e.activation(
                            out=pt[:, c_lo:256], in_=tb[:, c_lo:256], func=AF.Exp),
                            reads=[f"tb{sl}"], writes=[f"pt{sl}"])
                        for w, ka, m_i, tkk, kr in klist:
                            vt = vx4[:, m_i, tkk, hd * 128:(hd + 1) * 128]
                            if d == 16:
                                outs = [(b, accb[b][:, ti: ti + 31 * 16 + 1: 16],
                                         pt[:, w * 128 + b * 32: w * 128 + (b + 1) * 32]) for b in range(4)]
                            elif d == 4:
                                b, r = ti // 4, ti % 4
                                outs = [(b, accb[b][:, r: r + 127 * 4 + 1: 4], pt[:, w * 128:(w + 1) * 128])]
                            else:
                                b = ti // 4
                                outs = [(b, accb[b][:, (ti % 4) * 128:(ti % 4 + 1) * 128],
                                         pt[:, w * 128:(w + 1) * 128])]
                            for b, o_ap, r_ap in outs:
                                st_flag = first[b]
                                first[b] = False
                                sc.add("pe", lambda e, vt=vt, o_ap=o_ap, r_ap=r_ap, st_flag=st_flag: e.matmul(
                                    o_ap, lhsT=vt, rhs=r_ap, start=st_flag, stop=False, skip_group_check=True),
                                    reads=[f"pt{sl}"] + [f"VX{d_i}_{mm}_{q}" for mm in range(2) for q in range(4)],
                                    writes=[f"acc{b}"])

                    pend = {}
                    LA = NSB - 1
                    for t in range(min(LA, len(tiles))):
                        pend[t] = emit_S(t)
                    for t in range(len(tiles)):
                        if t + LA < len(tiles):
                            pend[t + LA] = emit_S(t + LA)
                        kl, hp = pend.pop(t)
                        emit_R(t, kl, hp)
                    for b in range(4):
                        up = slice(base, base + 64)
                        dn = slice(64 - base, 128 - base)
                        sc.add("dve", lambda e, b=b, up=up, dn=dn: e.reciprocal(out=rec[up, :], in_=accb[b][dn, :]),
                               reads=[f"acc{b}"], writes=["rec"])
                        sc.add("dve", lambda e, b=b, up=up: e.tensor_tensor(out=t2[up, :], in0=accb[b][up, :],
                                                                            in1=rec[up, :], op=ALU.mult),
                               reads=[f"acc{b}", "rec"], writes=["t2"])
                        sc.add("pool", lambda e, b=b, up=up: e.tensor_tensor(
                            out=yTB[up, b * 512:(b + 1) * 512], in0=t2[up, :], in1=GT[up, b * 512:(b + 1) * 512],
                            op=ALU.mult), reads=["t2", "GT"], writes=["yTB"])
                for q in range(2):
                    dma(ytB[2 * n + q, pr * 128:(pr + 1) * 128, :], yTB[:, q * 1024:(q + 1) * 1024], "ytbst",
                        reads=["yTB"], writes=[f"ytB{pr}_{n}_{q}"])
                if pr == NPAIR - 1 and os.environ.get("KCC", "1") == "1":
                    dsem("agB")
                    for q in range(2):
                        sc.add("pool", lambda e, c8=2 * n + q: e.collective_compute(
                            "AllGather", ALU.bypass, replica_groups=PAIRS, ins=[ytB[c8]], outs=[ygB[c8]]),
                            reads=[f"ytB{p}_{n}_{q}" for p in range(NPAIR)], writes=[f"ygB{2 * n + q}"],
                            dma="agB", inc=1)
        sc.barrier()
        _chk("B")
        dsem("agB")

        load_weights(wo_in, False)
        fgain = a16[:, 0:4096].bitcast(F32)
        dma(fgain, fg_in[0:1, :].partition_broadcast(128), "c_fgain", writes=["fgain"])
        sc.barrier()
        _chk("WO")
        xr = [a32[:, 0:2048], a32[:, 2048:4096]]
        xn = a32[:, 4096:6144]
        ot = a16[:, 4096:8192].bitcast(F32)
        junkc = a16[:, 8192:10240]
        thc = {}

        def th_of(e):
            if "th" not in thc:
                thc["th"] = e.partition_id() % 2
            return thc["th"]

        for tg in range(8):
            yk = hx[tg % 2]
            ykres = f"hx{tg % 2}"
            for part, yg, ygres in ((0, ygA, "ygA"), (1, ygB, "ygB")):
                ygres = [f"{ygres}{c8}" for c8 in range(8)]
                dsem(f"ykld{tg % 2}_{part}")
                sc.add("sp", lambda e, yk=yk, yg=yg, part=part, tg=tg: e.dma_start(
                    out=yk[:, part * 8:(part + 1) * 8, :],
                    in_=yg[bass.ts(th_of(e) * 4 + tg // 2, 1), :, (tg % 2) * 512:(tg % 2 + 1) * 512].rearrange(
                        "o (k p) t -> p (o k) t", p=128)),
                    reads=ygres, writes=[ykres + "ab"[part]], dma=f"ykld{tg % 2}_{part}")
            for j in range(4):
                tix = tg * 4 + j
                xs = xr[tix % 2]
                xsres = f"xr{tix % 2}"
                dma(xs, xres_in[tix * 128:(tix + 1) * 128, :], xsres, writes=[xsres])
                for cg in range(4):
                    for k in range(NKC):
                        sc.add("pe", lambda e, yk=yk, j=j, k=k, cg=cg: e.matmul(
                            accb[cg], lhsT=yk[:, k, j * 128:(j + 1) * 128], rhs=wbuf[:, k, cg * 512:(cg + 1) * 512],
                            start=(k == 0), stop=(k == NKC - 1)), reads=[ykres + "a", ykres + "b", "wbuf"], writes=[f"acc{cg}"])
                sc.add("dve", lambda e, xs=xs: e.tensor_tensor(out=xn, in0=pacc[:, :], in1=xs, op=ALU.add),
                       reads=[xsres] + [f"acc{c}" for c in range(4)], writes=["xn"])
                sc.add("act", lambda e, tix=tix: e.activation(out=junkc, in_=xn, func=AF.Square,
                                                              accum_out=ssq2[:, tix:tix + 1]),
                       reads=["xn"], writes=["junkc", f"ssq2{tix}"])
                sc.add("act", lambda e, tix=tix: e.activation(out=rstd2[:, tix:tix + 1], in_=ssq2[:, tix:tix + 1],
                                                              func=AF.Sqrt, scale=1.0 / D, bias=EPS),
                       reads=[f"ssq2{tix}"], writes=[f"rstd2{tix}"])
                sc.add("dve", lambda e, tix=tix: e.reciprocal(out=rstd2[:, tix:tix + 1], in_=rstd2[:, tix:tix + 1]),
                       reads=[f"rstd2{tix}"], writes=[f"rstd2{tix}"])
                sc.add("dve", lambda e, tix=tix: e.scalar_tensor_tensor(out=ot, in0=xn, scalar=rstd2[:, tix:tix + 1],
                                                                        in1=fgain, op0=ALU.mult, op1=ALU.mult),
                       reads=["xn", f"rstd2{tix}", "fgain"], writes=["ot"])
                dma(out[tix * 128:(tix + 1) * 128, :], ot, "outst", reads=["ot"], writes=[f"out{tix}"])
        if os.environ.get("KDBG", "0") == "1":
            dbgA = dt("dbgA", [1024, 1024], BF16, kind="ExternalOutput").ap()
            dbgB = dt("dbgB", [1024, 1024], BF16, kind="ExternalOutput").ap()
            dbgV = dt("dbgV", [1024, 1024], BF16, kind="ExternalOutput").ap()
            for q in range(8):
                dma(dbgA[q * 128:(q + 1) * 128, :], ygA[0, q * 128:(q + 1) * 128, :], "dbg", reads=[f"ygA{c8}" for c8 in range(8)], writes=[f"dbgA{q}"])
                dma(dbgB[q * 128:(q + 1) * 128, :], ygB[0, q * 128:(q + 1) * 128, :], "dbg", reads=[f"ygB{c8}" for c8 in range(8)], writes=[f"dbgB{q}"])
                dma(dbgV[q * 128:(q + 1) * 128, :], vd[q * 128:(q + 1) * 128, :, :].rearrange("t h c -> t (h c)"), "dbg", writes=[f"dbgV{q}"])
            sc.add("sp", lambda e: e.nop(), reads=[f"dbg{w}{q}" for w in "ABV" for q in range(8)])
        sc.add("sp", lambda e: e.nop(), reads=[f"out{t}" for t in range(32)])

        sc.emit(block, sems, dma_sems)
    return nc


_CACHE = {}


def _consts():
    ident = np.eye(128, dtype=np.float32).astype(ml_dtypes.bfloat16)
    t = np.arange(512)
    mask512 = np.broadcast_to((t % 64 != 0).astype(np.float32)[None, :], (128, 512)).copy()
    s = np.arange(128)[:, None]
    c = np.arange(128)[None, :]
    hm = ((s // 64 == c // 64) & (s <= c)).astype(np.float32)
    hmask = np.tile(hm, (1, 4)).astype(ml_dtypes.bfloat16)
    j = np.arange(128)[:, None]
    i = np.arange(128)[None, :]
    dprev = np.where(j >= i, 128.0 + i - j, BIG)
    dcur = np.where(j <= i, (i - j).astype(np.float64), BIG)
    dmat = np.concatenate([dprev, dcur], axis=1).astype(np.float32)
    return ident, mask512, hmask, dmat


def kernel(x, norm_gain, w_in, lb_logits, hgrn_gnorm, w_out, final_gain):
    x = np.asarray(x, np.float32)
    w_in = np.asarray(w_in, np.float32)[0]
    w_out = np.asarray(w_out, np.float32)[0]
    norm_gain = np.asarray(norm_gain, np.float32)[0]
    lb_logits = np.asarray(lb_logits, np.float32)
    gnorm = np.asarray(hgrn_gnorm, np.float32)[0]
    final_gain = np.asarray(final_gain, np.float32)
    if "nc" not in _CACHE:
        _CACHE["nc"] = build_program()
    nc = _CACHE["nc"]
    ident, mask512, hmask, dmat = _consts()
    slopes = [2.0 ** (-(h + 1) / 2.0) for h in range(16)]
    gain_l = np.ascontiguousarray(norm_gain.reshape(NKC, 128).T)
    in_maps = []
    for c in range(8):
        b, hh = c // 2, c % 2
        hcols = []
        for grp in range(4):
            for h in range(HH):
                gh = hh * HH + h
                hcols.append(np.arange(grp * 1024 + gh * 128, grp * 1024 + (gh + 1) * 128))
        wa = np.ascontiguousarray(w_in[:, np.concatenate(hcols)])
        acols = []
        for grp in range(4):
            for a in range(8):
                ga = hh * 8 + a
                acols.append(np.arange(4096 + grp * 1024 + ga * 64, 4096 + grp * 1024 + (ga + 1) * 64))
        wb = np.ascontiguousarray(w_in[:, np.concatenate(acols)])
        lbl = np.zeros((128, 2 * HH), np.float32)
        for h in range(HH):
            gh = hh * HH + h
            lbl[:, h] = lb_logits[0, gh * 128:(gh + 1) * 128]
            lbl[:, HH + h] = lb_logits[1, gh * 128:(gh + 1) * 128]
        cvv = np.zeros((128, 24), np.float32)
        for a in range(8):
            for d_i, d in enumerate((1, 4, 16)):
                cvv[:, a * 3 + d_i] = np.float32(-slopes[hh * 8 + a] * d)
        in_maps.append({
            "x": np.ascontiguousarray(x[b]),
            "xres": np.ascontiguousarray(x[b, hh * (S // 2):(hh + 1) * (S // 2)]),
            "wa": wa, "wb": wb, "wo": np.ascontiguousarray(w_out),
            "gain": gain_l, "lbl": lbl, "gn": np.ascontiguousarray(gnorm.reshape(128, 1)),
            "fg": np.ascontiguousarray(final_gain.reshape(1, D)), "cv": cvv,
            "ident": ident, "mask512": mask512, "hmask": hmask, "dmat": dmat,
        })
    res = run_bass_kernel_spmd(nc, in_maps, core_ids=list(range(8)))
    _CACHE["res"] = res
    outp = np.empty((4, S, D), np.float32)
    for c in range(8):
        b, hh = c // 2, c % 2
        outp[b, hh * (S // 2):(hh + 1) * (S // 2)] = np.asarray(res.results[c]["out"], np.float32)
    return outp
```

```python
import contextlib
import numpy as np
import ml_dtypes
import concourse.bass as bass
import concourse.mybir as mybir
from concourse.bass_utils import run_bass_kernel_spmd

F32 = mybir.dt.float32
BF16 = mybir.dt.bfloat16
ALU = mybir.AluOpType
AF = mybir.ActivationFunctionType

S = 8192
D = 2048
NKC = 16
EPS = 1e-6
HH = 4
NPAIR = 4
BIG = 1.0e8
SAME_ENGINE_SYNC = True
PAIRS = [[0, 1], [2, 3], [4, 5], [6, 7]]


class _Op:
    __slots__ = ("eng", "fn", "deps", "signal", "dma_key", "dma_val", "count", "inc")

    def __init__(self, eng, fn, deps, dma_key, inc):
        self.eng, self.fn, self.deps, self.dma_key, self.inc = eng, fn, deps, dma_key, inc
        self.signal = False
        self.dma_val = None
        self.count = None


class Sched:
    ENGS = ("pe", "act", "dve", "pool", "sp")

    def __init__(self):
        self.ops = {e: [] for e in self.ENGS}
        self.res_w = {}
        self.res_r = {}
        self.dma_cnt = {}
        self.dma_last = {}
        self.bar_deps = set()
        self.bar_pending = set()
        self.halted = False

    def add(self, eng, fn, reads=(), writes=(), dma=None, inc=16):
        if self.halted:
            return None
        deps = set()
        for r in reads:
            if r in self.res_w:
                deps.add(self.res_w[r])
        for w in writes:
            if w in self.res_w:
                deps.add(self.res_w[w])
            last = {}
            for e in self.res_r.get(w, ()):
                if e.dma_key is not None:
                    deps.add(e)
                else:
                    last[e.eng] = e
            deps.update(last.values())
        if eng in self.bar_pending:
            deps |= self.bar_deps
            self.bar_pending.discard(eng)
        op = _Op(eng, fn, deps, dma, inc)
        if dma is not None:
            self.dma_cnt[dma] = self.dma_cnt.get(dma, 0) + inc
            op.dma_val = self.dma_cnt[dma]
            self.dma_last[dma] = op
        for d in deps:
            if d.dma_key is None:
                d.signal = True
        self.ops[eng].append(op)
        for r in reads:
            self.res_r.setdefault(r, []).append(op)
        for w in writes:
            self.res_w[w] = op
            self.res_r[w] = []
        return op

    def barrier(self):
        deps = set(self.dma_last.values())
        for e in self.ENGS:
            if self.ops[e]:
                deps.add(self.ops[e][-1])
        self.bar_deps = deps
        self.bar_pending = set(self.ENGS)

    def emit(self, block, sems, dma_sems):
        for e in self.ENGS:
            c = 0
            for op in self.ops[e]:
                if op.dma_key is None and op.signal:
                    c += 1
                    op.count = c
        handles = {"pe": "tensor", "act": "scalar", "dve": "vector", "pool": "gpsimd", "sp": "sync"}

        def make(e):
            def body(engine):
                seen = {}
                for op in self.ops[e]:
                    need = {}
                    for d in op.deps:
                        if d.dma_key is not None:
                            sem, val, key = dma_sems[d.dma_key], d.dma_val, ("d", d.dma_key)
                        else:
                            if d.eng == e and (e in ("pe", "sp") or not SAME_ENGINE_SYNC):
                                continue
                            sem, val, key = sems[d.eng], d.count, ("e", d.eng)
                        if key not in need or need[key][1] < val:
                            need[key] = (sem, val)
                    for key, (sem, val) in need.items():
                        if seen.get(key, 0) >= val:
                            continue
                        seen[key] = val
                        engine.wait_ge(sem, val)
                    ins = op.fn(engine)
                    if op.dma_key is not None:
                        ins.then_inc(dma_sems[op.dma_key], op.inc)
                    elif op.signal:
                        ins.then_inc(sems[e], 1)
            return body

        for e in self.ENGS:
            getattr(block, handles[e])(make(e))


def build_program():
    nc = bass.Bass("TRN2", target_bir_lowering=False)
    dt = nc.dram_tensor
    x_in = dt("x", [S, D], F32, kind="ExternalInput").ap()
    xres_in = dt("xres", [S // 2, D], F32, kind="ExternalInput").ap()
    wa_in = dt("wa", [D, 2048], F32, kind="ExternalInput").ap()
    wb_in = dt("wb", [D, 2048], F32, kind="ExternalInput").ap()
    wo_in = dt("wo", [D, D], F32, kind="ExternalInput").ap()
    gain_in = dt("gain", [128, NKC], F32, kind="ExternalInput").ap()
    lbl_in = dt("lbl", [128, 2 * HH], F32, kind="ExternalInput").ap()
    gn_in = dt("gn", [128, 1], F32, kind="ExternalInput").ap()
    fg_in = dt("fg", [1, D], F32, kind="ExternalInput").ap()
    cv_in = dt("cv", [128, 24], F32, kind="ExternalInput").ap()
    ident_in = dt("ident", [128, 128], BF16, kind="ExternalInput").ap()
    mask512_in = dt("mask512", [128, 512], F32, kind="ExternalInput").ap()
    hmask_in = dt("hmask", [128, 512], BF16, kind="ExternalInput").ap()
    dmat_in = dt("dmat", [128, 256], F32, kind="ExternalInput").ap()
    out = dt("out", [S // 2, D], F32, kind="ExternalOutput").ap()

    hT = dt("hT", [NKC, 128, S], BF16).ap()
    ytA = dt("ytA", [8, 512, 1024], BF16).ap()
    ytB = dt("ytB", [8, 512, 1024], BF16).ap()
    ygA = dt("ygA", [8, 1024, 1024], BF16).ap()
    ygB = dt("ygB", [8, 1024, 1024], BF16).ap()
    vd = dt("vd", [S, 8, 128], BF16).ap()

    sc = Sched()
    WB = [f"wbuf{k}" for k in range(NKC)]

    with contextlib.ExitStack() as es:
        def sb(name, shape, dtype):
            return es.enter_context(nc.sbuf_tensor(name, shape, dtype))

        def ps(name, shape, dtype):
            return es.enter_context(nc.psum_tensor(name, shape, dtype))

        wbuf = sb("wbuf", [128, NKC, 2048], BF16)
        hx = [sb(f"hx{i}", [128, NKC, 512], BF16) for i in range(2)]
        a32 = sb("a32", [128, 6144], F32)
        a16 = sb("a16", [128, 38400], BF16)
        ident = sb("ident_s", [128, 128], BF16)
        ones = sb("ones_s", [128, 128], BF16)
        mask512 = sb("mask512_s", [128, 512], F32)
        hmask = sb("hmask_s", [128, 512], BF16)
        dmat = sb("dmat_s", [128, 256], F32)
        gain = sb("gain_s", [128, NKC], F32)
        lbl = sb("lbl_s", [128, 2 * HH], F32)
        lb = sb("lb_s", [128, HH], F32)
        oml = sb("oml_s", [128, HH], F32)
        noml = sb("noml_s", [128, HH], F32)
        gn = sb("gn_s", [128, 1], F32)
        cv = sb("cv_s", [128, 24], F32)
        ssq = sb("ssq_s", [128, 64], F32)
        rstd = sb("rstd_s", [128, 64], F32)
        ssq2 = sb("ssq2_s", [128, 32], F32)
        rstd2 = sb("rstd2_s", [128, 32], F32)
        s32 = sb("s32_s", [128, HH, 128], F32)
        s16 = sb("s16_s", [128, HH, 128], BF16)
        ebl4 = sb("ebl_s", [128, HH, 8], F32)

        pbank = [ps(f"pb{i}", [128, 512], F32) for i in range(4)]
        pacc = ps("pacc", [128, 2048], F32)

        sems = {e: es.enter_context(nc.semaphore(f"sem_{e}")) for e in ("pe", "act", "dve", "pool")}
        dma_sems = {}

        def dsem(key):
            if key not in dma_sems:
                dma_sems[key] = es.enter_context(nc.semaphore(f"dsem_{len(dma_sems)}"))
            return key

        block = es.enter_context(nc.Block())

        def dma(out_ap, in_ap, key, reads=(), writes=()):
            dsem(key)
            return sc.add("sp", lambda e, o=out_ap, i=in_ap: e.dma_start(out=o, in_=i),
                          reads=reads, writes=writes, dma=key)

        dma(ident[:], ident_in[:, :], "c_ident", writes=["ident"])
        dma(mask512[:], mask512_in[:, :], "c_mask512", writes=["mask512"])
        dma(hmask[:], hmask_in[:, :], "c_hmask", writes=["hmask"])
        dma(dmat[:], dmat_in[:, :], "c_dmat", writes=["dmat"])
        dma(gain[:], gain_in[:, :], "c_gain", writes=["gain"])
        dma(lbl[:], lbl_in[:, :], "c_lbl", writes=["lbl"])
        dma(gn[:], gn_in[:, :], "c_gn", writes=["gn"])
        dma(cv[:], cv_in[:, :], "c_cv", writes=["cv"])
        sc.add("dve", lambda e: e.memset(ones[:], 1.0), writes=["ones"])
        sc.add("dve", lambda e: e.tensor_tensor(out=lb[:], in0=lbl[:, 0:HH], in1=lbl[:, HH:2 * HH],
                                                op=ALU.subtract), reads=["lbl"], writes=["lb"])
        sc.add("act", lambda e: e.activation(out=lb[:], in_=lb[:], func=AF.Sigmoid),
               reads=["lb"], writes=["lb"])
        sc.add("dve", lambda e: e.tensor_scalar(out=oml[:], in0=lb[:], scalar1=-1.0, scalar2=1.0,
                                                op0=ALU.mult, op1=ALU.add), reads=["lb"], writes=["oml"])
        sc.add("dve", lambda e: e.tensor_scalar(out=noml[:], in0=oml[:], scalar1=-1.0, scalar2=None,
                                                op0=ALU.mult), reads=["oml"], writes=["noml"])

        def load_weights(w_in, use_gain):
            for k in range(NKC):
                slot = k % 2
                st = a32[:, slot * 2048:(slot + 1) * 2048]
                dma(st, w_in[k * 128:(k + 1) * 128, :], f"wst{slot}", writes=[f"wst{slot}"])
                if k % 2 == 0:
                    if use_gain:
                        fn = lambda e, st=st, k=k: e.tensor_scalar(out=wbuf[:, k, :], in0=st,
                                                                   scalar1=gain[:, k:k + 1], scalar2=None,
                                                                   op0=ALU.mult)
                    else:
                        fn = lambda e, st=st, k=k: e.tensor_copy(out=wbuf[:, k, :], in_=st)
                    sc.add("dve", fn, reads=[f"wst{slot}", "gain"], writes=[f"wbuf{k}"])
                else:
                    if use_gain:
                        fn = lambda e, st=st, k=k: e.activation(out=wbuf[:, k, :], in_=st, func=AF.Copy,
                                                                scale=gain[:, k:k + 1])
                    else:
                        fn = lambda e, st=st, k=k: e.activation(out=wbuf[:, k, :], in_=st, func=AF.Copy)
                    sc.add("act", fn, reads=[f"wst{slot}", "gain"], writes=[f"wbuf{k}"])

        import os
        _stop = os.environ.get("KSTOP", "")

        def _chk(tag):
            if _stop == tag:
                sc.halted = True

        load_weights(wa_in, True)
        sc.barrier()
        _chk("W")

        xt = [a32[:, i * 2048:(i + 1) * 2048] for i in range(2)]
        hbs = [a16[:, 0:2048], a16[:, 4096:6144]]
        junk = a16[:, 2048:4096]
        NG = int(os.environ.get('KNG', 16))
        for g in range(NG):
            hslot = hx[g % 2]
            hres = f"hx{g % 2}"
            for j in range(4):
                i = g * 4 + j
                xs = xt[i % 2]
                xres_ = f"xt{i % 2}"
                if i == 0:
                    dma(xs, x_in[0:128, :], xres_, writes=[xres_])
                if i + 1 < NG * 4:
                    dma(xt[(i + 1) % 2], x_in[(i + 1) * 128:(i + 2) * 128, :], f"xt{(i + 1) % 2}",
                        writes=[f"xt{(i + 1) % 2}"])
                sc.add("act", lambda e, xs=xs, i=i: e.activation(out=junk, in_=xs, func=AF.Square,
                                                                 accum_out=ssq[:, i:i + 1]),
                       reads=[xres_], writes=["junk", f"ssq{i}"])
                sc.add("act", lambda e, i=i: e.activation(out=rstd[:, i:i + 1], in_=ssq[:, i:i + 1],
                                                          func=AF.Sqrt, scale=1.0 / D, bias=EPS),
                       reads=[f"ssq{i}"], writes=[f"rstd{i}"])
                sc.add("dve", lambda e, i=i: e.reciprocal(out=rstd[:, i:i + 1], in_=rstd[:, i:i + 1]),
                       reads=[f"rstd{i}"], writes=[f"rstd{i}"])
                hb = hbs[i % 2]
                hbres = f"hb{i % 2}"
                sc.add("act", lambda e, xs=xs, i=i, hb=hb: e.activation(out=hb, in_=xs, func=AF.Copy,
                                                                        scale=rstd[:, i:i + 1]),
                       reads=[xres_, f"rstd{i}"], writes=[hbres])
                pp = (i % 2) * 2
                for k in range(NKC):
                    o = pbank[pp + k // 8][:].bitcast(BF16)[:, (k % 8) * 128:(k % 8 + 1) * 128]
                    sc.add("pe", lambda e, o=o, k=k, hb=hb: e.transpose(o, hb[:, k * 128:(k + 1) * 128], ident[:]),
                           reads=[hbres, "ident"], writes=[f"pb{pp + k // 8}"])
                for half in range(2):
                    src = pbank[pp + half][:].bitcast(BF16).rearrange("p (k t) -> p k t", k=8)
                    dst = hslot[:, half * 8:(half + 1) * 8, j * 128:(j + 1) * 128]
                    if half == 0:
                        sc.add("act", lambda e, s=src, d=dst: e.activation(out=d, in_=s, func=AF.Copy),
                               reads=[f"pb{pp + half}"], writes=[hres])
                    else:
                        sc.add("dve", lambda e, s=src, d=dst: e.tensor_copy(out=d, in_=s),
                               reads=[f"pb{pp + half}"], writes=[hres])
            dma(hT[:, :, g * 512:(g + 1) * 512].rearrange("k p t -> p k t"), hslot[:], f"hTst{g % 2}",
                reads=[hres], writes=[f"hT{g}"])
        sc.barrier()
        _chk("P")

        o32 = [0]
        o16 = [0]

        def f32buf(n):
            v = a32[:, o32[0]:o32[0] + n]
            o32[0] += n
            return v

        def b16buf(n):
            v = a16[:, o16[0]:o16[0] + n]
            o16[0] += n
            return v

        qs, sg, ff, lf, bb, e1, e2, kk, kd32, rr, tt = [f32buf(512) for _ in range(11)]
        kd16, ke16, sq, yT = [b16buf(512) for _ in range(4)]
        qd4 = [b16buf(512) for _ in range(HH)]
        keT4 = [b16buf(512) for _ in range(HH)]
        am4 = [b16buf(512) for _ in range(HH)]
        gg4 = [b16buf(512) for _ in range(HH)]
        v16 = b16buf(2048)
        v163 = v16.rearrange("p (j c) -> p j c", j=4)

        sc.add("dve", lambda e: e.memset(s32[:], 0.0), writes=[f"s32{h}" for h in range(HH)])
        sc.add("dve", lambda e: e.memset(s16[:], 0.0), writes=[f"s16{h}" for h in range(HH)])
        pk = pbank[1][:, 0:256].bitcast(BF16)
        oacc = [pacc[:, h * 512:(h + 1) * 512] for h in range(HH)]

        NBA = int(os.environ.get('KNB', 16))
        for blk in range(NBA):
            hs = hx[blk % 2]
            hres = f"hx{blk % 2}"
            if blk == 0:
                dma(hs[:], hT[:, :, 0:512].rearrange("k p t -> p k t"), "hxld0", reads=["hT0"], writes=[hres])
            if blk + 1 < NBA:
                dma(hx[(blk + 1) % 2][:], hT[:, :, (blk + 1) * 512:(blk + 2) * 512].rearrange("k p t -> p k t"),
                    f"hxld{(blk + 1) % 2}", reads=[f"hT{blk + 1}"], writes=[f"hx{(blk + 1) % 2}"])
            for j in range(4):
                vb = j % 2
                for k in range(NKC):
                    sc.add("pe", lambda e, j=j, k=k, hs=hs, vb=vb: e.matmul(
                        pbank[vb][:, :], lhsT=hs[:, k, j * 128:(j + 1) * 128], rhs=wbuf[:, k, 1024:1536],
                        start=(k == 0), stop=(k == NKC - 1)), reads=[hres, f"wbuf{k}"], writes=[f"pb{vb}"])
                sc.add("act", lambda e, j=j, vb=vb: e.activation(out=v163[:, j, :], in_=pbank[vb][:, :], func=AF.Copy),
                       reads=[f"pb{vb}"], writes=["v16"])
            for hd in range(HH):
                qd, keT, am, gg = qd4[hd], keT4[hd], am4[hd], gg4[hd]
                keT3 = keT.rearrange("p (j d) -> p j d", j=4)
                R = lambda nm, hd=hd: f"{nm}{hd}"

                def proj(bank, c0, hs=hs, hres=hres):
                    for k in range(NKC):
                        sc.add("pe", lambda e, k=k, bank=bank, c0=c0, hs=hs: e.matmul(
                            pbank[bank][:, :], lhsT=wbuf[:, k, c0:c0 + 128], rhs=hs[:, k, :],
                            start=(k == 0), stop=(k == NKC - 1)), reads=[hres, f"wbuf{k}"], writes=[f"pb{bank}"])
                proj(0, hd * 128)
                sc.add("act", lambda e: e.activation(out=qs, in_=pbank[0][:, :], func=AF.Silu),
                       reads=["pb0"], writes=["qs"])
                proj(1, 512 + hd * 128)
                sc.add("act", lambda e: e.activation(out=sg, in_=pbank[1][:, :], func=AF.Sigmoid),
                       reads=["pb1"], writes=["sg"])
                proj(0, 1536 + hd * 128)
                sc.add("act", lambda e, gg=gg: e.activation(out=gg, in_=pbank[0][:, :], func=AF.Silu),
                       reads=["pb0"], writes=[R("gg")])
                sc.add("dve", lambda e, hd=hd: e.tensor_scalar(out=ff, in0=sg, scalar1=oml[:, hd:hd + 1],
                                                               scalar2=lb[:, hd:hd + 1], op0=ALU.mult, op1=ALU.add),
                       reads=["sg", "oml", "lb"], writes=["ff"])
                sc.add("pool", lambda e, hd=hd: e.tensor_scalar(out=kk, in0=sg, scalar1=noml[:, hd:hd + 1],
                                                                scalar2=oml[:, hd:hd + 1], op0=ALU.mult, op1=ALU.add),
                       reads=["sg", "oml", "noml"], writes=["kk"])
                sc.add("dve", lambda e, gg=gg: e.tensor_scalar(out=gg, in0=gg, scalar1=gn[:, 0:1], scalar2=None,
                                                               op0=ALU.mult), reads=[R("gg"), "gn"], writes=[R("gg")])
                sc.add("act", lambda e: e.activation(out=lf, in_=ff, func=AF.Ln), reads=["ff"], writes=["lf"])
                sc.add("dve", lambda e: e.tensor_tensor_scan(out=bb, data0=mask512[:], data1=lf, initial=0.0,
                                                             op0=ALU.mult, op1=ALU.add),
                       reads=["lf", "mask512"], writes=["bb"])
                sc.add("act", lambda e: e.activation(out=e1, in_=bb, func=AF.Exp), reads=["bb"], writes=["e1"])
                sc.add("act", lambda e: e.activation(out=e2, in_=bb, func=AF.Exp, scale=-1.0),
                       reads=["bb"], writes=["e2"])
                sc.add("dve", lambda e, hd=hd: e.tensor_copy(out=ebl4[:, hd, :],
                                                             in_=e1.rearrange("p (n t) -> p n t", t=64)[:, :, 63]),
                       reads=["e1"], writes=[R("ebl")])
                sc.add("pool", lambda e, qd=qd: e.tensor_tensor(out=qd, in0=qs, in1=e1, op=ALU.mult),
                       reads=["qs", "e1"], writes=[R("qd")])
                sc.add("dve", lambda e: e.tensor_tensor(out=kd32, in0=kk, in1=e2, op=ALU.mult),
                       reads=["kk", "e2"], writes=["kd32"])
                sc.add("act", lambda e: e.activation(out=kd16, in_=kd32, func=AF.Copy), reads=["kd32"], writes=["kd16"])
                sc.add("dve", lambda e, hd=hd: e.tensor_tensor(
                    out=ke16.rearrange("p (n t) -> p n t", t=64), in0=kd32.rearrange("p (n t) -> p n t", t=64),
                    in1=ebl4[:, hd, :].unsqueeze(2).to_broadcast([128, 8, 64]), op=ALU.mult),
                    reads=["kd32", R("ebl")], writes=["ke16"])
                for j in range(4):
                    sc.add("pe", lambda e, j=j: e.transpose(pk[:, j * 128:(j + 1) * 128],
                                                            ke16[:, j * 128:(j + 1) * 128], ident[:]),
                           reads=["ke16", "ident"], writes=["pb1"])
                sc.add("act", lambda e, keT=keT: e.activation(out=keT, in_=pk, func=AF.Copy),
                       reads=["pb1"], writes=[R("keT")])
                for j in range(4):
                    sc.add("pe", lambda e, j=j, qd=qd: e.matmul(pbank[0][:, j * 128:(j + 1) * 128],
                                                                lhsT=kd16[:, j * 128:(j + 1) * 128],
                                                                rhs=qd[:, j * 128:(j + 1) * 128], start=True, stop=True,
                                                                skip_group_check=True),
                           reads=["kd16", R("qd")], writes=["pb0"])
                sc.add("dve", lambda e, am=am: e.tensor_tensor(out=am, in0=pbank[0][:, :], in1=hmask[:], op=ALU.mult),
                       reads=["pb0", "hmask"], writes=[R("am")])
            step = 0
            for n in range(8):
                j, half = n // 2, n % 2
                rows = slice(half * 64, half * 64 + 64)
                for hd in range(HH):
                    qd, keT, am = qd4[hd], keT4[hd], am4[hd]
                    keT3 = keT.rearrange("p (j d) -> p j d", j=4)
                    R = lambda nm, hd=hd: f"{nm}{hd}"
                    kb = 2 + step % 2
                    step += 1
                    kvb = pbank[kb][:, 0:128]
                    kvres = f"pb{kb}"
                    sc.add("pe", lambda e, j=j, rows=rows, kvb=kvb, hd=hd, keT3=keT3: e.matmul(
                        kvb, lhsT=keT3[rows, j, :], rhs=v163[rows, j, hd * 128:(hd + 1) * 128],
                        start=True, stop=True), reads=[R("keT"), "v16"], writes=[kvres])
                    sc.add("pe", lambda e, n=n, hd=hd, qd=qd: e.matmul(
                        oacc[hd][:, n * 64:(n + 1) * 64], lhsT=s16[:, hd, :], rhs=qd[:, n * 64:(n + 1) * 64],
                        start=(n == 0), stop=False, skip_group_check=True),
                        reads=[R("s16"), R("qd")], writes=[R("oacc")])
                    if half == 1:
                        sc.add("pe", lambda e, j=j, hd=hd, am=am: e.matmul(
                            oacc[hd][:, j * 128:(j + 1) * 128], lhsT=v163[:, j, hd * 128:(hd + 1) * 128],
                            rhs=am[:, j * 128:(j + 1) * 128], start=False, stop=True, skip_group_check=True),
                            reads=["v16", R("am")], writes=[R("oacc")])
                    sc.add("dve", lambda e, n=n, kvb=kvb, hd=hd: e.scalar_tensor_tensor(
                        out=s32[:, hd, :], in0=s32[:, hd, :], scalar=ebl4[:, hd, n:n + 1], in1=kvb,
                        op0=ALU.mult, op1=ALU.add), reads=[R("s32"), R("ebl"), kvres], writes=[R("s32")])
                    sc.add("act", lambda e, hd=hd: e.activation(out=s16[:, hd, :], in_=s32[:, hd, :], func=AF.Copy),
                           reads=[R("s32")], writes=[R("s16")])
            for hd in range(HH):
                gg = gg4[hd]
                R = lambda nm, hd=hd: f"{nm}{hd}"
                kb = 2 + hd % 2
                sc.add("act", lambda e, hd=hd: e.activation(out=sq, in_=oacc[hd], func=AF.Square),
                       reads=[R("oacc")], writes=["sq"])
                sc.add("pe", lambda e, kb=kb: e.matmul(pbank[kb][:, :], lhsT=ones[:], rhs=sq, start=True, stop=True),
                       reads=["sq", "ones"], writes=[f"pb{kb}"])
                sc.add("act", lambda e, kb=kb: e.activation(out=rr, in_=pbank[kb][:, :], func=AF.Ln,
                                                            scale=1.0 / 128, bias=EPS), reads=[f"pb{kb}"], writes=["rr"])
                sc.add("act", lambda e: e.activation(out=rr, in_=rr, func=AF.Exp, scale=-0.5),
                       reads=["rr"], writes=["rr"])
                sc.add("dve", lambda e, hd=hd: e.tensor_tensor(out=tt, in0=oacc[hd], in1=rr, op=ALU.mult),
                       reads=[R("oacc"), "rr"], writes=["tt"])
                sc.add("dve", lambda e, gg=gg: e.tensor_tensor(out=yT, in0=tt, in1=gg, op=ALU.mult),
                       reads=["tt", R("gg")], writes=["yT"])
                dma(ytA[blk // 2, hd * 128:(hd + 1) * 128, (blk % 2) * 512:(blk % 2 + 1) * 512], yT, "ytst",
                    reads=["yT"], writes=[f"ytA{hd}_{blk}"])
        sc.barrier()
        _chk("A")
        dsem("agA")
        if os.environ.get("KCC", "1") == "1":
          for c8 in range(8):
            sc.add("pool", lambda e, c8=c8: e.collective_compute("AllGather", ALU.bypass, replica_groups=PAIRS,
                                                                 ins=[ytA[c8]], outs=[ygA[c8]]),
                   writes=[f"ygA{c8}"], dma="agA", inc=1)
        _chk("AGA")

        load_weights(wb_in, True)
        sc.barrier()
        _chk("WB")
        o32[0] = 0
        o16[0] = 0
        NSB = 4
        tbuf = [f32buf(256) for _ in range(NSB)]
        rec = f32buf(512)
        t2 = f32buf(512)
        mbuf = [f32buf(256) for _ in range(3)]
        QT = b16buf(2048)
        KT = b16buf(4096)
        GT = b16buf(2048)
        VX = [b16buf(8192) for _ in range(3)]
        vst = [b16buf(1024) for _ in range(2)]
        PT = [b16buf(256) for _ in range(NSB)]
        yTB = b16buf(2048)
        vT = b16buf(512)
        pvtok = pacc[:, 0:256].bitcast(BF16)
        for i in range(2):
            sc.add("dve", lambda e, i=i: e.memset(vst[i], 1.0), writes=[f"vst{i}"])

        def tile_ap(buf, base, d, ti, off=0):
            if d == 1:
                s0, st = 128 * ti, 1
            elif d == 4:
                s0, st = 512 * (ti // 4) + (ti % 4), 4
            else:
                s0, st = ti, 16
            return buf[base:base + 64, off + s0: off + s0 + 127 * st + 1: st]

        def prev_tile(d, ti):
            if d == 1:
                return (True, ti - 1) if ti > 0 else (False, 15)
            if d == 4:
                return (True, ti - 4) if ti >= 4 else (False, ti + 12)
            return (False, ti)

        accb = [pacc[:, b * 512:(b + 1) * 512] for b in range(4)]
        vcnt = 0
        bflat = [n_ * 4 + s_ for _p in range(int(os.environ.get('KNP', NPAIR)))
                 for n_ in range(int(os.environ.get('KNS', 4))) for s_ in range(4)]
        bseq = 0
        for pr in range(int(os.environ.get('KNP', NPAIR))):
            for n in range(int(os.environ.get('KNS', 4))):
                kslot = (n % 2) * 2048
                kres = f"KT{n % 2}"
                for sbk in range(4):
                    blk = n * 4 + sbk
                    hs = hx[blk % 2]
                    hres = f"hx{blk % 2}"
                    if bseq == 0:
                        dma(hs[:], hT[:, :, blk * 512:(blk + 1) * 512].rearrange("k p t -> p k t"), f"hxld{blk % 2}",
                            reads=[f"hT{blk}"], writes=[hres])
                    bseq += 1
                    if bseq < len(bflat):
                        nb_ = bflat[bseq]
                        dma(hx[nb_ % 2][:], hT[:, :, nb_ * 512:(nb_ + 1) * 512].rearrange("k p t -> p k t"),
                            f"hxld{nb_ % 2}", reads=[f"hT{nb_}"], writes=[f"hx{nb_ % 2}"])
                    for which, c0, bank in (("q", pr * 128, 1), ("k", 512 + pr * 128, 2), ("g", 1536 + pr * 128, 0)):
                        for k in range(NKC):
                            sc.add("pe", lambda e, k=k, bank=bank, c0=c0, hs=hs: e.matmul(
                                pbank[bank][:, :], lhsT=wbuf[:, k, c0:c0 + 128], rhs=hs[:, k, :],
                                start=(k == 0), stop=(k == NKC - 1)),
                                reads=[hres, f"wbuf{k}"], writes=[f"pb{bank}"])
                        if which == "q":
                            sc.add("act", lambda e, sbk=sbk: e.activation(out=QT[:, sbk * 512:(sbk + 1) * 512],
                                                                          in_=pbank[1][:, :], func=AF.Copy, scale=0.125),
                                   reads=["pb1"], writes=["QT"])
                        elif which == "k":
                            sc.add("act", lambda e, sbk=sbk, kslot=kslot: e.activation(
                                out=KT[:, kslot + sbk * 512: kslot + (sbk + 1) * 512], in_=pbank[2][:, :], func=AF.Copy),
                                reads=["pb2"], writes=[kres])
                        else:
                            sc.add("act", lambda e, sbk=sbk: e.activation(out=GT[:, sbk * 512:(sbk + 1) * 512],
                                                                          in_=pbank[0][:, :], func=AF.Silu),
                                   reads=["pb0"], writes=["GT"])
                    for k in range(NKC):
                        sc.add("pe", lambda e, k=k, hs=hs, pr=pr: e.matmul(
                            pbank[3][:, :], lhsT=wbuf[:, k, 1024 + pr * 128: 1024 + (pr + 1) * 128], rhs=hs[:, k, :],
                            start=(k == 0), stop=(k == NKC - 1)), reads=[hres, f"wbuf{k}"], writes=["pb3"])
                    sc.add("act", lambda e: e.activation(out=vT, in_=pbank[3][:, :], func=AF.Copy),
                           reads=["pb3"], writes=["vT"])
                    for j in range(4):
                        sc.add("pe", lambda e, j=j: e.transpose(pvtok[:, j * 128:(j + 1) * 128],
                                                                vT[:, j * 128:(j + 1) * 128], ident[:]),
                               reads=["vT", "ident"], writes=["acc0"])
                    vs = vst[vcnt % 2]
                    vres = f"vst{vcnt % 2}"
                    vs4 = vs.rearrange("p (j h c) -> p j h c", j=4, h=2)
                    pv3 = pvtok.rearrange("p (j c) -> p j c", j=4)
                    sc.add("dve", lambda e, vs4=vs4, pv3=pv3: e.tensor_copy(out=vs4[:, :, 0, 0:64], in_=pv3[:, :, 0:64]),
                           reads=["acc0"], writes=[vres])
                    sc.add("dve", lambda e, vs4=vs4, pv3=pv3: e.tensor_copy(out=vs4[:, :, 1, 64:128], in_=pv3[:, :, 64:128]),
                           reads=["acc0"], writes=[vres])
                    dma(vd[blk * 512:(blk + 1) * 512, pr * 2:(pr + 1) * 2, :].rearrange("(j p) h c -> p j h c", p=128),
                        vs4, f"vdst{vcnt % 2}", reads=[vres], writes=[f"vd{pr}_{blk}"])
                    vcnt += 1
                for d_i, d in enumerate((1, 4, 16)):
                    vx4 = VX[d_i].rearrange("p (m t c) -> p m t c", m=2, t=16)
                    for m_i, m in ((n % 2, n),):
                        src = vd[m * 2048:(m + 1) * 2048, pr * 2:(pr + 1) * 2, :]
                        rd = [f"vd{pr}_{m * 4 + q}" for q in range(4)]
                        if d == 1:
                            dma(vx4[:, m_i, :, :], src.rearrange("(t j) h c -> j t (h c)", j=128),
                                f"vx{d_i}_{m_i}_0", reads=rd, writes=[f"VX{d_i}_{m_i}_0"])
                        elif d == 4:
                            for b4 in range(4):
                                dma(vx4[:, m_i, b4 * 4:(b4 + 1) * 4, :],
                                    src[b4 * 512:(b4 + 1) * 512].rearrange("(j r) h c -> j r (h c)", r=4),
                                    f"vx{d_i}_{m_i}_{b4}", reads=rd, writes=[f"VX{d_i}_{m_i}_{b4}"])
                        else:
                            dma(vx4[:, m_i, :, :], src.rearrange("(j r) h c -> j r (h c)", r=16),
                                f"vx{d_i}_{m_i}_0", reads=rd, writes=[f"VX{d_i}_{m_i}_0"])
                for hd in range(2):
                    base = hd * 64
                    first = [True] * 4
                    tiles = []
                    for d_i, d in enumerate((1, 4, 16)):
                        cidx = (pr * 2 + hd) * 3 + d_i
                        mb = mbuf[d_i]
                        mbres = f"mbuf{d_i}"
                        sc.add("pool", lambda e, mb=mb, cidx=cidx: e.tensor_scalar(
                            out=mb, in0=dmat[:], scalar1=cv[:, cidx:cidx + 1], scalar2=-100.0,
                            op0=ALU.mult, op1=ALU.max), reads=["dmat", "cv"], writes=[mbres])
                        for ti in range(16):
                            tiles.append((d_i, d, ti, mb, mbres))

                    def emit_S(t, tiles=tiles, base=base, n=n, kslot=kslot, kres=kres):
                        d_i, d, ti, mb, mbres = tiles[t]
                        same, tk = prev_tile(d, ti)
                        has_prev = same or n > 0
                        sl = t % NSB
                        stb = pbank[sl][:, 0:256]
                        qa = tile_ap(QT, base, d, ti)
                        klist = []
                        if has_prev:
                            koff = kslot if same else (2048 - kslot)
                            klist.append((0, tile_ap(KT, base, d, tk, koff), (n % 2 if same else 1 - n % 2), tk,
                                          kres if same else f"KT{(n - 1) % 2}"))
                        klist.append((1, tile_ap(KT, base, d, ti, kslot), n % 2, ti, kres))
                        for w, ka, m_i, tkk, kr in klist:
                            sc.add("pe", lambda e, w=w, ka=ka, qa=qa, stb=stb: e.matmul(
                                stb[:, w * 128:(w + 1) * 128], lhsT=ka, rhs=qa, start=True, stop=True,
                                skip_group_check=True), reads=[kr, "QT"], writes=[f"pb{sl}"])
                        return klist, has_prev

                    def emit_R(t, klist, has_prev, tiles=tiles, hd=hd, first=first):
                        d_i, d, ti, mb, mbres = tiles[t]
                        vx4 = VX[d_i].rearrange("p (m t c) -> p m t c", m=2, t=16)
                        sl = t % NSB
                        stb = pbank[sl][:, 0:256]
                        tb, pt = tbuf[sl], PT[sl]
                        c_lo = 0 if has_prev else 128
                        sc.add("dve", lambda e, tb=tb, stb=stb, c_lo=c_lo, mb=mb: e.tensor_tensor(
                            out=tb[:, c_lo:256], in0=stb[:, c_lo:256], in1=mb[:, c_lo:256], op=ALU.add),
                            reads=[f"pb{sl}", mbres], writes=[f"tb{sl}"])
                        sc.add("act", lambda e, tb=tb, pt=pt, c_lo=c_lo: e.activation(
                            out=pt[:, c_lo:256], in_=tb[:, c_lo:256], func=AF.Exp),
                            reads=[f"tb{sl}"], writes=[f"pt{sl}"])
                        for w, ka, m_i, tkk, kr in klist:
                            vt = vx4[:, m_i, tkk, hd * 128:(hd + 1) * 128]
                            if d == 16:
                                outs = [(b, accb[b][:, ti: ti + 31 * 16 + 1: 16],
                                         pt[:, w * 128 + b * 32: w * 128 + (b + 1) * 32]) for b in range(4)]
                            elif d == 4:
                                b, r = ti // 4, ti % 4
                                outs = [(b, accb[b][:, r: r + 127 * 4 + 1: 4], pt[:, w * 128:(w + 1) * 128])]
                            else:
                                b = ti // 4
                                outs = [(b, accb[b][:, (ti % 4) * 128:(ti % 4 + 1) * 128],
                                         pt[:, w * 128:(w + 1) * 128])]
                            for b, o_ap, r_ap in outs:
                                st_flag = first[b]
                                first[b] = False
                                sc.add("pe", lambda e, vt=vt, o_ap=o_ap, r_ap=r_ap, st_flag=st_flag: e.matmul(
                                    o_ap, lhsT=vt, rhs=r_ap, start=st_flag, stop=False, skip_group_check=True),
                                    reads=[f"pt{sl}"] + [f"VX{d_i}_{mm}_{q}" for mm in range(2) for q in range(4)],
                                    writes=[f"acc{b}"])

                    pend = {}
                    LA = NSB - 1
                    for t in range(min(LA, len(tiles))):
                        pend[t] = emit_S(t)
                    for t in range(len(tiles)):
                        if t + LA < len(tiles):
                            pend[t + LA] = emit_S(t + LA)
                        kl, hp = pend.pop(t)
                        emit_R(t, kl, hp)
                    for b in range(4):
                        up = slice(base, base + 64)
                        dn = slice(64 - base, 128 - base)
                        sc.add("dve", lambda e, b=b, up=up, dn=dn: e.reciprocal(out=rec[up, :], in_=accb[b][dn, :]),
                               reads=[f"acc{b}"], writes=["rec"])
                        sc.add("dve", lambda e, b=b, up=up: e.tensor_tensor(out=t2[up, :], in0=accb[b][up, :],
                                                                            in1=rec[up, :], op=ALU.mult),
                               reads=[f"acc{b}", "rec"], writes=["t2"])
                        sc.add("pool", lambda e, b=b, up=up: e.tensor_tensor(
                            out=yTB[up, b * 512:(b + 1) * 512], in0=t2[up, :], in1=GT[up, b * 512:(b + 1) * 512],
                            op=ALU.mult), reads=["t2", "GT"], writes=["yTB"])
                for q in range(2):
                    dma(ytB[2 * n + q, pr * 128:(pr + 1) * 128, :], yTB[:, q * 1024:(q + 1) * 1024], "ytbst",
                        reads=["yTB"], writes=[f"ytB{pr}_{n}_{q}"])
                if pr == NPAIR - 1 and os.environ.get("KCC", "1") == "1":
                    dsem("agB")
                    for q in range(2):
                        sc.add("pool", lambda e, c8=2 * n + q: e.collective_compute(
                            "AllGather", ALU.bypass, replica_groups=PAIRS, ins=[ytB[c8]], outs=[ygB[c8]]),
                            reads=[f"ytB{p}_{n}_{q}" for p in range(NPAIR)], writes=[f"ygB{2 * n + q}"],
                            dma="agB", inc=1)
        sc.barrier()
        _chk("B")
        dsem("agB")

        load_weights(wo_in, False)
        fgain = a16[:, 0:4096].bitcast(F32)
        dma(fgain, fg_in[0:1, :].partition_broadcast(128), "c_fgain", writes=["fgain"])
        sc.barrier()
        _chk("WO")
        xr = [a32[:, 0:2048], a32[:, 2048:4096]]
        xn = a32[:, 4096:6144]
        ot = a16[:, 4096:8192].bitcast(F32)
        junkc = a16[:, 8192:10240]
        thc = {}

        def th_of(e):
            if "th" not in thc:
                thc["th"] = e.partition_id() % 2
            return thc["th"]

        def load_yk(tg):
            yk = hx[tg % 2]
            ykres = f"hx{tg % 2}"
            for part, yg, ygres in ((0, ygA, "ygA"), (1, ygB, "ygB")):
                ygres = [f"{ygres}{c8}" for c8 in range(8)]
                dsem(f"ykld{tg % 2}_{part}")
                sc.add("sp", lambda e, yk=yk, yg=yg, part=part, tg=tg: e.dma_start(
                    out=yk[:, part * 8:(part + 1) * 8, :],
                    in_=yg[bass.ts(th_of(e) * 4 + tg // 2, 1), :, (tg % 2) * 512:(tg % 2 + 1) * 512].rearrange(
                        "o (k p) t -> p (o k) t", p=128)),
                    reads=ygres, writes=[ykres + "ab"[part]], dma=f"ykld{tg % 2}_{part}")

        def load_xr(tix):
            dma(xr[tix % 2], xres_in[tix * 128:(tix + 1) * 128, :], f"xr{tix % 2}", writes=[f"xr{tix % 2}"])

        load_yk(0)
        load_xr(0)
        for tg in range(8):
            yk = hx[tg % 2]
            ykres = f"hx{tg % 2}"
            if tg + 1 < 8:
                load_yk(tg + 1)
            for j in range(4):
                tix = tg * 4 + j
                xs = xr[tix % 2]
                xsres = f"xr{tix % 2}"
                if tix + 1 < 32:
                    load_xr(tix + 1)
                for cg in range(4):
                    for k in range(NKC):
                        sc.add("pe", lambda e, yk=yk, j=j, k=k, cg=cg: e.matmul(
                            accb[cg], lhsT=yk[:, k, j * 128:(j + 1) * 128], rhs=wbuf[:, k, cg * 512:(cg + 1) * 512],
                            start=(k == 0), stop=(k == NKC - 1)), reads=[ykres + "a", ykres + "b", f"wbuf{k}"], writes=[f"acc{cg}"])
                sc.add("dve", lambda e, xs=xs: e.tensor_tensor(out=xn, in0=pacc[:, :], in1=xs, op=ALU.add),
                       reads=[xsres] + [f"acc{c}" for c in range(4)], writes=["xn"])
                sc.add("act", lambda e, tix=tix: e.activation(out=junkc, in_=xn, func=AF.Square,
                                                              accum_out=ssq2[:, tix:tix + 1]),
                       reads=["xn"], writes=["junkc", f"ssq2{tix}"])
                sc.add("act", lambda e, tix=tix: e.activation(out=rstd2[:, tix:tix + 1], in_=ssq2[:, tix:tix + 1],
                                                              func=AF.Sqrt, scale=1.0 / D, bias=EPS),
                       reads=[f"ssq2{tix}"], writes=[f"rstd2{tix}"])
                sc.add("dve", lambda e, tix=tix: e.reciprocal(out=rstd2[:, tix:tix + 1], in_=rstd2[:, tix:tix + 1]),
                       reads=[f"rstd2{tix}"], writes=[f"rstd2{tix}"])
                sc.add("dve", lambda e, tix=tix: e.scalar_tensor_tensor(out=ot, in0=xn, scalar=rstd2[:, tix:tix + 1],
                                                                        in1=fgain, op0=ALU.mult, op1=ALU.mult),
                       reads=["xn", f"rstd2{tix}", "fgain"], writes=["ot"])
                dma(out[tix * 128:(tix + 1) * 128, :], ot, "outst", reads=["ot"], writes=[f"out{tix}"])
        if os.environ.get("KDBG", "0") == "1":
            dbgA = dt("dbgA", [1024, 1024], BF16, kind="ExternalOutput").ap()
            dbgB = dt("dbgB", [1024, 1024], BF16, kind="ExternalOutput").ap()
            dbgV = dt("dbgV", [1024, 1024], BF16, kind="ExternalOutput").ap()
            for q in range(8):
                dma(dbgA[q * 128:(q + 1) * 128, :], ygA[0, q * 128:(q + 1) * 128, :], "dbg", reads=[f"ygA{c8}" for c8 in range(8)], writes=[f"dbgA{q}"])
                dma(dbgB[q * 128:(q + 1) * 128, :], ygB[0, q * 128:(q + 1) * 128, :], "dbg", reads=[f"ygB{c8}" for c8 in range(8)], writes=[f"dbgB{q}"])
                dma(dbgV[q * 128:(q + 1) * 128, :], vd[q * 128:(q + 1) * 128, :, :].rearrange("t h c -> t (h c)"), "dbg", writes=[f"dbgV{q}"])
            sc.add("sp", lambda e: e.nop(), reads=[f"dbg{w}{q}" for w in "ABV" for q in range(8)])
        sc.add("sp", lambda e: e.nop(), reads=[f"out{t}" for t in range(32)])

        sc.emit(block, sems, dma_sems)
    return nc


_CACHE = {}


def _consts():
    ident = np.eye(128, dtype=np.float32).astype(ml_dtypes.bfloat16)
    t = np.arange(512)
    mask512 = np.broadcast_to((t % 64 != 0).astype(np.float32)[None, :], (128, 512)).copy()
    s = np.arange(128)[:, None]
    c = np.arange(128)[None, :]
    hm = ((s // 64 == c // 64) & (s <= c)).astype(np.float32)
    hmask = np.tile(hm, (1, 4)).astype(ml_dtypes.bfloat16)
    j = np.arange(128)[:, None]
    i = np.arange(128)[None, :]
    dprev = np.where(j >= i, 128.0 + i - j, BIG)
    dcur = np.where(j <= i, (i - j).astype(np.float64), BIG)
    dmat = np.concatenate([dprev, dcur], axis=1).astype(np.float32)
    return ident, mask512, hmask, dmat


def kernel(x, norm_gain, w_in, lb_logits, hgrn_gnorm, w_out, final_gain):
    x = np.asarray(x, np.float32)
    w_in = np.asarray(w_in, np.float32)[0]
    w_out = np.asarray(w_out, np.float32)[0]
    norm_gain = np.asarray(norm_gain, np.float32)[0]
    lb_logits = np.asarray(lb_logits, np.float32)
    gnorm = np.asarray(hgrn_gnorm, np.float32)[0]
    final_gain = np.asarray(final_gain, np.float32)
    if "nc" not in _CACHE:
        _CACHE["nc"] = build_program()
    nc = _CACHE["nc"]
    ident, mask512, hmask, dmat = _consts()
    slopes = [2.0 ** (-(h + 1) / 2.0) for h in range(16)]
    gain_l = np.ascontiguousarray(norm_gain.reshape(NKC, 128).T)
    in_maps = []
    for c in range(8):
        b, hh = c // 2, c % 2
        hcols = []
        for grp in range(4):
            for h in range(HH):
                gh = hh * HH + h
                hcols.append(np.arange(grp * 1024 + gh * 128, grp * 1024 + (gh + 1) * 128))
        wa = np.ascontiguousarray(w_in[:, np.concatenate(hcols)])
        acols = []
        for grp in range(4):
            for a in range(8):
                ga = hh * 8 + a
                acols.append(np.arange(4096 + grp * 1024 + ga * 64, 4096 + grp * 1024 + (ga + 1) * 64))
        wb = np.ascontiguousarray(w_in[:, np.concatenate(acols)])
        lbl = np.zeros((128, 2 * HH), np.float32)
        for h in range(HH):
            gh = hh * HH + h
            lbl[:, h] = lb_logits[0, gh * 128:(gh + 1) * 128]
            lbl[:, HH + h] = lb_logits[1, gh * 128:(gh + 1) * 128]
        cvv = np.zeros((128, 24), np.float32)
        for a in range(8):
            for d_i, d in enumerate((1, 4, 16)):
                cvv[:, a * 3 + d_i] = np.float32(-slopes[hh * 8 + a] * d)
        in_maps.append({
            "x": np.ascontiguousarray(x[b]),
            "xres": np.ascontiguousarray(x[b, hh * (S // 2):(hh + 1) * (S // 2)]),
            "wa": wa, "wb": wb, "wo": np.ascontiguousarray(w_out),
            "gain": gain_l, "lbl": lbl, "gn": np.ascontiguousarray(gnorm.reshape(128, 1)),
            "fg": np.ascontiguousarray(final_gain.reshape(1, D)), "cv": cvv,
            "ident": ident, "mask512": mask512, "hmask": hmask, "dmat": dmat,
        })
    res = run_bass_kernel_spmd(nc, in_maps, core_ids=list(range(8)))
    _CACHE["res"] = res
    outp = np.empty((4, S, D), np.float32)
    for c in range(8):
        b, hh = c // 2, c % 2
        outp[b, hh * (S // 2):(hh + 1) * (S // 2)] = np.asarray(res.results[c]["out"], np.float32)
    return outp
```

```python
import contextlib
import numpy as np
import ml_dtypes
import concourse.bass as bass
import concourse.mybir as mybir
from concourse.bass_utils import run_bass_kernel_spmd

F32 = mybir.dt.float32
BF16 = mybir.dt.bfloat16
ALU = mybir.AluOpType
AF = mybir.ActivationFunctionType

S = 8192
D = 2048
NKC = 16
EPS = 1e-6
HH = 4
NPAIR = 4
BIG = 1.0e8
SAME_ENGINE_SYNC = True
PAIRS = [[0, 1], [2, 3], [4, 5], [6, 7]]


class _Op:
    __slots__ = ("eng", "fn", "deps", "signal", "dma_key", "dma_val", "count", "inc")

    def __init__(self, eng, fn, deps, dma_key, inc):
        self.eng, self.fn, self.deps, self.dma_key, self.inc = eng, fn, deps, dma_key, inc
        self.signal = False
        self.dma_val = None
        self.count = None


class Sched:
    ENGS = ("pe", "act", "dve", "pool", "sp")

    def __init__(self):
        self.ops = {e: [] for e in self.ENGS}
        self.res_w = {}
        self.res_r = {}
        self.dma_cnt = {}
        self.dma_last = {}
        self.bar_deps = set()
        self.bar_pending = set()
        self.halted = False

    def add(self, eng, fn, reads=(), writes=(), dma=None, inc=16):
        if self.halted:
            return None
        deps = set()
        for r in reads:
            if r in self.res_w:
                deps.add(self.res_w[r])
        for w in writes:
            if w in self.res_w:
                deps.add(self.res_w[w])
            last = {}
            for e in self.res_r.get(w, ()):
                if e.dma_key is not None:
                    deps.add(e)
                else:
                    last[e.eng] = e
            deps.update(last.values())
        if eng in self.bar_pending:
            deps |= self.bar_deps
            self.bar_pending.discard(eng)
        op = _Op(eng, fn, deps, dma, inc)
        if dma is not None:
            self.dma_cnt[dma] = self.dma_cnt.get(dma, 0) + inc
            op.dma_val = self.dma_cnt[dma]
            self.dma_last[dma] = op
        for d in deps:
            if d.dma_key is None:
                d.signal = True
        self.ops[eng].append(op)
        for r in reads:
            self.res_r.setdefault(r, []).append(op)
        for w in writes:
            self.res_w[w] = op
            self.res_r[w] = []
        return op

    def barrier(self):
        deps = set(op for key, op in self.dma_last.items() if key not in ("agA", "agB"))
        for e in self.ENGS:
            if self.ops[e]:
                deps.add(self.ops[e][-1])
        self.bar_deps = deps
        self.bar_pending = set(self.ENGS)

    def emit(self, block, sems, dma_sems):
        for e in self.ENGS:
            c = 0
            for op in self.ops[e]:
                if op.dma_key is None and op.signal:
                    c += 1
                    op.count = c
        handles = {"pe": "tensor", "act": "scalar", "dve": "vector", "pool": "gpsimd", "sp": "sync"}

        def make(e):
            def body(engine):
                seen = {}
                for op in self.ops[e]:
                    need = {}
                    for d in op.deps:
                        if d.dma_key is not None:
                            sem, val, key = dma_sems[d.dma_key], d.dma_val, ("d", d.dma_key)
                        else:
                            if d.eng == e and (e in ("pe", "sp") or not SAME_ENGINE_SYNC):
                                continue
                            sem, val, key = sems[d.eng], d.count, ("e", d.eng)
                        if key not in need or need[key][1] < val:
                            need[key] = (sem, val)
                    for key, (sem, val) in need.items():
                        if seen.get(key, 0) >= val:
                            continue
                        seen[key] = val
                        engine.wait_ge(sem, val)
                    ins = op.fn(engine)
                    if op.dma_key is not None:
                        ins.then_inc(dma_sems[op.dma_key], op.inc)
                    elif op.signal:
                        ins.then_inc(sems[e], 1)
            return body

        for e in self.ENGS:
            getattr(block, handles[e])(make(e))


def build_program():
    nc = bass.Bass("TRN2", target_bir_lowering=False)
    dt = nc.dram_tensor
    x_in = dt("x", [S, D], F32, kind="ExternalInput").ap()
    xres_in = dt("xres", [S // 2, D], F32, kind="ExternalInput").ap()
    wa_in = dt("wa", [D, 2048], F32, kind="ExternalInput").ap()
    wb_in = dt("wb", [D, 2048], F32, kind="ExternalInput").ap()
    wo_in = dt("wo", [D, D], F32, kind="ExternalInput").ap()
    gain_in = dt("gain", [128, NKC], F32, kind="ExternalInput").ap()
    lbl_in = dt("lbl", [128, 2 * HH], F32, kind="ExternalInput").ap()
    gn_in = dt("gn", [128, 1], F32, kind="ExternalInput").ap()
    fg_in = dt("fg", [1, D], F32, kind="ExternalInput").ap()
    cv_in = dt("cv", [128, 24], F32, kind="ExternalInput").ap()
    ident_in = dt("ident", [128, 128], BF16, kind="ExternalInput").ap()
    mask512_in = dt("mask512", [128, 512], F32, kind="ExternalInput").ap()
    hmask_in = dt("hmask", [128, 512], BF16, kind="ExternalInput").ap()
    dmat_in = dt("dmat", [128, 256], F32, kind="ExternalInput").ap()
    out = dt("out", [S // 2, D], F32, kind="ExternalOutput").ap()

    hT = dt("hT", [NKC, 128, S], BF16).ap()
    ytA = dt("ytA", [8, 512, 1024], BF16).ap()
    ytB = dt("ytB", [8, 512, 1024], BF16).ap()
    ygA = dt("ygA", [8, 1024, 1024], BF16).ap()
    ygB = dt("ygB", [8, 1024, 1024], BF16).ap()
    vd = dt("vd", [S, 8, 128], BF16).ap()

    sc = Sched()
    WB = [f"wbuf{k}" for k in range(NKC)]

    with contextlib.ExitStack() as es:
        def sb(name, shape, dtype):
            return es.enter_context(nc.sbuf_tensor(name, shape, dtype))

        def ps(name, shape, dtype):
            return es.enter_context(nc.psum_tensor(name, shape, dtype))

        wbuf = sb("wbuf", [128, NKC, 2048], BF16)
        hx = [sb(f"hx{i}", [128, NKC, 512], BF16) for i in range(2)]
        a32 = sb("a32", [128, 6144], F32)
        a16 = sb("a16", [128, 38400], BF16)
        ident = sb("ident_s", [128, 128], BF16)
        ones = sb("ones_s", [128, 128], BF16)
        mask512 = sb("mask512_s", [128, 512], F32)
        hmask = sb("hmask_s", [128, 512], BF16)
        dmat = sb("dmat_s", [128, 256], F32)
        gain = sb("gain_s", [128, NKC], F32)
        lbl = sb("lbl_s", [128, 2 * HH], F32)
        lb = sb("lb_s", [128, HH], F32)
        oml = sb("oml_s", [128, HH], F32)
        noml = sb("noml_s", [128, HH], F32)
        gn = sb("gn_s", [128, 1], F32)
        cv = sb("cv_s", [128, 24], F32)
        ssq = sb("ssq_s", [128, 64], F32)
        rstd = sb("rstd_s", [128, 64], F32)
        ssq2 = sb("ssq2_s", [128, 32], F32)
        rstd2 = sb("rstd2_s", [128, 32], F32)
        s32 = sb("s32_s", [128, HH, 128], F32)
        s16 = sb("s16_s", [128, HH, 128], BF16)
        ebl4 = sb("ebl_s", [128, HH, 8], F32)

        pbank = [ps(f"pb{i}", [128, 512], F32) for i in range(4)]
        pacc = ps("pacc", [128, 2048], F32)

        sems = {e: es.enter_context(nc.semaphore(f"sem_{e}")) for e in ("pe", "act", "dve", "pool")}
        dma_sems = {}

        def dsem(key):
            if key not in dma_sems:
                dma_sems[key] = es.enter_context(nc.semaphore(f"dsem_{len(dma_sems)}"))
            return key

        block = es.enter_context(nc.Block())

        def dma(out_ap, in_ap, key, reads=(), writes=()):
            dsem(key)
            return sc.add("sp", lambda e, o=out_ap, i=in_ap: e.dma_start(out=o, in_=i),
                          reads=reads, writes=writes, dma=key)

        dma(ident[:], ident_in[:, :], "c_ident", writes=["ident"])
        dma(mask512[:], mask512_in[:, :], "c_mask512", writes=["mask512"])
        dma(hmask[:], hmask_in[:, :], "c_hmask", writes=["hmask"])
        dma(dmat[:], dmat_in[:, :], "c_dmat", writes=["dmat"])
        dma(gain[:], gain_in[:, :], "c_gain", writes=["gain"])
        dma(lbl[:], lbl_in[:, :], "c_lbl", writes=["lbl"])
        dma(gn[:], gn_in[:, :], "c_gn", writes=["gn"])
        dma(cv[:], cv_in[:, :], "c_cv", writes=["cv"])
        sc.add("dve", lambda e: e.memset(ones[:], 1.0), writes=["ones"])
        sc.add("dve", lambda e: e.tensor_tensor(out=lb[:], in0=lbl[:, 0:HH], in1=lbl[:, HH:2 * HH],
                                                op=ALU.subtract), reads=["lbl"], writes=["lb"])
        sc.add("act", lambda e: e.activation(out=lb[:], in_=lb[:], func=AF.Sigmoid),
               reads=["lb"], writes=["lb"])
        sc.add("dve", lambda e: e.tensor_scalar(out=oml[:], in0=lb[:], scalar1=-1.0, scalar2=1.0,
                                                op0=ALU.mult, op1=ALU.add), reads=["lb"], writes=["oml"])
        sc.add("dve", lambda e: e.tensor_scalar(out=noml[:], in0=oml[:], scalar1=-1.0, scalar2=None,
                                                op0=ALU.mult), reads=["oml"], writes=["noml"])

        def load_weights(w_in, use_gain):
            for k in range(NKC):
                slot = k % 2
                st = a32[:, slot * 2048:(slot + 1) * 2048]
                dma(st, w_in[k * 128:(k + 1) * 128, :], f"wst{slot}", writes=[f"wst{slot}"])
                if k % 2 == 0:
                    if use_gain:
                        fn = lambda e, st=st, k=k: e.tensor_scalar(out=wbuf[:, k, :], in0=st,
                                                                   scalar1=gain[:, k:k + 1], scalar2=None,
                                                                   op0=ALU.mult)
                    else:
                        fn = lambda e, st=st, k=k: e.tensor_copy(out=wbuf[:, k, :], in_=st)
                    sc.add("dve", fn, reads=[f"wst{slot}", "gain"], writes=[f"wbuf{k}"])
                else:
                    if use_gain:
                        fn = lambda e, st=st, k=k: e.activation(out=wbuf[:, k, :], in_=st, func=AF.Copy,
                                                                scale=gain[:, k:k + 1])
                    else:
                        fn = lambda e, st=st, k=k: e.activation(out=wbuf[:, k, :], in_=st, func=AF.Copy)
                    sc.add("act", fn, reads=[f"wst{slot}", "gain"], writes=[f"wbuf{k}"])

        import os
        _stop = os.environ.get("KSTOP", "")

        def _chk(tag):
            if _stop == tag:
                sc.halted = True

        load_weights(wa_in, True)
        sc.barrier()
        _chk("W")

        xt = [a32[:, i * 2048:(i + 1) * 2048] for i in range(2)]
        hbs = [a16[:, 0:2048], a16[:, 4096:6144]]
        junk = a16[:, 2048:4096]
        NG = int(os.environ.get('KNG', 16))
        for g in range(NG):
            hslot = hx[g % 2]
            hres = f"hx{g % 2}"
            for j in range(4):
                i = g * 4 + j
                xs = xt[i % 2]
                xres_ = f"xt{i % 2}"
                if i == 0:
                    dma(xs, x_in[0:128, :], xres_, writes=[xres_])
                if i + 1 < NG * 4:
                    dma(xt[(i + 1) % 2], x_in[(i + 1) * 128:(i + 2) * 128, :], f"xt{(i + 1) % 2}",
                        writes=[f"xt{(i + 1) % 2}"])
                sc.add("act", lambda e, xs=xs, i=i: e.activation(out=junk, in_=xs, func=AF.Square,
                                                                 accum_out=ssq[:, i:i + 1]),
                       reads=[xres_], writes=["junk", f"ssq{i}"])
                sc.add("act", lambda e, i=i: e.activation(out=rstd[:, i:i + 1], in_=ssq[:, i:i + 1],
                                                          func=AF.Sqrt, scale=1.0 / D, bias=EPS),
                       reads=[f"ssq{i}"], writes=[f"rstd{i}"])
                sc.add("dve", lambda e, i=i: e.reciprocal(out=rstd[:, i:i + 1], in_=rstd[:, i:i + 1]),
                       reads=[f"rstd{i}"], writes=[f"rstd{i}"])
                hb = hbs[i % 2]
                hbres = f"hb{i % 2}"
                sc.add("act", lambda e, xs=xs, i=i, hb=hb: e.activation(out=hb, in_=xs, func=AF.Copy,
                                                                        scale=rstd[:, i:i + 1]),
                       reads=[xres_, f"rstd{i}"], writes=[hbres])
                pp = (i % 2) * 2
                for k in range(NKC):
                    o = pbank[pp + k // 8][:].bitcast(BF16)[:, (k % 8) * 128:(k % 8 + 1) * 128]
                    sc.add("pe", lambda e, o=o, k=k, hb=hb: e.transpose(o, hb[:, k * 128:(k + 1) * 128], ident[:]),
                           reads=[hbres, "ident"], writes=[f"pb{pp + k // 8}"])
                for half in range(2):
                    src = pbank[pp + half][:].bitcast(BF16).rearrange("p (k t) -> p k t", k=8)
                    dst = hslot[:, half * 8:(half + 1) * 8, j * 128:(j + 1) * 128]
                    if half == 0:
                        sc.add("act", lambda e, s=src, d=dst: e.activation(out=d, in_=s, func=AF.Copy),
                               reads=[f"pb{pp + half}"], writes=[hres])
                    else:
                        sc.add("dve", lambda e, s=src, d=dst: e.tensor_copy(out=d, in_=s),
                               reads=[f"pb{pp + half}"], writes=[hres])
            dma(hT[:, :, g * 512:(g + 1) * 512].rearrange("k p t -> p k t"), hslot[:], f"hTst{g % 2}",
                reads=[hres], writes=[f"hT{g}"])
        sc.barrier()
        _chk("P")

        o32 = [0]
        o16 = [0]

        def f32buf(n):
            v = a32[:, o32[0]:o32[0] + n]
            o32[0] += n
            return v

        def b16buf(n):
            v = a16[:, o16[0]:o16[0] + n]
            o16[0] += n
            return v

        qs, sg, ff, lf, bb, e1, e2, kk, kd32, rr, tt = [f32buf(512) for _ in range(11)]
        kd16, ke16, sq, yT = [b16buf(512) for _ in range(4)]
        qd4 = [b16buf(512) for _ in range(HH)]
        keT4 = [b16buf(512) for _ in range(HH)]
        am4 = [b16buf(512) for _ in range(HH)]
        gg4 = [b16buf(512) for _ in range(HH)]
        v16 = b16buf(2048)
        v163 = v16.rearrange("p (j c) -> p j c", j=4)

        sc.add("dve", lambda e: e.memset(s32[:], 0.0), writes=[f"s32{h}" for h in range(HH)])
        sc.add("dve", lambda e: e.memset(s16[:], 0.0), writes=[f"s16{h}" for h in range(HH)])
        pk = pbank[1][:, 0:256].bitcast(BF16)
        oacc = [pacc[:, h * 512:(h + 1) * 512] for h in range(HH)]

        NBA = int(os.environ.get('KNB', 16))
        for blk in range(NBA):
            hs = hx[blk % 2]
            hres = f"hx{blk % 2}"
            if blk == 0:
                dma(hs[:], hT[:, :, 0:512].rearrange("k p t -> p k t"), "hxld0", reads=["hT0"], writes=[hres])
            if blk + 1 < NBA:
                dma(hx[(blk + 1) % 2][:], hT[:, :, (blk + 1) * 512:(blk + 2) * 512].rearrange("k p t -> p k t"),
                    f"hxld{(blk + 1) % 2}", reads=[f"hT{blk + 1}"], writes=[f"hx{(blk + 1) % 2}"])
            for j in range(4):
                vb = j % 2
                for k in range(NKC):
                    sc.add("pe", lambda e, j=j, k=k, hs=hs, vb=vb: e.matmul(
                        pbank[vb][:, :], lhsT=hs[:, k, j * 128:(j + 1) * 128], rhs=wbuf[:, k, 1024:1536],
                        start=(k == 0), stop=(k == NKC - 1)), reads=[hres, f"wbuf{k}"], writes=[f"pb{vb}"])
                sc.add("act", lambda e, j=j, vb=vb: e.activation(out=v163[:, j, :], in_=pbank[vb][:, :], func=AF.Copy),
                       reads=[f"pb{vb}"], writes=["v16"])
            for hd in range(HH):
                qd, keT, am, gg = qd4[hd], keT4[hd], am4[hd], gg4[hd]
                keT3 = keT.rearrange("p (j d) -> p j d", j=4)
                R = lambda nm, hd=hd: f"{nm}{hd}"

                def proj(bank, c0, hs=hs, hres=hres):
                    for k in range(NKC):
                        sc.add("pe", lambda e, k=k, bank=bank, c0=c0, hs=hs: e.matmul(
                            pbank[bank][:, :], lhsT=wbuf[:, k, c0:c0 + 128], rhs=hs[:, k, :],
                            start=(k == 0), stop=(k == NKC - 1)), reads=[hres, f"wbuf{k}"], writes=[f"pb{bank}"])
                proj(0, hd * 128)
                sc.add("act", lambda e: e.activation(out=qs, in_=pbank[0][:, :], func=AF.Silu),
                       reads=["pb0"], writes=["qs"])
                proj(1, 512 + hd * 128)
                sc.add("act", lambda e: e.activation(out=sg, in_=pbank[1][:, :], func=AF.Sigmoid),
                       reads=["pb1"], writes=["sg"])
                proj(0, 1536 + hd * 128)
                sc.add("act", lambda e, gg=gg: e.activation(out=gg, in_=pbank[0][:, :], func=AF.Silu),
                       reads=["pb0"], writes=[R("gg")])
                sc.add("dve", lambda e, hd=hd: e.tensor_scalar(out=ff, in0=sg, scalar1=oml[:, hd:hd + 1],
                                                               scalar2=lb[:, hd:hd + 1], op0=ALU.mult, op1=ALU.add),
                       reads=["sg", "oml", "lb"], writes=["ff"])
                sc.add("pool", lambda e, hd=hd: e.tensor_scalar(out=kk, in0=sg, scalar1=noml[:, hd:hd + 1],
                                                                scalar2=oml[:, hd:hd + 1], op0=ALU.mult, op1=ALU.add),
                       reads=["sg", "oml", "noml"], writes=["kk"])
                sc.add("dve", lambda e, gg=gg: e.tensor_scalar(out=gg, in0=gg, scalar1=gn[:, 0:1], scalar2=None,
                                                               op0=ALU.mult), reads=[R("gg"), "gn"], writes=[R("gg")])
                sc.add("act", lambda e: e.activation(out=lf, in_=ff, func=AF.Ln), reads=["ff"], writes=["lf"])
                sc.add("dve", lambda e: e.tensor_tensor_scan(out=bb, data0=mask512[:], data1=lf, initial=0.0,
                                                             op0=ALU.mult, op1=ALU.add),
                       reads=["lf", "mask512"], writes=["bb"])
                sc.add("act", lambda e: e.activation(out=e1, in_=bb, func=AF.Exp), reads=["bb"], writes=["e1"])
                sc.add("act", lambda e: e.activation(out=e2, in_=bb, func=AF.Exp, scale=-1.0),
                       reads=["bb"], writes=["e2"])
                sc.add("dve", lambda e, hd=hd: e.tensor_copy(out=ebl4[:, hd, :],
                                                             in_=e1.rearrange("p (n t) -> p n t", t=64)[:, :, 63]),
                       reads=["e1"], writes=[R("ebl")])
                sc.add("pool", lambda e, qd=qd: e.tensor_tensor(out=qd, in0=qs, in1=e1, op=ALU.mult),
                       reads=["qs", "e1"], writes=[R("qd")])
                sc.add("dve", lambda e: e.tensor_tensor(out=kd32, in0=kk, in1=e2, op=ALU.mult),
                       reads=["kk", "e2"], writes=["kd32"])
                sc.add("act", lambda e: e.activation(out=kd16, in_=kd32, func=AF.Copy), reads=["kd32"], writes=["kd16"])
                sc.add("dve", lambda e, hd=hd: e.tensor_tensor(
                    out=ke16.rearrange("p (n t) -> p n t", t=64), in0=kd32.rearrange("p (n t) -> p n t", t=64),
                    in1=ebl4[:, hd, :].unsqueeze(2).to_broadcast([128, 8, 64]), op=ALU.mult),
                    reads=["kd32", R("ebl")], writes=["ke16"])
                for j in range(4):
                    sc.add("pe", lambda e, j=j: e.transpose(pk[:, j * 128:(j + 1) * 128],
                                                            ke16[:, j * 128:(j + 1) * 128], ident[:]),
                           reads=["ke16", "ident"], writes=["pb1"])
                sc.add("act", lambda e, keT=keT: e.activation(out=keT, in_=pk, func=AF.Copy),
                       reads=["pb1"], writes=[R("keT")])
                for j in range(4):
                    sc.add("pe", lambda e, j=j, qd=qd: e.matmul(pbank[0][:, j * 128:(j + 1) * 128],
                                                                lhsT=kd16[:, j * 128:(j + 1) * 128],
                                                                rhs=qd[:, j * 128:(j + 1) * 128], start=True, stop=True,
                                                                skip_group_check=True),
                           reads=["kd16", R("qd")], writes=["pb0"])
                sc.add("dve", lambda e, am=am: e.tensor_tensor(out=am, in0=pbank[0][:, :], in1=hmask[:], op=ALU.mult),
                       reads=["pb0", "hmask"], writes=[R("am")])
            step = 0
            for n in range(8):
                j, half = n // 2, n % 2
                rows = slice(half * 64, half * 64 + 64)
                for hd in range(HH):
                    qd, keT, am = qd4[hd], keT4[hd], am4[hd]
                    keT3 = keT.rearrange("p (j d) -> p j d", j=4)
                    R = lambda nm, hd=hd: f"{nm}{hd}"
                    kb = 2 + step % 2
                    step += 1
                    kvb = pbank[kb][:, 0:128]
                    kvres = f"pb{kb}"
                    sc.add("pe", lambda e, j=j, rows=rows, kvb=kvb, hd=hd, keT3=keT3: e.matmul(
                        kvb, lhsT=keT3[rows, j, :], rhs=v163[rows, j, hd * 128:(hd + 1) * 128],
                        start=True, stop=True), reads=[R("keT"), "v16"], writes=[kvres])
                    sc.add("pe", lambda e, n=n, hd=hd, qd=qd: e.matmul(
                        oacc[hd][:, n * 64:(n + 1) * 64], lhsT=s16[:, hd, :], rhs=qd[:, n * 64:(n + 1) * 64],
                        start=(n == 0), stop=False, skip_group_check=True),
                        reads=[R("s16"), R("qd")], writes=[R("oacc")])
                    if half == 1:
                        sc.add("pe", lambda e, j=j, hd=hd, am=am: e.matmul(
                            oacc[hd][:, j * 128:(j + 1) * 128], lhsT=v163[:, j, hd * 128:(hd + 1) * 128],
                            rhs=am[:, j * 128:(j + 1) * 128], start=False, stop=True, skip_group_check=True),
                            reads=["v16", R("am")], writes=[R("oacc")])
                    sc.add("dve", lambda e, n=n, kvb=kvb, hd=hd: e.scalar_tensor_tensor(
                        out=s32[:, hd, :], in0=s32[:, hd, :], scalar=ebl4[:, hd, n:n + 1], in1=kvb,
                        op0=ALU.mult, op1=ALU.add), reads=[R("s32"), R("ebl"), kvres], writes=[R("s32")])
                    sc.add("act", lambda e, hd=hd: e.activation(out=s16[:, hd, :], in_=s32[:, hd, :], func=AF.Copy),
                           reads=[R("s32")], writes=[R("s16")])
            for hd in range(HH):
                gg = gg4[hd]
                R = lambda nm, hd=hd: f"{nm}{hd}"
                kb = 2 + hd % 2
                sc.add("act", lambda e, hd=hd: e.activation(out=sq, in_=oacc[hd], func=AF.Square),
                       reads=[R("oacc")], writes=["sq"])
                sc.add("pe", lambda e, kb=kb: e.matmul(pbank[kb][:, :], lhsT=ones[:], rhs=sq, start=True, stop=True),
                       reads=["sq", "ones"], writes=[f"pb{kb}"])
                sc.add("act", lambda e, kb=kb: e.activation(out=rr, in_=pbank[kb][:, :], func=AF.Ln,
                                                            scale=1.0 / 128, bias=EPS), reads=[f"pb{kb}"], writes=["rr"])
                sc.add("act", lambda e: e.activation(out=rr, in_=rr, func=AF.Exp, scale=-0.5),
                       reads=["rr"], writes=["rr"])
                sc.add("dve", lambda e, hd=hd: e.tensor_tensor(out=tt, in0=oacc[hd], in1=rr, op=ALU.mult),
                       reads=[R("oacc"), "rr"], writes=["tt"])
                sc.add("dve", lambda e, gg=gg: e.tensor_tensor(out=yT, in0=tt, in1=gg, op=ALU.mult),
                       reads=["tt", R("gg")], writes=["yT"])
                dma(ytA[blk // 2, hd * 128:(hd + 1) * 128, (blk % 2) * 512:(blk % 2 + 1) * 512], yT, "ytst",
                    reads=["yT"], writes=[f"ytA{hd}_{blk}"])
        sc.barrier()
        _chk("A")
        dsem("agA")
        if os.environ.get("KCC", "1") == "1":
          for c8 in range(8):
            sc.add("pool", lambda e, c8=c8: e.collective_compute("AllGather", ALU.bypass, replica_groups=PAIRS,
                                                                 ins=[ytA[c8]], outs=[ygA[c8]]),
                   writes=[f"ygA{c8}"], dma="agA", inc=1)
        _chk("AGA")

        load_weights(wb_in, True)
        sc.barrier()
        _chk("WB")
        o32[0] = 0
        o16[0] = 0
        NSB = 4
        tbuf = [f32buf(256) for _ in range(NSB)]
        rec = f32buf(512)
        t2 = f32buf(512)
        mbuf = [f32buf(256) for _ in range(3)]
        QT = b16buf(2048)
        KT = b16buf(4096)
        GT = b16buf(2048)
        VX = [b16buf(8192) for _ in range(3)]
        vst = [b16buf(1024) for _ in range(2)]
        PT = [b16buf(256) for _ in range(NSB)]
        yTB = b16buf(2048)
        vT = b16buf(512)
        pvtok = pacc[:, 0:256].bitcast(BF16)
        for i in range(2):
            sc.add("dve", lambda e, i=i: e.memset(vst[i], 1.0), writes=[f"vst{i}"])

        def tile_ap(buf, base, d, ti, off=0):
            if d == 1:
                s0, st = 128 * ti, 1
            elif d == 4:
                s0, st = 512 * (ti // 4) + (ti % 4), 4
            else:
                s0, st = ti, 16
            return buf[base:base + 64, off + s0: off + s0 + 127 * st + 1: st]

        def prev_tile(d, ti):
            if d == 1:
                return (True, ti - 1) if ti > 0 else (False, 15)
            if d == 4:
                return (True, ti - 4) if ti >= 4 else (False, ti + 12)
            return (False, ti)

        accb = [pacc[:, b * 512:(b + 1) * 512] for b in range(4)]
        vcnt = 0
        bflat = [n_ * 4 + s_ for _p in range(int(os.environ.get('KNP', NPAIR)))
                 for n_ in range(int(os.environ.get('KNS', 4))) for s_ in range(4)]
        bseq = 0
        for pr in range(int(os.environ.get('KNP', NPAIR))):
            for n in range(int(os.environ.get('KNS', 4))):
                kslot = (n % 2) * 2048
                kres = f"KT{n % 2}"
                for sbk in range(4):
                    blk = n * 4 + sbk
                    hs = hx[blk % 2]
                    hres = f"hx{blk % 2}"
                    if bseq == 0:
                        dma(hs[:], hT[:, :, blk * 512:(blk + 1) * 512].rearrange("k p t -> p k t"), f"hxld{blk % 2}",
                            reads=[f"hT{blk}"], writes=[hres])
                    bseq += 1
                    if bseq < len(bflat):
                        nb_ = bflat[bseq]
                        dma(hx[nb_ % 2][:], hT[:, :, nb_ * 512:(nb_ + 1) * 512].rearrange("k p t -> p k t"),
                            f"hxld{nb_ % 2}", reads=[f"hT{nb_}"], writes=[f"hx{nb_ % 2}"])
                    for k in range(NKC):
                        sc.add("pe", lambda e, k=k, hs=hs, pr=pr: e.matmul(
                            pbank[3][:, :], lhsT=wbuf[:, k, 1024 + pr * 128: 1024 + (pr + 1) * 128], rhs=hs[:, k, :],
                            start=(k == 0), stop=(k == NKC - 1)), reads=[hres, f"wbuf{k}"], writes=["pb3"])
                    sc.add("act", lambda e: e.activation(out=vT, in_=pbank[3][:, :], func=AF.Copy),
                           reads=["pb3"], writes=["vT"])
                    QKG = (("q", pr * 128, 1),)
                    for which, c0, bank in QKG:
                        for k in range(NKC):
                            sc.add("pe", lambda e, k=k, bank=bank, c0=c0, hs=hs: e.matmul(
                                pbank[bank][:, :], lhsT=wbuf[:, k, c0:c0 + 128], rhs=hs[:, k, :],
                                start=(k == 0), stop=(k == NKC - 1)),
                                reads=[hres, f"wbuf{k}"], writes=[f"pb{bank}"])
                        if which == "q":
                            sc.add("act", lambda e, sbk=sbk: e.activation(out=QT[:, sbk * 512:(sbk + 1) * 512],
                                                                          in_=pbank[1][:, :], func=AF.Copy, scale=0.125),
                                   reads=["pb1"], writes=["QT"])
                        elif which == "k":
                            sc.add("act", lambda e, sbk=sbk, kslot=kslot: e.activation(
                                out=KT[:, kslot + sbk * 512: kslot + (sbk + 1) * 512], in_=pbank[2][:, :], func=AF.Copy),
                                reads=["pb2"], writes=[kres])
                        else:
                            sc.add("act", lambda e, sbk=sbk: e.activation(out=GT[:, sbk * 512:(sbk + 1) * 512],
                                                                          in_=pbank[0][:, :], func=AF.Silu),
                                   reads=["pb0"], writes=["GT"])
                    for j in range(4):
                        sc.add("pe", lambda e, j=j: e.transpose(pvtok[:, j * 128:(j + 1) * 128],
                                                                vT[:, j * 128:(j + 1) * 128], ident[:]),
                               reads=["vT", "ident"], writes=["acc0"])
                    vs = vst[vcnt % 2]
                    vres = f"vst{vcnt % 2}"
                    vs4 = vs.rearrange("p (j h c) -> p j h c", j=4, h=2)
                    pv3 = pvtok.rearrange("p (j c) -> p j c", j=4)
                    sc.add("dve", lambda e, vs4=vs4, pv3=pv3: e.tensor_copy(out=vs4[:, :, 0, 0:64], in_=pv3[:, :, 0:64]),
                           reads=["acc0"], writes=[vres])
                    sc.add("dve", lambda e, vs4=vs4, pv3=pv3: e.tensor_copy(out=vs4[:, :, 1, 64:128], in_=pv3[:, :, 64:128]),
                           reads=["acc0"], writes=[vres])
                    dma(vd[blk * 512:(blk + 1) * 512, pr * 2:(pr + 1) * 2, :].rearrange("(j p) h c -> p j h c", p=128),
                        vs4, f"vdst{vcnt % 2}", reads=[vres], writes=[f"vd{pr}_{blk}"])
                    vcnt += 1
                    QKG = (("k", 512 + pr * 128, 2), ("g", 1536 + pr * 128, 0))
                    for which, c0, bank in QKG:
                        for k in range(NKC):
                            sc.add("pe", lambda e, k=k, bank=bank, c0=c0, hs=hs: e.matmul(
                                pbank[bank][:, :], lhsT=wbuf[:, k, c0:c0 + 128], rhs=hs[:, k, :],
                                start=(k == 0), stop=(k == NKC - 1)),
                                reads=[hres, f"wbuf{k}"], writes=[f"pb{bank}"])
                        if which == "q":
                            sc.add("act", lambda e, sbk=sbk: e.activation(out=QT[:, sbk * 512:(sbk + 1) * 512],
                                                                          in_=pbank[1][:, :], func=AF.Copy, scale=0.125),
                                   reads=["pb1"], writes=["QT"])
                        elif which == "k":
                            sc.add("act", lambda e, sbk=sbk, kslot=kslot: e.activation(
                                out=KT[:, kslot + sbk * 512: kslot + (sbk + 1) * 512], in_=pbank[2][:, :], func=AF.Copy),
                                reads=["pb2"], writes=[kres])
                        else:
                            sc.add("act", lambda e, sbk=sbk: e.activation(out=GT[:, sbk * 512:(sbk + 1) * 512],
                                                                          in_=pbank[0][:, :], func=AF.Silu),
                                   reads=["pb0"], writes=["GT"])
                for d_i, d in enumerate((1, 4, 16)):
                    vx4 = VX[d_i].rearrange("p (m t c) -> p m t c", m=2, t=16)
                    for m_i, m in ((n % 2, n),):
                        src = vd[m * 2048:(m + 1) * 2048, pr * 2:(pr + 1) * 2, :]
                        rd = [f"vd{pr}_{m * 4 + q}" for q in range(4)]
                        if d == 1:
                            dma(vx4[:, m_i, :, :], src.rearrange("(t j) h c -> j t (h c)", j=128),
                                f"vx{d_i}_{m_i}_0", reads=rd, writes=[f"VX{d_i}_{m_i}_0"])
                        elif d == 4:
                            for b4 in range(4):
                                dma(vx4[:, m_i, b4 * 4:(b4 + 1) * 4, :],
                                    src[b4 * 512:(b4 + 1) * 512].rearrange("(j r) h c -> j r (h c)", r=4),
                                    f"vx{d_i}_{m_i}_{b4}", reads=rd, writes=[f"VX{d_i}_{m_i}_{b4}"])
                        else:
                            dma(vx4[:, m_i, :, :], src.rearrange("(j r) h c -> j r (h c)", r=16),
                                f"vx{d_i}_{m_i}_0", reads=rd, writes=[f"VX{d_i}_{m_i}_0"])
                for hd in range(2):
                    base = hd * 64
                    first = [True] * 4
                    tiles = []
                    for d_i, d in enumerate((1, 4, 16)):
                        cidx = (pr * 2 + hd) * 3 + d_i
                        mb = mbuf[d_i]
                        mbres = f"mbuf{d_i}"
                        sc.add("pool", lambda e, mb=mb, cidx=cidx: e.tensor_scalar(
                            out=mb, in0=dmat[:], scalar1=cv[:, cidx:cidx + 1], scalar2=-100.0,
                            op0=ALU.mult, op1=ALU.max), reads=["dmat", "cv"], writes=[mbres])
                        for ti in range(16):
                            tiles.append((d_i, d, ti, mb, mbres))

                    def emit_S(t, tiles=tiles, base=base, n=n, kslot=kslot, kres=kres):
                        d_i, d, ti, mb, mbres = tiles[t]
                        same, tk = prev_tile(d, ti)
                        has_prev = same or n > 0
                        sl = t % NSB
                        stb = pbank[sl][:, 0:256]
                        qa = tile_ap(QT, base, d, ti)
                        klist = []
                        if has_prev:
                            koff = kslot if same else (2048 - kslot)
                            klist.append((0, tile_ap(KT, base, d, tk, koff), (n % 2 if same else 1 - n % 2), tk,
                                          kres if same else f"KT{(n - 1) % 2}"))
                        klist.append((1, tile_ap(KT, base, d, ti, kslot), n % 2, ti, kres))
                        for w, ka, m_i, tkk, kr in klist:
                            sc.add("pe", lambda e, w=w, ka=ka, qa=qa, stb=stb: e.matmul(
                                stb[:, w * 128:(w + 1) * 128], lhsT=ka, rhs=qa, start=True, stop=True,
                                skip_group_check=True), reads=[kr, "QT"], writes=[f"pb{sl}"])
                        return klist, has_prev

                    def emit_R(t, klist, has_prev, tiles=tiles, hd=hd, first=first):
                        d_i, d, ti, mb, mbres = tiles[t]
                        vx4 = VX[d_i].rearrange("p (m t c) -> p m t c", m=2, t=16)
                        sl = t % NSB
                        stb = pbank[sl][:, 0:256]
                        tb, pt = tbuf[sl], PT[sl]
                        c_lo = 0 if has_prev else 128
                        sc.add("dve", lambda e, tb=tb, stb=stb, c_lo=c_lo, mb=mb: e.tensor_tensor(
                            out=tb[:, c_lo:256], in0=stb[:, c_lo:256], in1=mb[:, c_lo:256], op=ALU.add),
                            reads=[f"pb{sl}", mbres], writes=[f"tb{sl}"])
                        sc.add("act", lambda e, tb=tb, pt=pt, c_lo=c_lo: e.activation(
                            out=pt[:, c_lo:256], in_=tb[:, c_lo:256], func=AF.Exp),
                            reads=[f"tb{sl}"], writes=[f"pt{sl}"])
                        for w, ka, m_i, tkk, kr in klist:
                            vt = vx4[:, m_i, tkk, hd * 128:(hd + 1) * 128]
                            if d == 16:
                                outs = [(b, accb[b][:, ti: ti + 31 * 16 + 1: 16],
                                         pt[:, w * 128 + b * 32: w * 128 + (b + 1) * 32]) for b in range(4)]
                            elif d == 4:
                                b, r = ti // 4, ti % 4
                                outs = [(b, accb[b][:, r: r + 127 * 4 + 1: 4], pt[:, w * 128:(w + 1) * 128])]
                            else:
                                b = ti // 4
                                outs = [(b, accb[b][:, (ti % 4) * 128:(ti % 4 + 1) * 128],
                                         pt[:, w * 128:(w + 1) * 128])]
                            for b, o_ap, r_ap in outs:
                                st_flag = first[b]
                                first[b] = False
                                sc.add("pe", lambda e, vt=vt, o_ap=o_ap, r_ap=r_ap, st_flag=st_flag: e.matmul(
                                    o_ap, lhsT=vt, rhs=r_ap, start=st_flag, stop=False, skip_group_check=True),
                                    reads=[f"pt{sl}"] + [f"VX{d_i}_{mm}_{q}" for mm in range(2) for q in range(4)],
                                    writes=[f"acc{b}"])

                    pend = {}
                    LA = NSB - 1
                    for t in range(min(LA, len(tiles))):
                        pend[t] = emit_S(t)
                    for t in range(len(tiles)):
                        if t + LA < len(tiles):
                            pend[t + LA] = emit_S(t + LA)
                        kl, hp = pend.pop(t)
                        emit_R(t, kl, hp)
                    for b in range(4):
                        up = slice(base, base + 64)
                        dn = slice(64 - base, 128 - base)
                        sc.add("dve", lambda e, b=b, up=up, dn=dn: e.reciprocal(out=rec[up, :], in_=accb[b][dn, :]),
                               reads=[f"acc{b}"], writes=["rec"])
                        sc.add("dve", lambda e, b=b, up=up: e.tensor_tensor(out=t2[up, :], in0=accb[b][up, :],
                                                                            in1=rec[up, :], op=ALU.mult),
                               reads=[f"acc{b}", "rec"], writes=["t2"])
                        sc.add("pool", lambda e, b=b, up=up: e.tensor_tensor(
                            out=yTB[up, b * 512:(b + 1) * 512], in0=t2[up, :], in1=GT[up, b * 512:(b + 1) * 512],
                            op=ALU.mult), reads=["t2", "GT"], writes=["yTB"])
                for q in range(2):
                    dma(ytB[2 * n + q, pr * 128:(pr + 1) * 128, :], yTB[:, q * 1024:(q + 1) * 1024], "ytbst",
                        reads=["yTB"], writes=[f"ytB{pr}_{n}_{q}"])
                if pr == NPAIR - 1 and os.environ.get("KCC", "1") == "1":
                    dsem("agB")
                    for q in range(2):
                        sc.add("pool", lambda e, c8=2 * n + q: e.collective_compute(
                            "AllGather", ALU.bypass, replica_groups=PAIRS, ins=[ytB[c8]], outs=[ygB[c8]]),
                            reads=[f"ytB{p}_{n}_{q}" for p in range(NPAIR)], writes=[f"ygB{2 * n + q}"],
                            dma="agB", inc=1)
        sc.barrier()
        _chk("B")
        dsem("agB")

        load_weights(wo_in, False)
        fgain = a16[:, 0:4096].bitcast(F32)
        dma(fgain, fg_in[0:1, :].partition_broadcast(128), "c_fgain", writes=["fgain"])
        sc.barrier()
        _chk("WO")
        xr = [a32[:, 0:2048], a32[:, 2048:4096]]
        xn = a32[:, 4096:6144]
        ot = a16[:, 4096:8192].bitcast(F32)
        junkc = a16[:, 8192:10240]
        thc = {}

        def th_of(e):
            if "th" not in thc:
                thc["th"] = e.partition_id() % 2
            return thc["th"]

        def load_yk(tg):
            yk = hx[tg % 2]
            ykres = f"hx{tg % 2}"
            for part, yg, ygres in ((0, ygA, "ygA"), (1, ygB, "ygB")):
                ygres = [f"{ygres}{tg // 2}", f"{ygres}{4 + tg // 2}"]
                dsem(f"ykld{tg % 2}_{part}")
                sc.add("sp", lambda e, yk=yk, yg=yg, part=part, tg=tg: e.dma_start(
                    out=yk[:, part * 8:(part + 1) * 8, :],
                    in_=yg[bass.ts(th_of(e) * 4 + tg // 2, 1), :, (tg % 2) * 512:(tg % 2 + 1) * 512].rearrange(
                        "o (k p) t -> p (o k) t", p=128)),
                    reads=ygres, writes=[ykres + "ab"[part]], dma=f"ykld{tg % 2}_{part}")

        def load_xr(tix):
            dma(xr[tix % 2], xres_in[tix * 128:(tix + 1) * 128, :], f"xr{tix % 2}", writes=[f"xr{tix % 2}"])

        load_yk(0)
        load_xr(0)
        for tg in range(8):
            yk = hx[tg % 2]
            ykres = f"hx{tg % 2}"
            if tg + 1 < 8:
                load_yk(tg + 1)
            for j in range(4):
                tix = tg * 4 + j
                xs = xr[tix % 2]
                xsres = f"xr{tix % 2}"
                if tix + 1 < 32:
                    load_xr(tix + 1)
                for cg in range(4):
                    for k in range(NKC):
                        sc.add("pe", lambda e, yk=yk, j=j, k=k, cg=cg: e.matmul(
                            accb[cg], lhsT=yk[:, k, j * 128:(j + 1) * 128], rhs=wbuf[:, k, cg * 512:(cg + 1) * 512],
                            start=(k == 0), stop=(k == NKC - 1)), reads=[ykres + "a", ykres + "b", f"wbuf{k}"], writes=[f"acc{cg}"])
                sc.add("dve", lambda e, xs=xs: e.tensor_tensor(out=xn, in0=pacc[:, :], in1=xs, op=ALU.add),
                       reads=[xsres] + [f"acc{c}" for c in range(4)], writes=["xn"])
                sc.add("act", lambda e, tix=tix: e.activation(out=junkc, in_=xn, func=AF.Square,
                                                              accum_out=ssq2[:, tix:tix + 1]),
                       reads=["xn"], writes=["junkc", f"ssq2{tix}"])
                sc.add("act", lambda e, tix=tix: e.activation(out=rstd2[:, tix:tix + 1], in_=ssq2[:, tix:tix + 1],
                                                              func=AF.Sqrt, scale=1.0 / D, bias=EPS),
                       reads=[f"ssq2{tix}"], writes=[f"rstd2{tix}"])
                sc.add("dve", lambda e, tix=tix: e.reciprocal(out=rstd2[:, tix:tix + 1], in_=rstd2[:, tix:tix + 1]),
                       reads=[f"rstd2{tix}"], writes=[f"rstd2{tix}"])
                sc.add("dve", lambda e, tix=tix: e.scalar_tensor_tensor(out=ot, in0=xn, scalar=rstd2[:, tix:tix + 1],
                                                                        in1=fgain, op0=ALU.mult, op1=ALU.mult),
                       reads=["xn", f"rstd2{tix}", "fgain"], writes=["ot"])
                dma(out[tix * 128:(tix + 1) * 128, :], ot, "outst", reads=["ot"], writes=[f"out{tix}"])
        if os.environ.get("KDBG", "0") == "1":
            dbgA = dt("dbgA", [1024, 1024], BF16, kind="ExternalOutput").ap()
            dbgB = dt("dbgB", [1024, 1024], BF16, kind="ExternalOutput").ap()
            dbgV = dt("dbgV", [1024, 1024], BF16, kind="ExternalOutput").ap()
            for q in range(8):
                dma(dbgA[q * 128:(q + 1) * 128, :], ygA[0, q * 128:(q + 1) * 128, :], "dbg", reads=[f"ygA{c8}" for c8 in range(8)], writes=[f"dbgA{q}"])
                dma(dbgB[q * 128:(q + 1) * 128, :], ygB[0, q * 128:(q + 1) * 128, :], "dbg", reads=[f"ygB{c8}" for c8 in range(8)], writes=[f"dbgB{q}"])
                dma(dbgV[q * 128:(q + 1) * 128, :], vd[q * 128:(q + 1) * 128, :, :].rearrange("t h c -> t (h c)"), "dbg", writes=[f"dbgV{q}"])
            sc.add("sp", lambda e: e.nop(), reads=[f"dbg{w}{q}" for w in "ABV" for q in range(8)])
        sc.add("sp", lambda e: e.nop(), reads=[f"out{t}" for t in range(32)])

        sc.emit(block, sems, dma_sems)
    return nc


_CACHE = {}


def _consts():
    ident = np.eye(128, dtype=np.float32).astype(ml_dtypes.bfloat16)
    t = np.arange(512)
    mask512 = np.broadcast_to((t % 64 != 0).astype(np.float32)[None, :], (128, 512)).copy()
    s = np.arange(128)[:, None]
    c = np.arange(128)[None, :]
    hm = ((s // 64 == c // 64) & (s <= c)).astype(np.float32)
    hmask = np.tile(hm, (1, 4)).astype(ml_dtypes.bfloat16)
    j = np.arange(128)[:, None]
    i = np.arange(128)[None, :]
    dprev = np.where(j >= i, 128.0 + i - j, BIG)
    dcur = np.where(j <= i, (i - j).astype(np.float64), BIG)
    dmat = np.concatenate([dprev, dcur], axis=1).astype(np.float32)
    return ident, mask512, hmask, dmat


def kernel(x, norm_gain, w_in, lb_logits, hgrn_gnorm, w_out, final_gain):
    x = np.asarray(x, np.float32)
    w_in = np.asarray(w_in, np.float32)[0]
    w_out = np.asarray(w_out, np.float32)[0]
    norm_gain = np.asarray(norm_gain, np.float32)[0]
    lb_logits = np.asarray(lb_logits, np.float32)
    gnorm = np.asarray(hgrn_gnorm, np.float32)[0]
    final_gain = np.asarray(final_gain, np.float32)
    if "nc" not in _CACHE:
        _CACHE["nc"] = build_program()
    nc = _CACHE["nc"]
    ident, mask512, hmask, dmat = _consts()
    slopes = [2.0 ** (-(h + 1) / 2.0) for h in range(16)]
    gain_l = np.ascontiguousarray(norm_gain.reshape(NKC, 128).T)
    in_maps = []
    for c in range(8):
        b, hh = c // 2, c % 2
        hcols = []
        for grp in range(4):
            for h in range(HH):
                gh = hh * HH + h
                hcols.append(np.arange(grp * 1024 + gh * 128, grp * 1024 + (gh + 1) * 128))
        wa = np.ascontiguousarray(w_in[:, np.concatenate(hcols)])
        acols = []
        for grp in range(4):
            for a in range(8):
                ga = hh * 8 + a
                acols.append(np.arange(4096 + grp * 1024 + ga * 64, 4096 + grp * 1024 + (ga + 1) * 64))
        wb = np.ascontiguousarray(w_in[:, np.concatenate(acols)])
        lbl = np.zeros((128, 2 * HH), np.float32)
        for h in range(HH):
            gh = hh * HH + h
            lbl[:, h] = lb_logits[0, gh * 128:(gh + 1) * 128]
            lbl[:, HH + h] = lb_logits[1, gh * 128:(gh + 1) * 128]
        cvv = np.zeros((128, 24), np.float32)
        for a in range(8):
            for d_i, d in enumerate((1, 4, 16)):
                cvv[:, a * 3 + d_i] = np.float32(-slopes[hh * 8 + a] * d)
        in_maps.append({
            "x": np.ascontiguousarray(x[b]),
            "xres": np.ascontiguousarray(x[b, hh * (S // 2):(hh + 1) * (S // 2)]),
            "wa": wa, "wb": wb, "wo": np.ascontiguousarray(w_out),
            "gain": gain_l, "lbl": lbl, "gn": np.ascontiguousarray(gnorm.reshape(128, 1)),
            "fg": np.ascontiguousarray(final_gain.reshape(1, D)), "cv": cvv,
            "ident": ident, "mask512": mask512, "hmask": hmask, "dmat": dmat,
        })
    res = run_bass_kernel_spmd(nc, in_maps, core_ids=list(range(8)))
    _CACHE["res"] = res
    outp = np.empty((4, S, D), np.float32)
    for c in range(8):
        b, hh = c // 2, c % 2
        outp[b, hh * (S // 2):(hh + 1) * (S // 2)] = np.asarray(res.results[c]["out"], np.float32)
    return outp
```

```python
import contextlib
import numpy as np
import ml_dtypes
import concourse.bass as bass
import concourse.mybir as mybir
from concourse.bass_utils import run_bass_kernel_spmd

F32 = mybir.dt.float32
BF16 = mybir.dt.bfloat16
ALU = mybir.AluOpType
AF = mybir.ActivationFunctionType

S = 8192
D = 2048
NKC = 16
EPS = 1e-6
HH = 4
NPAIR = 4
BIG = 1.0e8
SAME_ENGINE_SYNC = True
PAIRS = [[0, 1], [2, 3], [4, 5], [6, 7]]


class _Op:
    __slots__ = ("eng", "fn", "deps", "signal", "dma_key", "dma_val", "count", "inc")

    def __init__(self, eng, fn, deps, dma_key, inc):
        self.eng, self.fn, self.deps, self.dma_key, self.inc = eng, fn, deps, dma_key, inc
        self.signal = False
        self.dma_val = None
        self.count = None


class Sched:
    ENGS = ("pe", "act", "dve", "pool", "sp")

    def __init__(self):
        self.ops = {e: [] for e in self.ENGS}
        self.res_w = {}
        self.res_r = {}
        self.dma_cnt = {}
        self.dma_last = {}
        self.bar_deps = set()
        self.bar_pending = set()
        self.halted = False

    def add(self, eng, fn, reads=(), writes=(), dma=None, inc=16):
        if self.halted:
            return None
        deps = set()
        for r in reads:
            if r in self.res_w:
                deps.add(self.res_w[r])
        for w in writes:
            if w in self.res_w:
                deps.add(self.res_w[w])
            last = {}
            for e in self.res_r.get(w, ()):
                if e.dma_key is not None:
                    deps.add(e)
                else:
                    last[e.eng] = e
            deps.update(last.values())
        if eng in self.bar_pending:
            deps |= self.bar_deps
            self.bar_pending.discard(eng)
        op = _Op(eng, fn, deps, dma, inc)
        if dma is not None:
            self.dma_cnt[dma] = self.dma_cnt.get(dma, 0) + inc
            op.dma_val = self.dma_cnt[dma]
            self.dma_last[dma] = op
        for d in deps:
            if d.dma_key is None:
                d.signal = True
        self.ops[eng].append(op)
        for r in reads:
            self.res_r.setdefault(r, []).append(op)
        for w in writes:
            self.res_w[w] = op
            self.res_r[w] = []
        return op

    def barrier(self):
        deps = set(op for key, op in self.dma_last.items() if key not in ("agA", "agB"))
        for e in self.ENGS:
            if self.ops[e]:
                deps.add(self.ops[e][-1])
        self.bar_deps = deps
        self.bar_pending = set(self.ENGS)

    def emit(self, block, sems, dma_sems):
        for e in self.ENGS:
            c = 0
            for op in self.ops[e]:
                if op.dma_key is None and op.signal:
                    c += 1
                    op.count = c
        handles = {"pe": "tensor", "act": "scalar", "dve": "vector", "pool": "gpsimd", "sp": "sync"}

        def make(e):
            def body(engine):
                seen = {}
                for op in self.ops[e]:
                    need = {}
                    for d in op.deps:
                        if d.dma_key is not None:
                            sem, val, key = dma_sems[d.dma_key], d.dma_val, ("d", d.dma_key)
                        else:
                            if d.eng == e and (e in ("pe", "sp") or not SAME_ENGINE_SYNC):
                                continue
                            sem, val, key = sems[d.eng], d.count, ("e", d.eng)
                        if key not in need or need[key][1] < val:
                            need[key] = (sem, val)
                    for key, (sem, val) in need.items():
                        if seen.get(key, 0) >= val:
                            continue
                        seen[key] = val
                        engine.wait_ge(sem, val)
                    ins = op.fn(engine)
                    if op.dma_key is not None:
                        ins.then_inc(dma_sems[op.dma_key], op.inc)
                    elif op.signal:
                        ins.then_inc(sems[e], 1)
            return body

        for e in self.ENGS:
            getattr(block, handles[e])(make(e))


def build_program():
    nc = bass.Bass("TRN2", target_bir_lowering=False)
    dt = nc.dram_tensor
    x_in = dt("x", [S, D], F32, kind="ExternalInput").ap()
    xres_in = dt("xres", [S // 2, D], F32, kind="ExternalInput").ap()
    wa_in = dt("wa", [D, 2048], F32, kind="ExternalInput").ap()
    wb_in = dt("wb", [D, 2048], F32, kind="ExternalInput").ap()
    wo_in = dt("wo", [D, D], F32, kind="ExternalInput").ap()
    gain_in = dt("gain", [128, NKC], F32, kind="ExternalInput").ap()
    lbl_in = dt("lbl", [128, 2 * HH], F32, kind="ExternalInput").ap()
    gn_in = dt("gn", [128, 1], F32, kind="ExternalInput").ap()
    fg_in = dt("fg", [1, D], F32, kind="ExternalInput").ap()
    cv_in = dt("cv", [128, 24], F32, kind="ExternalInput").ap()
    ident_in = dt("ident", [128, 128], BF16, kind="ExternalInput").ap()
    mask512_in = dt("mask512", [128, 512], F32, kind="ExternalInput").ap()
    hmask_in = dt("hmask", [128, 512], BF16, kind="ExternalInput").ap()
    dmat_in = dt("dmat", [128, 256], F32, kind="ExternalInput").ap()
    out = dt("out", [S // 2, D], F32, kind="ExternalOutput").ap()

    hT = dt("hT", [NKC, 128, S], BF16).ap()
    ytA = dt("ytA", [8, 512, 1024], BF16).ap()
    ytB = dt("ytB", [8, 512, 1024], BF16).ap()
    ygA = dt("ygA", [8, 1024, 1024], BF16).ap()
    ygB = dt("ygB", [8, 1024, 1024], BF16).ap()
    vd = dt("vd", [S, 8, 128], BF16).ap()

    sc = Sched()
    WB = [f"wbuf{k}" for k in range(NKC)]

    with contextlib.ExitStack() as es:
        def sb(name, shape, dtype):
            return es.enter_context(nc.sbuf_tensor(name, shape, dtype))

        def ps(name, shape, dtype):
            return es.enter_context(nc.psum_tensor(name, shape, dtype))

        wbuf = sb("wbuf", [128, NKC, 2048], BF16)
        hx = [sb(f"hx{i}", [128, NKC, 512], BF16) for i in range(2)]
        a32 = sb("a32", [128, 6144], F32)
        a16 = sb("a16", [128, 38400], BF16)
        ident = sb("ident_s", [128, 128], BF16)
        ones = sb("ones_s", [128, 128], BF16)
        mask512 = sb("mask512_s", [128, 512], F32)
        hmask = sb("hmask_s", [128, 512], BF16)
        dmat = sb("dmat_s", [128, 256], F32)
        gain = sb("gain_s", [128, NKC], F32)
        lbl = sb("lbl_s", [128, 2 * HH], F32)
        lb = sb("lb_s", [128, HH], F32)
        oml = sb("oml_s", [128, HH], F32)
        noml = sb("noml_s", [128, HH], F32)
        gn = sb("gn_s", [128, 1], F32)
        cv = sb("cv_s", [128, 24], F32)
        ssq = sb("ssq_s", [128, 64], F32)
        rstd = sb("rstd_s", [128, 64], F32)
        ssq2 = sb("ssq2_s", [128, 32], F32)
        rstd2 = sb("rstd2_s", [128, 32], F32)
        s32 = sb("s32_s", [128, HH, 128], F32)
        s16 = sb("s16_s", [128, HH, 128], BF16)
        ebl4 = sb("ebl_s", [128, HH, 8], F32)

        pbank = [ps(f"pb{i}", [128, 512], F32) for i in range(4)]
        pacc = ps("pacc", [128, 2048], F32)

        sems = {e: es.enter_context(nc.semaphore(f"sem_{e}")) for e in ("pe", "act", "dve", "pool")}
        dma_sems = {}

        def dsem(key):
            if key not in dma_sems:
                dma_sems[key] = es.enter_context(nc.semaphore(f"dsem_{len(dma_sems)}"))
            return key

        block = es.enter_context(nc.Block())

        def dma(out_ap, in_ap, key, reads=(), writes=()):
            dsem(key)
            return sc.add("sp", lambda e, o=out_ap, i=in_ap: e.dma_start(out=o, in_=i),
                          reads=reads, writes=writes, dma=key)

        dma(ident[:], ident_in[:, :], "c_ident", writes=["ident"])
        dma(mask512[:], mask512_in[:, :], "c_mask512", writes=["mask512"])
        dma(hmask[:], hmask_in[:, :], "c_hmask", writes=["hmask"])
        dma(dmat[:], dmat_in[:, :], "c_dmat", writes=["dmat"])
        dma(gain[:], gain_in[:, :], "c_gain", writes=["gain"])
        dma(lbl[:], lbl_in[:, :], "c_lbl", writes=["lbl"])
        dma(gn[:], gn_in[:, :], "c_gn", writes=["gn"])
        dma(cv[:], cv_in[:, :], "c_cv", writes=["cv"])
        sc.add("dve", lambda e: e.memset(ones[:], 1.0), writes=["ones"])
        sc.add("dve", lambda e: e.tensor_tensor(out=lb[:], in0=lbl[:, 0:HH], in1=lbl[:, HH:2 * HH],
                                                op=ALU.subtract), reads=["lbl"], writes=["lb"])
        sc.add("act", lambda e: e.activation(out=lb[:], in_=lb[:], func=AF.Sigmoid),
               reads=["lb"], writes=["lb"])
        sc.add("dve", lambda e: e.tensor_scalar(out=oml[:], in0=lb[:], scalar1=-1.0, scalar2=1.0,
                                                op0=ALU.mult, op1=ALU.add), reads=["lb"], writes=["oml"])
        sc.add("dve", lambda e: e.tensor_scalar(out=noml[:], in0=oml[:], scalar1=-1.0, scalar2=None,
                                                op0=ALU.mult), reads=["oml"], writes=["noml"])

        def load_weights(w_in, use_gain):
            for k in range(NKC):
                slot = k % 2
                st = a32[:, slot * 2048:(slot + 1) * 2048]
                dma(st, w_in[k * 128:(k + 1) * 128, :], f"wst{slot}", writes=[f"wst{slot}"])
                if k % 2 == 0:
                    if use_gain:
                        fn = lambda e, st=st, k=k: e.tensor_scalar(out=wbuf[:, k, :], in0=st,
                                                                   scalar1=gain[:, k:k + 1], scalar2=None,
                                                                   op0=ALU.mult)
                    else:
                        fn = lambda e, st=st, k=k: e.tensor_copy(out=wbuf[:, k, :], in_=st)
                    sc.add("dve", fn, reads=[f"wst{slot}", "gain"], writes=[f"wbuf{k}"])
                else:
                    if use_gain:
                        fn = lambda e, st=st, k=k: e.activation(out=wbuf[:, k, :], in_=st, func=AF.Copy,
                                                                scale=gain[:, k:k + 1])
                    else:
                        fn = lambda e, st=st, k=k: e.activation(out=wbuf[:, k, :], in_=st, func=AF.Copy)
                    sc.add("act", fn, reads=[f"wst{slot}", "gain"], writes=[f"wbuf{k}"])

        import os
        _stop = os.environ.get("KSTOP", "")

        def _chk(tag):
            if _stop == tag:
                sc.halted = True

        load_weights(wa_in, True)
        sc.barrier()
        _chk("W")

        xt = [a32[:, i * 2048:(i + 1) * 2048] for i in range(2)]
        hbs = [a16[:, 0:2048], a16[:, 4096:6144]]
        junk = a16[:, 2048:4096]
        NG = int(os.environ.get('KNG', 16))
        for g in range(NG):
            hslot = hx[g % 2]
            hres = f"hx{g % 2}"
            for j in range(4):
                i = g * 4 + j
                xs = xt[i % 2]
                xres_ = f"xt{i % 2}"
                if i == 0:
                    dma(xs, x_in[0:128, :], xres_, writes=[xres_])
                if i + 1 < NG * 4:
                    dma(xt[(i + 1) % 2], x_in[(i + 1) * 128:(i + 2) * 128, :], f"xt{(i + 1) % 2}",
                        writes=[f"xt{(i + 1) % 2}"])
                sc.add("act", lambda e, xs=xs, i=i: e.activation(out=junk, in_=xs, func=AF.Square,
                                                                 accum_out=ssq[:, i:i + 1]),
                       reads=[xres_], writes=["junk", f"ssq{i}"])
                sc.add("act", lambda e, i=i: e.activation(out=rstd[:, i:i + 1], in_=ssq[:, i:i + 1],
                                                          func=AF.Sqrt, scale=1.0 / D, bias=EPS),
                       reads=[f"ssq{i}"], writes=[f"rstd{i}"])
                sc.add("dve", lambda e, i=i: e.reciprocal(out=rstd[:, i:i + 1], in_=rstd[:, i:i + 1]),
                       reads=[f"rstd{i}"], writes=[f"rstd{i}"])
                hb = hbs[i % 2]
                hbres = f"hb{i % 2}"
                sc.add("act", lambda e, xs=xs, i=i, hb=hb: e.activation(out=hb, in_=xs, func=AF.Copy,
                                                                        scale=rstd[:, i:i + 1]),
                       reads=[xres_, f"rstd{i}"], writes=[hbres])
                pp = (i % 2) * 2
                for k in range(NKC):
                    o = pbank[pp + k // 8][:].bitcast(BF16)[:, (k % 8) * 128:(k % 8 + 1) * 128]
                    sc.add("pe", lambda e, o=o, k=k, hb=hb: e.transpose(o, hb[:, k * 128:(k + 1) * 128], ident[:]),
                           reads=[hbres, "ident"], writes=[f"pb{pp + k // 8}"])
                for half in range(2):
                    src = pbank[pp + half][:].bitcast(BF16).rearrange("p (k t) -> p k t", k=8)
                    dst = hslot[:, half * 8:(half + 1) * 8, j * 128:(j + 1) * 128]
                    if half == 0:
                        sc.add("act", lambda e, s=src, d=dst: e.activation(out=d, in_=s, func=AF.Copy),
                               reads=[f"pb{pp + half}"], writes=[hres])
                    else:
                        sc.add("dve", lambda e, s=src, d=dst: e.tensor_copy(out=d, in_=s),
                               reads=[f"pb{pp + half}"], writes=[hres])
            dma(hT[:, :, g * 512:(g + 1) * 512].rearrange("k p t -> p k t"), hslot[:], f"hTst{g % 2}",
                reads=[hres], writes=[f"hT{g}"])
        sc.barrier()
        _chk("P")

        o32 = [0]
        o16 = [0]

        def f32buf(n):
            v = a32[:, o32[0]:o32[0] + n]
            o32[0] += n
            return v

        def b16buf(n):
            v = a16[:, o16[0]:o16[0] + n]
            o16[0] += n
            return v

        qs, sg, ff, lf, bb, e1, e2, kk, kd32, rr, tt = [f32buf(512) for _ in range(11)]
        kd16, ke16, sq, yT = [b16buf(512) for _ in range(4)]
        qd4 = [b16buf(512) for _ in range(HH)]
        keT4 = [b16buf(512) for _ in range(HH)]
        am4 = [b16buf(512) for _ in range(HH)]
        gg4 = [b16buf(512) for _ in range(HH)]
        v16 = b16buf(2048)
        v163 = v16.rearrange("p (j c) -> p j c", j=4)

        sc.add("dve", lambda e: e.memset(s32[:], 0.0), writes=[f"s32{h}" for h in range(HH)])
        sc.add("dve", lambda e: e.memset(s16[:], 0.0), writes=[f"s16{h}" for h in range(HH)])
        pk = pbank[1][:, 0:256].bitcast(BF16)
        oacc = [pacc[:, h * 512:(h + 1) * 512] for h in range(HH)]

        NBA = int(os.environ.get('KNB', 16))
        for blk in range(NBA):
            hs = hx[blk % 2]
            hres = f"hx{blk % 2}"
            if blk == 0:
                dma(hs[:], hT[:, :, 0:512].rearrange("k p t -> p k t"), "hxld0", reads=["hT0"], writes=[hres])
            if blk + 1 < NBA:
                dma(hx[(blk + 1) % 2][:], hT[:, :, (blk + 1) * 512:(blk + 2) * 512].rearrange("k p t -> p k t"),
                    f"hxld{(blk + 1) % 2}", reads=[f"hT{blk + 1}"], writes=[f"hx{(blk + 1) % 2}"])
            for j in range(4):
                vb = j % 2
                for k in range(NKC):
                    sc.add("pe", lambda e, j=j, k=k, hs=hs, vb=vb: e.matmul(
                        pbank[vb][:, :], lhsT=hs[:, k, j * 128:(j + 1) * 128], rhs=wbuf[:, k, 1024:1536],
                        start=(k == 0), stop=(k == NKC - 1)), reads=[hres, f"wbuf{k}"], writes=[f"pb{vb}"])
                sc.add("act", lambda e, j=j, vb=vb: e.activation(out=v163[:, j, :], in_=pbank[vb][:, :], func=AF.Copy),
                       reads=[f"pb{vb}"], writes=["v16"])
            for hd in range(HH):
                qd, keT, am, gg = qd4[hd], keT4[hd], am4[hd], gg4[hd]
                keT3 = keT.rearrange("p (j d) -> p j d", j=4)
                R = lambda nm, hd=hd: f"{nm}{hd}"

                def proj(bank, c0, hs=hs, hres=hres):
                    for k in range(NKC):
                        sc.add("pe", lambda e, k=k, bank=bank, c0=c0, hs=hs: e.matmul(
                            pbank[bank][:, :], lhsT=wbuf[:, k, c0:c0 + 128], rhs=hs[:, k, :],
                            start=(k == 0), stop=(k == NKC - 1)), reads=[hres, f"wbuf{k}"], writes=[f"pb{bank}"])
                proj(0, hd * 128)
                sc.add("act", lambda e: e.activation(out=qs, in_=pbank[0][:, :], func=AF.Silu),
                       reads=["pb0"], writes=["qs"])
                proj(1, 512 + hd * 128)
                sc.add("act", lambda e: e.activation(out=sg, in_=pbank[1][:, :], func=AF.Sigmoid),
                       reads=["pb1"], writes=["sg"])
                proj(0, 1536 + hd * 128)
                sc.add("act", lambda e, gg=gg: e.activation(out=gg, in_=pbank[0][:, :], func=AF.Silu),
                       reads=["pb0"], writes=[R("gg")])
                sc.add("dve", lambda e, hd=hd: e.tensor_scalar(out=ff, in0=sg, scalar1=oml[:, hd:hd + 1],
                                                               scalar2=lb[:, hd:hd + 1], op0=ALU.mult, op1=ALU.add),
                       reads=["sg", "oml", "lb"], writes=["ff"])
                sc.add("pool", lambda e, hd=hd: e.tensor_scalar(out=kk, in0=sg, scalar1=noml[:, hd:hd + 1],
                                                                scalar2=oml[:, hd:hd + 1], op0=ALU.mult, op1=ALU.add),
                       reads=["sg", "oml", "noml"], writes=["kk"])
                sc.add("dve", lambda e, gg=gg: e.tensor_scalar(out=gg, in0=gg, scalar1=gn[:, 0:1], scalar2=None,
                                                               op0=ALU.mult), reads=[R("gg"), "gn"], writes=[R("gg")])
                sc.add("act", lambda e: e.activation(out=lf, in_=ff, func=AF.Ln), reads=["ff"], writes=["lf"])
                sc.add("dve", lambda e: e.tensor_tensor_scan(out=bb, data0=mask512[:], data1=lf, initial=0.0,
                                                             op0=ALU.mult, op1=ALU.add),
                       reads=["lf", "mask512"], writes=["bb"])
                sc.add("act", lambda e: e.activation(out=e1, in_=bb, func=AF.Exp), reads=["bb"], writes=["e1"])
                sc.add("act", lambda e: e.activation(out=e2, in_=bb, func=AF.Exp, scale=-1.0),
                       reads=["bb"], writes=["e2"])
                sc.add("dve", lambda e, hd=hd: e.tensor_copy(out=ebl4[:, hd, :],
                                                             in_=e1.rearrange("p (n t) -> p n t", t=64)[:, :, 63]),
                       reads=["e1"], writes=[R("ebl")])
                sc.add("pool", lambda e, qd=qd: e.tensor_tensor(out=qd, in0=qs, in1=e1, op=ALU.mult),
                       reads=["qs", "e1"], writes=[R("qd")])
                sc.add("dve", lambda e: e.tensor_tensor(out=kd32, in0=kk, in1=e2, op=ALU.mult),
                       reads=["kk", "e2"], writes=["kd32"])
                sc.add("act", lambda e: e.activation(out=kd16, in_=kd32, func=AF.Copy), reads=["kd32"], writes=["kd16"])
                sc.add("dve", lambda e, hd=hd: e.tensor_tensor(
                    out=ke16.rearrange("p (n t) -> p n t", t=64), in0=kd32.rearrange("p (n t) -> p n t", t=64),
                    in1=ebl4[:, hd, :].unsqueeze(2).to_broadcast([128, 8, 64]), op=ALU.mult),
                    reads=["kd32", R("ebl")], writes=["ke16"])
                for j in range(4):
                    sc.add("pe", lambda e, j=j: e.transpose(pk[:, j * 128:(j + 1) * 128],
                                                            ke16[:, j * 128:(j + 1) * 128], ident[:]),
                           reads=["ke16", "ident"], writes=["pb1"])
                sc.add("act", lambda e, keT=keT: e.activation(out=keT, in_=pk, func=AF.Copy),
                       reads=["pb1"], writes=[R("keT")])
                for j in range(4):
                    sc.add("pe", lambda e, j=j, qd=qd: e.matmul(pbank[0][:, j * 128:(j + 1) * 128],
                                                                lhsT=kd16[:, j * 128:(j + 1) * 128],
                                                                rhs=qd[:, j * 128:(j + 1) * 128], start=True, stop=True,
                                                                skip_group_check=True),
                           reads=["kd16", R("qd")], writes=["pb0"])
                sc.add("dve", lambda e, am=am: e.tensor_tensor(out=am, in0=pbank[0][:, :], in1=hmask[:], op=ALU.mult),
                       reads=["pb0", "hmask"], writes=[R("am")])
            step = 0
            for n in range(8):
                j, half = n // 2, n % 2
                rows = slice(half * 64, half * 64 + 64)
                for hd in range(HH):
                    qd, keT, am = qd4[hd], keT4[hd], am4[hd]
                    keT3 = keT.rearrange("p (j d) -> p j d", j=4)
                    R = lambda nm, hd=hd: f"{nm}{hd}"
                    kb = 2 + step % 2
                    step += 1
                    kvb = pbank[kb][:, 0:128]
                    kvres = f"pb{kb}"
                    sc.add("pe", lambda e, j=j, rows=rows, kvb=kvb, hd=hd, keT3=keT3: e.matmul(
                        kvb, lhsT=keT3[rows, j, :], rhs=v163[rows, j, hd * 128:(hd + 1) * 128],
                        start=True, stop=True), reads=[R("keT"), "v16"], writes=[kvres])
                    sc.add("pe", lambda e, n=n, hd=hd, qd=qd: e.matmul(
                        oacc[hd][:, n * 64:(n + 1) * 64], lhsT=s16[:, hd, :], rhs=qd[:, n * 64:(n + 1) * 64],
                        start=(n == 0), stop=False, skip_group_check=True),
                        reads=[R("s16"), R("qd")], writes=[R("oacc")])
                    if half == 1:
                        sc.add("pe", lambda e, j=j, hd=hd, am=am: e.matmul(
                            oacc[hd][:, j * 128:(j + 1) * 128], lhsT=v163[:, j, hd * 128:(hd + 1) * 128],
                            rhs=am[:, j * 128:(j + 1) * 128], start=False, stop=True, skip_group_check=True),
                            reads=["v16", R("am")], writes=[R("oacc")])
                    sc.add("dve", lambda e, n=n, kvb=kvb, hd=hd: e.scalar_tensor_tensor(
                        out=s32[:, hd, :], in0=s32[:, hd, :], scalar=ebl4[:, hd, n:n + 1], in1=kvb,
                        op0=ALU.mult, op1=ALU.add), reads=[R("s32"), R("ebl"), kvres], writes=[R("s32")])
                    sc.add("act", lambda e, hd=hd: e.activation(out=s16[:, hd, :], in_=s32[:, hd, :], func=AF.Copy),
                           reads=[R("s32")], writes=[R("s16")])
            for hd in range(HH):
                gg = gg4[hd]
                R = lambda nm, hd=hd: f"{nm}{hd}"
                kb = 2 + hd % 2
                sc.add("act", lambda e, hd=hd: e.activation(out=sq, in_=oacc[hd], func=AF.Square),
                       reads=[R("oacc")], writes=["sq"])
                sc.add("pe", lambda e, kb=kb: e.matmul(pbank[kb][:, :], lhsT=ones[:], rhs=sq, start=True, stop=True),
                       reads=["sq", "ones"], writes=[f"pb{kb}"])
                sc.add("act", lambda e, kb=kb: e.activation(out=rr, in_=pbank[kb][:, :], func=AF.Ln,
                                                            scale=1.0 / 128, bias=EPS), reads=[f"pb{kb}"], writes=["rr"])
                sc.add("act", lambda e: e.activation(out=rr, in_=rr, func=AF.Exp, scale=-0.5),
                       reads=["rr"], writes=["rr"])
                sc.add("dve", lambda e, hd=hd: e.tensor_tensor(out=tt, in0=oacc[hd], in1=rr, op=ALU.mult),
                       reads=[R("oacc"), "rr"], writes=["tt"])
                sc.add("dve", lambda e, gg=gg: e.tensor_tensor(out=yT, in0=tt, in1=gg, op=ALU.mult),
                       reads=["tt", R("gg")], writes=["yT"])
                dma(ytA[blk // 2, hd * 128:(hd + 1) * 128, (blk % 2) * 512:(blk % 2 + 1) * 512], yT, "ytst",
                    reads=["yT"], writes=[f"ytA{hd}_{blk}"])
        sc.barrier()
        _chk("A")
        dsem("agA")
        if os.environ.get("KCC", "1") == "1":
          for c8 in range(8):
            sc.add("pool", lambda e, c8=c8: e.collective_compute("AllGather", ALU.bypass, replica_groups=PAIRS,
                                                                 ins=[ytA[c8]], outs=[ygA[c8]]),
                   writes=[f"ygA{c8}"], dma="agA", inc=1)
        _chk("AGA")

        load_weights(wb_in, True)
        sc.barrier()
        _chk("WB")
        o32[0] = 0
        o16[0] = 0
        NSB = 4
        tbuf = [f32buf(256) for _ in range(NSB)]
        rec = f32buf(512)
        t2 = f32buf(512)
        mbuf = [f32buf(256) for _ in range(3)]
        QT = b16buf(2048)
        KT = b16buf(4096)
        GT = b16buf(2048)
        VX = [b16buf(8192) for _ in range(3)]
        vst = [b16buf(1024) for _ in range(2)]
        PT = [b16buf(256) for _ in range(NSB)]
        yTB = b16buf(2048)
        vT = b16buf(512)
        pvtok = pacc[:, 0:256].bitcast(BF16)
        for i in range(2):
            sc.add("dve", lambda e, i=i: e.memset(vst[i], 1.0), writes=[f"vst{i}"])

        def tile_ap(buf, base, d, ti, off=0):
            if d == 1:
                s0, st = 128 * ti, 1
            elif d == 4:
                s0, st = 512 * (ti // 4) + (ti % 4), 4
            else:
                s0, st = ti, 16
            return buf[base:base + 64, off + s0: off + s0 + 127 * st + 1: st]

        def prev_tile(d, ti):
            if d == 1:
                return (True, ti - 1) if ti > 0 else (False, 15)
            if d == 4:
                return (True, ti - 4) if ti >= 4 else (False, ti + 12)
            return (False, ti)

        accb = [pacc[:, b * 512:(b + 1) * 512] for b in range(4)]
        vcnt = 0
        bflat = [n_ * 4 + s_ for _p in range(int(os.environ.get('KNP', NPAIR)))
                 for n_ in range(int(os.environ.get('KNS', 4))) for s_ in range(4)]
        bseq = 0
        for pr in range(int(os.environ.get('KNP', NPAIR))):
            for n in range(int(os.environ.get('KNS', 4))):
                kslot = (n % 2) * 2048
                kres = f"KT{n % 2}"
                for sbk in range(4):
                    blk = n * 4 + sbk
                    hs = hx[blk % 2]
                    hres = f"hx{blk % 2}"
                    if bseq == 0:
                        dma(hs[:], hT[:, :, blk * 512:(blk + 1) * 512].rearrange("k p t -> p k t"), f"hxld{blk % 2}",
                            reads=[f"hT{blk}"], writes=[hres])
                    bseq += 1
                    if bseq < len(bflat):
                        nb_ = bflat[bseq]
                        dma(hx[nb_ % 2][:], hT[:, :, nb_ * 512:(nb_ + 1) * 512].rearrange("k p t -> p k t"),
                            f"hxld{nb_ % 2}", reads=[f"hT{nb_}"], writes=[f"hx{nb_ % 2}"])
                    for k in range(NKC):
                        sc.add("pe", lambda e, k=k, hs=hs, pr=pr: e.matmul(
                            pbank[3][:, :], lhsT=wbuf[:, k, 1024 + pr * 128: 1024 + (pr + 1) * 128], rhs=hs[:, k, :],
                            start=(k == 0), stop=(k == NKC - 1)), reads=[hres, f"wbuf{k}"], writes=["pb3"])
                    sc.add("act", lambda e: e.activation(out=vT, in_=pbank[3][:, :], func=AF.Copy),
                           reads=["pb3"], writes=["vT"])
                    QKG = (("q", pr * 128, 1),)
                    for which, c0, bank in QKG:
                        for k in range(NKC):
                            sc.add("pe", lambda e, k=k, bank=bank, c0=c0, hs=hs: e.matmul(
                                pbank[bank][:, :], lhsT=wbuf[:, k, c0:c0 + 128], rhs=hs[:, k, :],
                                start=(k == 0), stop=(k == NKC - 1)),
                                reads=[hres, f"wbuf{k}"], writes=[f"pb{bank}"])
                        if which == "q":
                            sc.add("act", lambda e, sbk=sbk: e.activation(out=QT[:, sbk * 512:(sbk + 1) * 512],
                                                                          in_=pbank[1][:, :], func=AF.Copy, scale=0.125),
                                   reads=["pb1"], writes=["QT"])
                        elif which == "k":
                            sc.add("act", lambda e, sbk=sbk, kslot=kslot: e.activation(
                                out=KT[:, kslot + sbk * 512: kslot + (sbk + 1) * 512], in_=pbank[2][:, :], func=AF.Copy),
                                reads=["pb2"], writes=[kres])
                        else:
                            sc.add("act", lambda e, sbk=sbk: e.activation(out=GT[:, sbk * 512:(sbk + 1) * 512],
                                                                          in_=pbank[0][:, :], func=AF.Silu),
                                   reads=["pb0"], writes=["GT"])
                    for j in range(4):
                        sc.add("pe", lambda e, j=j: e.transpose(pvtok[:, j * 128:(j + 1) * 128],
                                                                vT[:, j * 128:(j + 1) * 128], ident[:]),
                               reads=["vT", "ident"], writes=["acc0"])
                    vs = vst[vcnt % 2]
                    vres = f"vst{vcnt % 2}"
                    vs4 = vs.rearrange("p (j h c) -> p j h c", j=4, h=2)
                    pv3 = pvtok.rearrange("p (j c) -> p j c", j=4)
                    sc.add("dve", lambda e, vs4=vs4, pv3=pv3: e.tensor_copy(out=vs4[:, :, 0, 0:64], in_=pv3[:, :, 0:64]),
                           reads=["acc0"], writes=[vres])
                    sc.add("dve", lambda e, vs4=vs4, pv3=pv3: e.tensor_copy(out=vs4[:, :, 1, 64:128], in_=pv3[:, :, 64:128]),
                           reads=["acc0"], writes=[vres])
                    dma(vd[blk * 512:(blk + 1) * 512, pr * 2:(pr + 1) * 2, :].rearrange("(j p) h c -> p j h c", p=128),
                        vs4, f"vdst{vcnt % 2}", reads=[vres], writes=[f"vd{pr}_{blk}"])
                    vcnt += 1
                    QKG = (("k", 512 + pr * 128, 2), ("g", 1536 + pr * 128, 0))
                    for which, c0, bank in QKG:
                        for k in range(NKC):
                            sc.add("pe", lambda e, k=k, bank=bank, c0=c0, hs=hs: e.matmul(
                                pbank[bank][:, :], lhsT=wbuf[:, k, c0:c0 + 128], rhs=hs[:, k, :],
                                start=(k == 0), stop=(k == NKC - 1)),
                                reads=[hres, f"wbuf{k}"], writes=[f"pb{bank}"])
                        if which == "q":
                            sc.add("act", lambda e, sbk=sbk: e.activation(out=QT[:, sbk * 512:(sbk + 1) * 512],
                                                                          in_=pbank[1][:, :], func=AF.Copy, scale=0.125),
                                   reads=["pb1"], writes=["QT"])
                        elif which == "k":
                            sc.add("act", lambda e, sbk=sbk, kslot=kslot: e.activation(
                                out=KT[:, kslot + sbk * 512: kslot + (sbk + 1) * 512], in_=pbank[2][:, :], func=AF.Copy),
                                reads=["pb2"], writes=[kres])
                        else:
                            sc.add("act", lambda e, sbk=sbk: e.activation(out=GT[:, sbk * 512:(sbk + 1) * 512],
                                                                          in_=pbank[0][:, :], func=AF.Silu),
                                   reads=["pb0"], writes=["GT"])
                for d_i, d in enumerate((1, 4, 16)):
                    vx4 = VX[d_i].rearrange("p (m t c) -> p m t c", m=2, t=16)
                    for m_i, m in ((n % 2, n),):
                        src = vd[m * 2048:(m + 1) * 2048, pr * 2:(pr + 1) * 2, :]
                        rd = [f"vd{pr}_{m * 4 + q}" for q in range(4)]
                        if d == 1:
                            dma(vx4[:, m_i, :, :], src.rearrange("(t j) h c -> j t (h c)", j=128),
                                f"vx{d_i}_{m_i}_0", reads=rd, writes=[f"VX{d_i}_{m_i}_0"])
                        elif d == 4:
                            for b4 in range(4):
                                dma(vx4[:, m_i, b4 * 4:(b4 + 1) * 4, :],
                                    src[b4 * 512:(b4 + 1) * 512].rearrange("(j r) h c -> j r (h c)", r=4),
                                    f"vx{d_i}_{m_i}_{b4}", reads=rd, writes=[f"VX{d_i}_{m_i}_{b4}"])
                        else:
                            dma(vx4[:, m_i, :, :], src.rearrange("(j r) h c -> j r (h c)", r=16),
                                f"vx{d_i}_{m_i}_0", reads=rd, writes=[f"VX{d_i}_{m_i}_0"])
                for hd in range(2):
                    base = hd * 64
                    first = [True] * 4
                    tiles = []
                    for d_i, d in enumerate((1, 4, 16)):
                        cidx = (pr * 2 + hd) * 3 + d_i
                        mb = mbuf[d_i]
                        mbres = f"mbuf{d_i}"
                        sc.add("pool", lambda e, mb=mb, cidx=cidx: e.tensor_scalar(
                            out=mb, in0=dmat[:], scalar1=cv[:, cidx:cidx + 1], scalar2=-100.0,
                            op0=ALU.mult, op1=ALU.max), reads=["dmat", "cv"], writes=[mbres])
                        for ti in range(16):
                            tiles.append((d_i, d, ti, mb, mbres))

                    def emit_S(t, tiles=tiles, base=base, n=n, kslot=kslot, kres=kres):
                        d_i, d, ti, mb, mbres = tiles[t]
                        same, tk = prev_tile(d, ti)
                        has_prev = same or n > 0
                        sl = t % NSB
                        stb = pbank[sl][:, 0:256]
                        qa = tile_ap(QT, base, d, ti)
                        klist = []
                        if has_prev:
                            koff = kslot if same else (2048 - kslot)
                            klist.append((0, tile_ap(KT, base, d, tk, koff), (n % 2 if same else 1 - n % 2), tk,
                                          kres if same else f"KT{(n - 1) % 2}"))
                        klist.append((1, tile_ap(KT, base, d, ti, kslot), n % 2, ti, kres))
                        for w, ka, m_i, tkk, kr in klist:
                            sc.add("pe", lambda e, w=w, ka=ka, qa=qa, stb=stb: e.matmul(
                                stb[:, w * 128:(w + 1) * 128], lhsT=ka, rhs=qa, start=True, stop=True,
                                skip_group_check=True), reads=[kr, "QT"], writes=[f"pb{sl}"])
                        return klist, has_prev

                    def emit_R(t, klist, has_prev, tiles=tiles, hd=hd, first=first):
                        d_i, d, ti, mb, mbres = tiles[t]
                        vx4 = VX[d_i].rearrange("p (m t c) -> p m t c", m=2, t=16)
                        sl = t % NSB
                        stb = pbank[sl][:, 0:256]
                        tb, pt = tbuf[sl], PT[sl]
                        c_lo = 0 if has_prev else 128
                        sc.add("dve", lambda e, tb=tb, stb=stb, c_lo=c_lo, mb=mb: e.tensor_tensor(
                            out=tb[:, c_lo:256], in0=stb[:, c_lo:256], in1=mb[:, c_lo:256], op=ALU.add),
                            reads=[f"pb{sl}", mbres], writes=[f"tb{sl}"])
                        sc.add("act", lambda e, tb=tb, pt=pt, c_lo=c_lo: e.activation(
                            out=pt[:, c_lo:256], in_=tb[:, c_lo:256], func=AF.Exp),
                            reads=[f"tb{sl}"], writes=[f"pt{sl}"])
                        for w, ka, m_i, tkk, kr in klist:
                            vt = vx4[:, m_i, tkk, hd * 128:(hd + 1) * 128]
                            if d == 16:
                                outs = [(b, accb[b][:, ti: ti + 31 * 16 + 1: 16],
                                         pt[:, w * 128 + b * 32: w * 128 + (b + 1) * 32]) for b in range(4)]
                            elif d == 4:
                                b, r = ti // 4, ti % 4
                                outs = [(b, accb[b][:, r: r + 127 * 4 + 1: 4], pt[:, w * 128:(w + 1) * 128])]
                            else:
                                b = ti // 4
                                outs = [(b, accb[b][:, (ti % 4) * 128:(ti % 4 + 1) * 128],
                                         pt[:, w * 128:(w + 1) * 128])]
                            for b, o_ap, r_ap in outs:
                                st_flag = first[b]
                                first[b] = False
                                sc.add("pe", lambda e, vt=vt, o_ap=o_ap, r_ap=r_ap, st_flag=st_flag: e.matmul(
                                    o_ap, lhsT=vt, rhs=r_ap, start=st_flag, stop=False, skip_group_check=True),
                                    reads=[f"pt{sl}"] + [f"VX{d_i}_{mm}_{q}" for mm in range(2) for q in range(4)],
                                    writes=[f"acc{b}"])

                    pend = {}
                    LA = NSB - 1
                    for t in range(min(LA, len(tiles))):
                        pend[t] = emit_S(t)
                    for t in range(len(tiles)):
                        if t + LA < len(tiles):
                            pend[t + LA] = emit_S(t + LA)
                        kl, hp = pend.pop(t)
                        emit_R(t, kl, hp)
                    for b in range(4):
                        up = slice(base, base + 64)
                        dn = slice(64 - base, 128 - base)
                        sc.add("dve", lambda e, b=b, up=up, dn=dn: e.reciprocal(out=rec[up, :], in_=accb[b][dn, :]),
                               reads=[f"acc{b}"], writes=["rec"])
                        sc.add("dve", lambda e, b=b, up=up: e.tensor_tensor(out=t2[up, :], in0=accb[b][up, :],
                                                                            in1=rec[up, :], op=ALU.mult),
                               reads=[f"acc{b}", "rec"], writes=["t2"])
                        sc.add("pool", lambda e, b=b, up=up: e.tensor_tensor(
                            out=yTB[up, b * 512:(b + 1) * 512], in0=t2[up, :], in1=GT[up, b * 512:(b + 1) * 512],
                            op=ALU.mult), reads=["t2", "GT"], writes=["yTB"])
                for q in range(2):
                    dma(ytB[2 * n + q, pr * 128:(pr + 1) * 128, :], yTB[:, q * 1024:(q + 1) * 1024], f"ytbst{q}",
                        reads=["yTB"], writes=[f"ytB{pr}_{n}_{q}"])
                if pr == NPAIR - 1 and os.environ.get("KCC", "1") == "1":
                    dsem("agB")
                    for q in range(2):
                        sc.add("pool", lambda e, c8=2 * n + q: e.collective_compute(
                            "AllGather", ALU.bypass, replica_groups=PAIRS, ins=[ytB[c8]], outs=[ygB[c8]]),
                            reads=[f"ytB{p}_{n}_{q}" for p in range(NPAIR)], writes=[f"ygB{2 * n + q}"],
                            dma="agB", inc=1)
        sc.barrier()
        _chk("B")
        dsem("agB")

        load_weights(wo_in, False)
        fgain = a16[:, 0:4096].bitcast(F32)
        dma(fgain, fg_in[0:1, :].partition_broadcast(128), "c_fgain", writes=["fgain"])
        sc.barrier()
        _chk("WO")
        xr = [a32[:, 0:2048], a32[:, 2048:4096]]
        xn = a32[:, 4096:6144]
        ot = a16[:, 4096:8192].bitcast(F32)
        junkc = a16[:, 8192:10240]
        thc = {}

        def th_of(e):
            if "th" not in thc:
                thc["th"] = e.partition_id() % 2
            return thc["th"]

        def load_yk(tg):
            yk = hx[tg % 2]
            ykres = f"hx{tg % 2}"
            for part, yg, ygres in ((0, ygA, "ygA"), (1, ygB, "ygB")):
                ygres = [f"{ygres}{tg // 2}", f"{ygres}{4 + tg // 2}"]
                dsem(f"ykld{tg % 2}_{part}")
                sc.add("sp", lambda e, yk=yk, yg=yg, part=part, tg=tg: e.dma_start(
                    out=yk[:, part * 8:(part + 1) * 8, :],
                    in_=yg[bass.ts(th_of(e) * 4 + tg // 2, 1), :, (tg % 2) * 512:(tg % 2 + 1) * 512].rearrange(
                        "o (k p) t -> p (o k) t", p=128)),
                    reads=ygres, writes=[ykres + "ab"[part]], dma=f"ykld{tg % 2}_{part}")

        def load_xr(tix):
            dma(xr[tix % 2], xres_in[tix * 128:(tix + 1) * 128, :], f"xr{tix % 2}", writes=[f"xr{tix % 2}"])

        load_yk(0)
        load_xr(0)
        for tg in range(8):
            yk = hx[tg % 2]
            ykres = f"hx{tg % 2}"
            if tg + 1 < 8:
                load_yk(tg + 1)
            for j in range(4):
                tix = tg * 4 + j
                xs = xr[tix % 2]
                xsres = f"xr{tix % 2}"
                if tix + 1 < 32:
                    load_xr(tix + 1)
                for cg in range(4):
                    for k in range(NKC):
                        sc.add("pe", lambda e, yk=yk, j=j, k=k, cg=cg: e.matmul(
                            accb[cg], lhsT=yk[:, k, j * 128:(j + 1) * 128], rhs=wbuf[:, k, cg * 512:(cg + 1) * 512],
                            start=(k == 0), stop=(k == NKC - 1)), reads=[ykres + "a", ykres + "b", f"wbuf{k}"], writes=[f"acc{cg}"])
                sc.add("dve", lambda e, xs=xs: e.tensor_tensor(out=xn, in0=pacc[:, :], in1=xs, op=ALU.add),
                       reads=[xsres] + [f"acc{c}" for c in range(4)], writes=["xn"])
                sc.add("act", lambda e, tix=tix: e.activation(out=junkc, in_=xn, func=AF.Square,
                                                              accum_out=ssq2[:, tix:tix + 1]),
                       reads=["xn"], writes=["junkc", f"ssq2{tix}"])
                sc.add("act", lambda e, tix=tix: e.activation(out=rstd2[:, tix:tix + 1], in_=ssq2[:, tix:tix + 1],
                                                              func=AF.Sqrt, scale=1.0 / D, bias=EPS),
                       reads=[f"ssq2{tix}"], writes=[f"rstd2{tix}"])
                sc.add("dve", lambda e, tix=tix: e.reciprocal(out=rstd2[:, tix:tix + 1], in_=rstd2[:, tix:tix + 1]),
                       reads=[f"rstd2{tix}"], writes=[f"rstd2{tix}"])
                sc.add("dve", lambda e, tix=tix: e.scalar_tensor_tensor(out=ot, in0=xn, scalar=rstd2[:, tix:tix + 1],
                                                                        in1=fgain, op0=ALU.mult, op1=ALU.mult),
                       reads=["xn", f"rstd2{tix}", "fgain"], writes=["ot"])
                dma(out[tix * 128:(tix + 1) * 128, :], ot, "outst", reads=["ot"], writes=[f"out{tix}"])
        if os.environ.get("KDBG", "0") == "1":
            dbgA = dt("dbgA", [1024, 1024], BF16, kind="ExternalOutput").ap()
            dbgB = dt("dbgB", [1024, 1024], BF16, kind="ExternalOutput").ap()
            dbgV = dt("dbgV", [1024, 1024], BF16, kind="ExternalOutput").ap()
            for q in range(8):
                dma(dbgA[q * 128:(q + 1) * 128, :], ygA[0, q * 128:(q + 1) * 128, :], "dbg", reads=[f"ygA{c8}" for c8 in range(8)], writes=[f"dbgA{q}"])
                dma(dbgB[q * 128:(q + 1) * 128, :], ygB[0, q * 128:(q + 1) * 128, :], "dbg", reads=[f"ygB{c8}" for c8 in range(8)], writes=[f"dbgB{q}"])
                dma(dbgV[q * 128:(q + 1) * 128, :], vd[q * 128:(q + 1) * 128, :, :].rearrange("t h c -> t (h c)"), "dbg", writes=[f"dbgV{q}"])
            sc.add("sp", lambda e: e.nop(), reads=[f"dbg{w}{q}" for w in "ABV" for q in range(8)])
        sc.add("sp", lambda e: e.nop(), reads=[f"out{t}" for t in range(32)])

        sc.emit(block, sems, dma_sems)
    return nc


_CACHE = {}


def _consts():
    ident = np.eye(128, dtype=np.float32).astype(ml_dtypes.bfloat16)
    t = np.arange(512)
    mask512 = np.broadcast_to((t % 64 != 0).astype(np.float32)[None, :], (128, 512)).copy()
    s = np.arange(128)[:, None]
    c = np.arange(128)[None, :]
    hm = ((s // 64 == c // 64) & (s <= c)).astype(np.float32)
    hmask = np.tile(hm, (1, 4)).astype(ml_dtypes.bfloat16)
    j = np.arange(128)[:, None]
    i = np.arange(128)[None, :]
    dprev = np.where(j >= i, 128.0 + i - j, BIG)
    dcur = np.where(j <= i, (i - j).astype(np.float64), BIG)
    dmat = np.concatenate([dprev, dcur], axis=1).astype(np.float32)
    return ident, mask512, hmask, dmat


def kernel(x, norm_gain, w_in, lb_logits, hgrn_gnorm, w_out, final_gain):
    x = np.asarray(x, np.float32)
    w_in = np.asarray(w_in, np.float32)[0]
    w_out = np.asarray(w_out, np.float32)[0]
    norm_gain = np.asarray(norm_gain, np.float32)[0]
    lb_logits = np.asarray(lb_logits, np.float32)
    gnorm = np.asarray(hgrn_gnorm, np.float32)[0]
    final_gain = np.asarray(final_gain, np.float32)
    if "nc" not in _CACHE:
        _CACHE["nc"] = build_program()
    nc = _CACHE["nc"]
    ident, mask512, hmask, dmat = _consts()
    slopes = [2.0 ** (-(h + 1) / 2.0) for h in range(16)]
    gain_l = np.ascontiguousarray(norm_gain.reshape(NKC, 128).T)
    in_maps = []
    for c in range(8):
        b, hh = c // 2, c % 2
        hcols = []
        for grp in range(4):
            for h in range(HH):
                gh = hh * HH + h
                hcols.append(np.arange(grp * 1024 + gh * 128, grp * 1024 + (gh + 1) * 128))
        wa = np.ascontiguousarray(w_in[:, np.concatenate(hcols)])
        acols = []
        for grp in range(4):
            for a in range(8):
                ga = hh * 8 + a
                acols.append(np.arange(4096 + grp * 1024 + ga * 64, 4096 + grp * 1024 + (ga + 1) * 64))
        wb = np.ascontiguousarray(w_in[:, np.concatenate(acols)])
        lbl = np.zeros((128, 2 * HH), np.float32)
        for h in range(HH):
            gh = hh * HH + h
            lbl[:, h] = lb_logits[0, gh * 128:(gh + 1) * 128]
            lbl[:, HH + h] = lb_logits[1, gh * 128:(gh + 1) * 128]
        cvv = np.zeros((128, 24), np.float32)
        for a in range(8):
            for d_i, d in enumerate((1, 4, 16)):
                cvv[:, a * 3 + d_i] = np.float32(-slopes[hh * 8 + a] * d)
        in_maps.append({
            "x": np.ascontiguousarray(x[b]),
            "xres": np.ascontiguousarray(x[b, hh * (S // 2):(hh + 1) * (S // 2)]),
            "wa": wa, "wb": wb, "wo": np.ascontiguousarray(w_out),
            "gain": gain_l, "lbl": lbl, "gn": np.ascontiguousarray(gnorm.reshape(128, 1)),
            "fg": np.ascontiguousarray(final_gain.reshape(1, D)), "cv": cvv,
            "ident": ident, "mask512": mask512, "hmask": hmask, "dmat": dmat,
        })
    res = run_bass_kernel_spmd(nc, in_maps, core_ids=list(range(8)))
    _CACHE["res"] = res
    outp = np.empty((4, S, D), np.float32)
    for c in range(8):
        b, hh = c // 2, c % 2
        outp[b, hh * (S // 2):(hh + 1) * (S // 2)] = np.asarray(res.results[c]["out"], np.float32)
    return outp
```
